# Optimizing a Trainium2 kernel written in Bass

```python
import jax, jax.numpy as jnp
from jax import lax
import numpy as np

D_MODEL = 1024
BATCH = 8
SEQ = 8192
DEPTH = 2

CHUNK = 64
D_MIX = D_MODEL
RWKV_WIDTH = D_MIX // 2
CONV_WIDTH = D_MIX - RWKV_WIDTH
HEAD_SIZE = 64
N_RWKV_HEADS = RWKV_WIDTH // HEAD_SIZE
CONV_K = 3
D_DECAY_LORA = 64
D_AAA_LORA = 64
D_MV_LORA = 32
RWKV_COLS = 4 * RWKV_WIDTH + D_DECAY_LORA + D_AAA_LORA
CONV_COLS = 4 * CONV_WIDTH
IN_COLS = RWKV_COLS + CONV_COLS
NORM_EPS = 1e-6
GN_EPS = 1e-5 * HEAD_SIZE
L2_EPS = 1e-12

kernel_name = "hybrid_rwkv7_shortconv_parallel_heads"


def rmsnorm(x, gain):
    xf = x.astype(jnp.float32)
    y = xf * lax.rsqrt(jnp.mean(xf * xf, axis=-1, keepdims=True) + NORM_EPS)
    return (y * gain.astype(jnp.float32)).astype(x.dtype)


def token_shift(z, mu):
    prev = jnp.pad(z, ((0, 0), (1, 0), (0, 0)))[:, :-1]
    return z + (prev - z) * mu


def wkv7_scan(r, decay, k, v, kk, b):
    bsz, _, n_heads, hs = r.shape

    def step(S, inp):
        r_t, w_t, k_t, v_t, kk_t, b_t = inp
        sa = -jnp.einsum("bhvk,bhk->bhv", S, kk_t)
        S = S * w_t[:, :, None, :] + sa[..., None] * b_t[:, :, None, :] + v_t[..., None] * k_t[:, :, None, :]
        y = jnp.einsum("bhvk,bhk->bhv", S, r_t)
        return S, y

    xs = tuple(jnp.moveaxis(t, 1, 0) for t in (r, decay, k, v, kk, b))
    S0 = jnp.zeros((bsz, n_heads, hs, hs), jnp.float32)
    _, ys = lax.scan(step, S0, xs)
    return jnp.moveaxis(ys, 0, 1)


def rwkv7_branch(z, mu, decay_up, decay_bias, aaa_up, aaa_bias, k_k, k_a, r_k, ln_gain, ln_bias,
                 v_first, v_down, v_up, v_bias):
    bsz, seq, _ = z.shape
    z = token_shift(z, mu)
    r, k, v, g, zw, za = jnp.split(
        z, [RWKV_WIDTH, 2 * RWKV_WIDTH, 3 * RWKV_WIDTH, 4 * RWKV_WIDTH, 4 * RWKV_WIDTH + D_DECAY_LORA], axis=-1)
    w_log = -jax.nn.softplus(-(decay_bias + jnp.tanh(zw) @ decay_up)) - 0.5
    decay = jnp.exp(-jnp.exp(w_log.astype(jnp.float32)))
    a = jax.nn.sigmoid(aaa_bias + za @ aaa_up)
    if v_first is None:
        v_first = v
    else:
        v = v + (v_first - v) * jax.nn.sigmoid(v_bias + (v @ v_down) @ v_up)

    heads = lambda t: t.astype(jnp.float32).reshape(bsz, seq, N_RWKV_HEADS, HEAD_SIZE)
    r_h, k_h, v_h, a_h, w_h = heads(r), heads(k), heads(v), heads(a), heads(decay)
    kk = k_h * heads(jnp.broadcast_to(k_k, k.shape))
    kk = kk / jnp.maximum(jnp.linalg.norm(kk, axis=-1, keepdims=True), L2_EPS)
    k_h = k_h * (1.0 + (a_h - 1.0) * heads(jnp.broadcast_to(k_a, k.shape)))

    y = wkv7_scan(r_h, w_h, k_h, v_h, kk, kk * a_h)
    mean = jnp.mean(y, axis=-1, keepdims=True)
    var = jnp.mean(jnp.square(y - mean), axis=-1, keepdims=True)
    y = (y - mean) * lax.rsqrt(var + GN_EPS)
    y = y.reshape(bsz, seq, RWKV_WIDTH) * ln_gain + ln_bias
    bonus = jnp.sum(r_h * k_h * r_k, axis=-1, keepdims=True) * v_h
    y = y + bonus.reshape(bsz, seq, RWKV_WIDTH)
    y = y.astype(z.dtype) * jax.nn.silu(g)
    return y, v_first


def short_conv_branch(z, conv_w):
    gate_b, gate_c, h, g = jnp.split(z, 4, axis=-1)
    u = gate_c * h
    u = lax.conv_general_dilated(
        u, conv_w[:, None, :].astype(u.dtype), window_strides=(1,), padding=[(CONV_K - 1, 0)],
        dimension_numbers=("NWC", "WIO", "NWC"), feature_group_count=CONV_WIDTH)
    return gate_b * u * jax.nn.silu(g)


def setup_inputs(seed: int = 0) -> dict:
    key = jax.random.key(seed)
    ks = jax.random.split(key, 20)
    nrm = lambda k, shape, s: jax.random.normal(k, shape, jnp.float32) * s
    n_v = DEPTH - 1
    return {
        "x": nrm(ks[0], (BATCH, SEQ, D_MODEL), 1.0),
        "norm_gain": 1.0 + nrm(ks[1], (DEPTH, D_MODEL), 0.1),
        "w_in": nrm(ks[2], (DEPTH, D_MODEL, IN_COLS), D_MODEL ** -0.5),
        "shift_mu": jax.random.uniform(ks[3], (DEPTH, RWKV_COLS), jnp.float32),
        "decay_up": nrm(ks[4], (DEPTH, D_DECAY_LORA, RWKV_WIDTH), 0.5 * D_DECAY_LORA ** -0.5),
        "decay_bias": jax.random.uniform(ks[5], (DEPTH, RWKV_WIDTH), jnp.float32, -5.0, 1.0),
        "aaa_up": nrm(ks[6], (DEPTH, D_AAA_LORA, RWKV_WIDTH), 0.5 * D_AAA_LORA ** -0.5),
        "aaa_bias": nrm(ks[7], (DEPTH, RWKV_WIDTH), 0.5),
        "k_k": 0.85 + nrm(ks[8], (DEPTH, RWKV_WIDTH), 0.1),
        "k_a": 1.0 + nrm(ks[9], (DEPTH, RWKV_WIDTH), 0.1),
        "r_k": nrm(ks[10], (DEPTH, N_RWKV_HEADS, HEAD_SIZE), 0.1),
        "ln_gain": 1.0 + nrm(ks[11], (DEPTH, RWKV_WIDTH), 0.1),
        "ln_bias": nrm(ks[12], (DEPTH, RWKV_WIDTH), 0.02),
        "v_down": nrm(ks[13], (n_v, RWKV_WIDTH, D_MV_LORA), RWKV_WIDTH ** -0.5),
        "v_up": nrm(ks[14], (n_v, D_MV_LORA, RWKV_WIDTH), 0.5 * D_MV_LORA ** -0.5),
        "v_bias": nrm(ks[15], (n_v, RWKV_WIDTH), 0.5),
        "conv_w": nrm(ks[16], (DEPTH, CONV_K, CONV_WIDTH), CONV_K ** -0.5),
        "w_out": nrm(ks[17], (DEPTH, D_MIX, D_MODEL), D_MIX ** -0.5),
        "final_gain": 1.0 + nrm(ks[18], (D_MODEL,), 0.1),
    }


def reference(x, norm_gain, w_in, shift_mu, decay_up, decay_bias, aaa_up, aaa_bias, k_k, k_a, r_k,
              ln_gain, ln_bias, v_down, v_up, v_bias, conv_w, w_out, final_gain):
    v_first = None
    for l in range(DEPTH):
        h = rmsnorm(x, norm_gain[l])
        z = h @ w_in[l]
        z_rwkv, z_conv = z[..., :RWKV_COLS], z[..., RWKV_COLS:]
        if l == 0:
            vd, vu, vb = None, None, None
        else:
            vd, vu, vb = v_down[l - 1], v_up[l - 1], v_bias[l - 1]
        y_rwkv, v_first = rwkv7_branch(
            z_rwkv, shift_mu[l], decay_up[l], decay_bias[l], aaa_up[l], aaa_bias[l], k_k[l], k_a[l], r_k[l],
            ln_gain[l], ln_bias[l], v_first, vd, vu, vb)
        y_conv = short_conv_branch(z_conv, conv_w[l])
        y = jnp.concatenate([y_rwkv, y_conv], axis=-1) @ w_out[l]
        x = x + y
    return rmsnorm(x, final_gain)
```

```python
import contextlib
import types
import numpy as np
import concourse.bass as bass
import concourse.mybir as mybir
from concourse.bass_utils import run_bass_kernel_spmd

F32 = mybir.dt.float32
BF16 = mybir.dt.bfloat16
ALU = mybir.AluOpType
AF = mybir.ActivationFunctionType

D_MODEL = 1024
IN_COLS = 4224
NCH = 33
TILE = 512
CH = 128
C0 = float(np.exp(-0.5))
NORM_EPS = 1e-6
GN_EPS = 64e-5
NLEV = 6

O_MU, O_DB, O_AB, O_KK, O_KA, O_RK, O_LG, O_LB, O_VB, O_CW, O_NG = 0, 17, 21, 25, 29, 33, 37, 41, 45, 49, 61
NCOLS = 69


class StopBuild(Exception):
    pass


STAGE_LIMIT = [None]


def stage(n):
    if STAGE_LIMIT[0] is not None and STAGE_LIMIT[0] == n:
        raise StopBuild()


class Dep:
    __slots__ = ("lw", "rd", "name", "psum")

    def __init__(self, name="", psum=False):
        self.lw = {}
        self.rd = {}
        self.name = name
        self.psum = psum


class Buf:
    def __init__(self, t, name=""):
        self.t = t
        self.dep = Dep(name)

    def __getitem__(self, k):
        return self.t[k]


class Eng:
    def __init__(self, name):
        self.name = name
        self.ops = []
        self.count = 0
        self.seen = {}
        self.sem = None
        self.dsems = []
        self.dvals = []
        self.dnext = 0


def _dep(x):
    return x.dep if isinstance(x, Buf) else x


class Sched:
    def __init__(self, nc, es, ndma=None):
        self.nc = nc
        ndma = ndma or {"sp": 8, "pool": 4, "act": 2}
        self.engs = {}
        for n in ("pe", "act", "dve", "pool", "sp"):
            e = Eng(n)
            e.sem = es.enter_context(nc.semaphore("s_" + n))
            for k in range(ndma.get(n, 0)):
                e.dsems.append(es.enter_context(nc.semaphore("d_%s%d" % (n, k))))
                e.dvals.append(0)
            self.engs[n] = e
        self.final = []

    def op(self, eng, fn, R=(), W=(), dma=False):
        if getattr(self, "dry", False):
            return ("dry", 0)
        E = self.engs[eng]
        waits = {}

        def need(sem, val):
            if sem is E.sem and eng == "pe":
                return
            if E.seen.get(sem, 0) >= val:
                return
            if waits.get(sem, 0) < val:
                waits[sem] = val

        for b in R:
            d = _dep(b)
            for s, v in d.lw.items():
                need(s, v)
            if d.psum:
                for s, v in d.rd.items():
                    if s is not E.sem:
                        need(s, v)
        for b in W:
            d = _dep(b)
            for s, v in d.lw.items():
                need(s, v)
            for s, v in d.rd.items():
                need(s, v)
        if dma:
            k = E.dnext
            E.dnext = (k + 1) % len(E.dsems)
            sem = E.dsems[k]
            if E.dvals[k] > 0:
                need(sem, E.dvals[k])
            E.dvals[k] += 16
            tok = (sem, E.dvals[k])
            inc = 16
        else:
            E.count += 1
            tok = (E.sem, E.count)
            inc = 1
        for s, v in waits.items():
            E.seen[s] = v
        E.ops.append((list(waits.items()), fn, tok[0], inc))
        for b in R:
            d = _dep(b)
            if d.rd.get(tok[0], 0) < tok[1]:
                d.rd[tok[0]] = tok[1]
        for b in W:
            d = _dep(b)
            if d.lw.get(tok[0], 0) < tok[1]:
                d.lw[tok[0]] = tok[1]
        return tok

    def replay(self, name, e):
        E = self.engs[name]
        for waits, fn, sem, inc in E.ops:
            for s, v in waits:
                e.wait_ge(s, v)
            ins = fn(e)
            ins.then_inc(sem, inc)
        if name == "sp":
            for s, v in self.final:
                e.wait_ge(s, v)


def build(T, layers, n_layers_total, x_kind="ExternalInput", out_kind="ExternalOutput",
          vf_in=False, vf_out=False, final_norm=True, dbg=None):
    NT = T // TILE
    nc = bass.Bass("TRN2", target_bir_lowering=False)
    dr = lambda name, shape, kind: nc.dram_tensor(name, shape, F32, kind=kind).ap()
    x_d = dr("x", [T, D_MODEL], "ExternalInput")
    out_d = dr("out", [T, D_MODEL], "ExternalOutput")
    w_in_d = dr("w_in", [len(layers), D_MODEL, IN_COLS], "ExternalInput")
    w_out_d = dr("w_out", [len(layers), D_MODEL, D_MODEL], "ExternalInput")
    cols_d = dr("cols", [len(layers), 128, NCOLS], "ExternalInput")
    lora_d = dr("lora", [len(layers), 128, 512], "ExternalInput")
    vdn_d = dr("v_down", [512, 32], "ExternalInput")
    vup_d = dr("v_up", [32, 512], "ExternalInput")
    fg_d = dr("final_gain", [1, D_MODEL], "ExternalInput")
    cst_d = dr("consts", [128, 1408], "ExternalInput")
    if vf_in:
        vf_d = dr("vf", [4, 128, T], "ExternalInput")
    elif vf_out:
        vf_d = dr("vf", [4, 128, T], "ExternalOutput")
    else:
        vf_d = dr("vf", [4, 128, T], "Internal")
    xmid_d = dr("xmid", [T, D_MODEL], "Internal") if len(layers) > 1 else None
    wscr_d = nc.dram_tensor("wscr", [len(layers), NCH, 128, 1024], BF16, kind="Internal").ap()
    dbg_d = {}
    if dbg:
        for k, shp in dbg.items():
            dbg_d[k] = dr("dbg_" + k, shp, "ExternalOutput")

    es = contextlib.ExitStack()
    with es:
        es.enter_context(nc.allow_low_precision("bf16 matmul operands, fp32 accumulation"))
        S = Sched(nc, es)

        def sb(name, shape, dt=F32):
            return Buf(es.enter_context(nc.sbuf_tensor("sb_" + name, shape, dt)), name)

        def ps(name, shape, dt=F32):
            b = Buf(es.enter_context(nc.psum_tensor("ps_" + name, shape, dt)), name)
            b.dep.psum = True
            return b

        WD = 8
        wring = [sb("wr%d" % k, [128, 1024], BF16) for k in range(WD)]
        wscr_dep = [Dep("wscr%d" % k) for k in range(len(layers))]
        TSEQ = [16, 8, 9, 10, 11]
        for hp_ in range(4):
            TSEQ += [hp_, 4 + hp_, 12 + hp_]
        for c_ in range(4):
            TSEQ += [21 + c_, 25 + c_, 17 + c_, 29 + c_]
        Wob = sb("Wob", [128, 8, D_MODEL], BF16)
        lora = sb("lora", [128, 512])
        colt = sb("colt", [128, NCOLS])
        omka = sb("omka", [128, 4])
        vdnb = sb("vdnb", [128, 4, 32], BF16)
        vupb = sb("vupb", [32, 512], BF16)
        mkb = sb("mkb", [128, 512], BF16)
        mkl = sb("mkl", [128, 128], BF16)
        scm = sb("scm", [128, 512])
        ident = sb("ident", [128, 128], BF16)
        onesb = sb("onesb", [128, 128], BF16)
        ones64 = sb("ones64", [128, 128], BF16)
        fgb = sb("fgb", [128, D_MODEL])
        MASKB2 = mkb.t[:, :]
        MASKL = mkl.t[:, :]
        SCANM = scm.t[:, :]

        xs = [sb("xs%d" % k, [128, 4, D_MODEL]) for k in range(2)]
        xjds = [[Dep("xj%d_%d" % (q, k)) for k in range(4)] for q in range(2)]
        hTs = [sb("hT%d" % k, [128, 8, TILE], BF16) for k in range(2)]
        hb = [sb("hb%d" % k, [128, D_MODEL], BF16) for k in range(2)]
        ycat = sb("ycat", [128, 8, TILE], BF16)
        ss = sb("ss", [128, 8])
        rs = sb("rs", [128, 8])
        carry = sb("carry", [128, 17])
        ucarry = sb("ucarry", [128, 4, 2])
        S32 = es.enter_context(nc.sbuf_tensor("S32", [128, 4, 64], F32))
        Sbf = es.enter_context(nc.sbuf_tensor("Sbf", [128, 4, 64], BF16))
        Sdep = [Dep("S%d" % h) for h in range(8)]
        Sbdep = [Dep("Sb%d" % h) for h in range(8)]

        class Pool:
            def __init__(self, name, n, shape, dt=F32, mk=sb):
                self.b = [mk("%s%d" % (name, k), shape, dt) for k in range(n)]
                self.i = 0

            def get(self):
                b = self.b[self.i]
                self.i = (self.i + 1) % len(self.b)
                return b

        tmp = Pool("tmp", 7, [128, 514])
        tb16 = Pool("tb", 4, [128, TILE], BF16)
        wstp = Pool("wst", 3, [128, TILE], BF16)
        zl = sb("zl", [128, TILE])
        vp = [sb("vp%d" % k, [128, TILE]) for k in range(4)]
        rp, kp, gp = sb("rp", [128, TILE]), sb("kp", [128, TILE]), sb("gp", [128, TILE])
        sigw, cl, aa = sb("sigw", [128, TILE]), sb("cl", [128, TILE]), sb("aa", [128, TILE])
        Wt, Winv = sb("Wt", [128, TILE]), sb("Winv", [128, TILE])
        kkn, kf, bb = sb("kkn", [128, TILE]), sb("kf", [128, TILE]), sb("bb", [128, TILE])
        AR = sb("AR", [128, 4, 256], BF16)
        Bt, Kt = sb("Bt", [128, TILE], BF16), sb("Kt", [128, TILE], BF16)
        Bh, Kh = sb("Bh", [128, TILE], BF16), sb("Kh", [128, TILE], BF16)
        BKT = sb("BKT", [128, 1024], BF16)
        VT = sb("VT", [128, TILE], BF16)
        lob = sb("lob", [32, TILE], BF16)
        wcs = sb("wcs", [128, 4])
        bonb = sb("bonb", [128, TILE])
        sgb = sb("sgb", [128, TILE], BF16)
        GmC = [sb("Gm%d" % k, [128, 512], BF16) for k in range(8)]
        NLC = [sb("NL%d" % k, [128, 256], BF16) for k in range(8)]
        YC = [sb("Yv%d" % k, [128, 128], BF16) for k in range(8)]
        P1b = [sb("P1b%d" % k, [128, 64], BF16) for k in range(2)]
        Ub = [sb("Ub%d" % k, [128, 64], BF16) for k in range(2)]

        big = Pool("pbig", 2, [128, 512], F32, mk=ps)
        trps = ps("ptr", [128, 1024], BF16)
        smallb = ps("psmall", [128, 512])
        small = smallb.t
        smalldep = smallb.dep
        P1ps = [smalldep for k in range(2)]
        Ups = [smalldep for k in range(2)]
        Stps = [smalldep for k in range(2)]
        invbs = [ps("pinv%d" % k, [128, 512]) for k in range(3)]
        Ytps = ps("pYt", [128, 512])

        class RR:
            def __init__(self, lst):
                self.b = lst
                self.i = 0

            def get(self):
                b = self.b[self.i]
                self.i = (self.i + 1) % len(self.b)
                return b
        ipool = RR(invbs[0:2] + [smallb, Ytps])
        big.b.append(invbs[2])

        xmid_dep = [Dep("xmid%d" % k) for k in range(NT)]
        vf_dep = [Dep("vf%d" % k) for k in range(NT)]
        def act(out, in_, func, R, W, bias=0.0, scale=1.0, accum=None):
            kw = {}
            if accum is not None:
                kw["accum_out"] = accum
            S.op("act", lambda e: e.activation(out=out, in_=in_, func=func, bias=bias, scale=scale, **kw), R, W)

        def tt(eng, out, in0, in1, op, R, W):
            S.op(eng, lambda e: e.tensor_tensor(out=out, in0=in0, in1=in1, op=op), R, W)

        def tsc(eng, out, in0, s1, s2, op0, op1, R, W):
            if op1 is None:
                S.op(eng, lambda e: e.tensor_scalar(out=out, in0=in0, scalar1=s1, scalar2=None, op0=op0), R, W)
            else:
                S.op(eng, lambda e: e.tensor_scalar(out=out, in0=in0, scalar1=s1, scalar2=s2, op0=op0, op1=op1), R, W)

        def stt(out, in0, scalar, in1, op0, op1, R, W):
            S.op("dve", lambda e: e.scalar_tensor_tensor(out=out, in0=in0, scalar=scalar, in1=in1, op0=op0, op1=op1), R, W)

        def rsqrt(out, in_, eps, R, W):
            act(out, in_, AF.Sqrt, R, W, bias=eps)
            S.op("dve", lambda e: e.reciprocal(out=out, in_=out), W, W)

        def cp(eng, out, in_, R, W):
            if eng == "act":
                S.op("act", lambda e: e.copy(out=out, in_=in_), R, W)
            else:
                S.op(eng, lambda e: e.tensor_copy(out=out, in_=in_), R, W)

        def mm(out, lhsT, rhs, R, W, start=True, stop=True):
            S.op("pe", lambda e: e.matmul(out, lhsT, rhs, start=start, stop=stop), R, W)

        def dma(out, in_, R, W, eng="sp"):
            return S.op(eng, lambda e: e.dma_start(out=out, in_=in_), R, W, dma=True)

        rr = {"i": 0}

        def anyeng(choices=("dve", "pool")):
            rr["i"] += 1
            return choices[rr["i"] % len(choices)]

        c0_ = tmp.get()
        dma(c0_.t[:, 0:256], cst_d[:, 0:256], [], [c0_])
        cp("dve", ident.t[:, :], c0_.t[:, 0:128], [c0_], [ident])
        cp("dve", onesb.t[:, :], c0_.t[:, 128:256], [c0_], [onesb])
        tsc("dve", ones64.t[:, :], c0_.t[:, 128:256], 1.0 / 64.0, None, ALU.mult, None, [c0_], [ones64])
        c1_ = tmp.get()
        dma(c1_.t[:, 0:512], cst_d[:, 256:768], [], [c1_])
        cp("dve", mkb.t[:, :], c1_.t[:, 0:512], [c1_], [mkb])
        c2_ = tmp.get()
        dma(c2_.t[:, 0:128], cst_d[:, 768:896], [], [c2_])
        cp("dve", mkl.t[:, :], c2_.t[:, 0:128], [c2_], [mkl])
        c3_ = tmp.get()
        dma(c3_.t[:, 0:512], cst_d[:, 896:1408], [], [c3_])
        cp("dve", scm.t[:, :], c3_.t[:, 0:512], [c3_], [scm])
        if final_norm:
            dma(fgb.t[:, :], fg_d[0:1, :].partition_broadcast(128), [], [fgb])
            tsc("dve", fgb.t[:, :], fgb.t[:, :], 32.0, None, ALU.mult, None, [fgb], [fgb])

        def colv(off, j=0):
            return colt.t[:, off + j:off + j + 1]

        proj_log = {}
        all_pools = []

        def emit_all(dry):
          S.dry = dry
          for p_ in all_pools:
              p_.i = 0
          rr["i"] = 0
          for li, l in enumerate(layers):
            first_global = (l == 0)
            last_global = (l == n_layers_total - 1)
            xin_d = x_d if li == 0 else xmid_d
            xout_d = out_d if li == len(layers) - 1 else xmid_d

            dma(colt.t[:, :], cols_d[li], [], [colt])
            dma(lora.t[:, :], lora_d[li], [], [lora])
            tsc("dve", omka.t[:, :], colt.t[:, O_KA:O_KA + 4], -1.0, 1.0, ALU.mult, ALU.add, [colt], [omka])
            S.op("pool", lambda e: e.memset(carry.t[:, :], 0.0), [], [carry])
            S.op("pool", lambda e: e.memset(ucarry.t[:, :, :], 0.0), [], [ucarry])
            S.op("pool", lambda e: e.memset(S32[:, :, :], 0.0), [], Sdep)
            S.op("pool", lambda e: e.memset(Sbf[:, :, :], 0.0), [], Sbdep)
            k3 = 0
            for kc in range(8):
                for c0 in range(0, IN_COLS, 512):
                    w = min(512, IN_COLS - c0)
                    st = tmp.get()
                    dma(st.t[:, 0:w], w_in_d[li, kc * 128:(kc + 1) * 128, c0:c0 + w], [], [st])
                    eng = ("act", "dve", "pool")[k3 % 3]
                    k3 += 1
                    wb = wstp.get()
                    if eng == "act":
                        act(wb.t[:, 0:w], st.t[:, 0:w], AF.Copy, [st, colt], [wb], scale=colv(O_NG, kc))
                    else:
                        tsc(eng, wb.t[:, 0:w], st.t[:, 0:w], colv(O_NG, kc), None, ALU.mult, None, [st, colt], [wb])
                    cc0, nch = c0 // 128, w // 128
                    dma(wscr_d[li, cc0:cc0 + nch, :, kc * 128:(kc + 1) * 128].rearrange("c p m -> p c m"),
                        wb.t[:, 0:w].rearrange("p (c m) -> p c m", m=128), [wb], [wscr_dep[li]])
            for kc in range(8):
                for c0 in range(0, D_MODEL, 512):
                    st = tmp.get()
                    dma(st.t[:, 0:512], w_out_d[li, kc * 128:(kc + 1) * 128, c0:c0 + 512], [], [st])
                    eng = ("act", "dve", "pool")[k3 % 3]
                    k3 += 1
                    cp(eng, Wob.t[:, kc, c0:c0 + 512], st.t[:, 0:512], [st], [Wob])
            if not first_global:
                st = tmp.get()
                dma(st.t[:, 0:128].rearrange("p (h m) -> p h m", m=32),
                    vdn_d.rearrange("(h p) m -> p h m", p=128), [], [st])
                cp("dve", vdnb.t[:, :, :], st.t[:, 0:128].rearrange("p (h m) -> p h m", m=32), [st], [vdnb])
                st = tmp.get()
                dma(st.t[0:32, 0:512], vup_d[:, :], [], [st])
                cp("dve", vupb.t[:, :], st.t[0:32, 0:512], [st], [vupb])

            if dry:
                proj_log[li] = []
            uses = proj_log[li]
            wstate = {"u": 0, "l": 0}

            def ensure_loads(upto, li=li, uses=uses, wstate=wstate):
                while wstate["l"] < min(upto, len(uses)):
                    k = wstate["l"]
                    slot = wring[k % WD]
                    dma(slot.t[:, :], wscr_d[li, uses[k]], [wscr_dep[li]], [slot])
                    wstate["l"] = k + 1

            stage(1)
            def xsrc(g):
                lj, ij = divmod(g, NT)
                src = x_d if lj == 0 else xmid_d
                return src, ([xmid_dep[ij]] if lj > 0 else []), ij

            def load_x(g):
                src, deps, ij = xsrc(g)
                for j in range(4):
                    r0 = ij * TILE + j * 128
                    dma(xs[g % 2].t[:, j, :], src[r0:r0 + 128, :], deps, [xjds[g % 2][j]])

            def prologue_steps(g):
                xt_, xd_, hT_ = xs[g % 2], xjds[g % 2], hTs[g % 2]
                st = []
                for j in range(4):
                    def pj(j=j):
                        junk = tmp.get()
                        hbj = hb[j % 2]
                        act(junk.t[:, 0:512], xt_.t[:, j, 0:512], AF.Square, [xd_[j]], [junk, ss],
                            accum=ss.t[:, 2 * j:2 * j + 1])
                        act(junk.t[:, 0:512], xt_.t[:, j, 512:1024], AF.Square, [xd_[j]], [junk, ss],
                            accum=ss.t[:, 2 * j + 1:2 * j + 2])
                        tt("dve", rs.t[:, 2 * j:2 * j + 1], ss.t[:, 2 * j:2 * j + 1], ss.t[:, 2 * j + 1:2 * j + 2], ALU.add,
                           [ss], [rs])
                        rsqrt(rs.t[:, 2 * j:2 * j + 1], rs.t[:, 2 * j:2 * j + 1], 1024.0 * NORM_EPS, [rs], [rs])
                        tsc(anyeng(), hbj.t[:, :], xt_.t[:, j, :], rs.t[:, 2 * j:2 * j + 1], 32.0, ALU.mult, ALU.mult,
                            [xd_[j], rs], [hbj])

                        def trs(e, hbj=hbj):
                            ins = None
                            for kc in range(8):
                                ins = e.transpose(trps.t[:, kc * 128:(kc + 1) * 128], hbj.t[:, kc * 128:(kc + 1) * 128],
                                                  ident.t[:, :])
                            return ins
                        S.op("pe", trs, [hbj, ident], [trps])
                        cp(anyeng(("act", "dve")), hT_.t[:, :, j * 128:(j + 1) * 128],
                           trps.t[:, :].rearrange("p (k t) -> p k t", t=128), [trps], [hT_])
                    st.append(pj)
                return st

            if li == 0:
                load_x(0)
                for f_ in prologue_steps(0):
                    f_()
            def tile_ctx(i):
                g = li * NT + i
                xt = xs[g % 2]
                xjd = xjds[g % 2]
                hT = hTs[g % 2]
                t0 = i * TILE
                has_next = (g + 1 < len(layers) * NT)

                stage(2)

                def proj(cc):
                    if dry:
                        uses.append(cc)
                        return big.get()
                    u = wstate["u"]
                    assert uses[u] == cc, (u, uses[u], cc)
                    ensure_loads(u + WD)
                    slot = wring[u % WD]
                    wstate["u"] = u + 1
                    pb = big.get()

                    def f(e, pb=pb, slot=slot, hT=hT):
                        ins = None
                        for kc in range(8):
                            ins = e.matmul(pb.t[:, :], slot.t[:, kc * 128:(kc + 1) * 128], hT.t[:, kc, :],
                                           start=(kc == 0), stop=(kc == 7))
                        return ins
                    S.op("pe", f, [slot, hT], [pb])
                    return pb

                def tokshift(pb, cc, out):
                    zs = tmp.get()
                    cp("pool", zs.t[:, 0:1], carry.t[:, cc:cc + 1], [carry], [zs])
                    cp("act", zs.t[:, 1:513], pb.t[:, :], [pb], [zs])
                    cp("pool", carry.t[:, cc:cc + 1], zs.t[:, 512:513], [zs], [carry])
                    dd = tmp.get()
                    tt("pool", dd.t[:, 0:512], zs.t[:, 0:512], zs.t[:, 1:513], ALU.subtract, [zs], [dd])
                    stt(out.t[:, :], dd.t[:, 0:512], colv(O_MU, cc), zs.t[:, 1:513], ALU.mult, ALU.add,
                        [dd, zs, colt], [out])

                def s_zl():
                    tokshift(proj(16), 16, zl)
                    act(zl.t[0:64, :], zl.t[0:64, :], AF.Tanh, [zl], [zl])

                def s_vall():
                    vb4 = [tb16.get() for _ in range(4)] if not first_global else None
                    for hp in range(4):
                        tokshift(proj(8 + hp), 8 + hp, vp[hp])
                        if first_global:
                            dma(vf_d[hp, :, t0:t0 + TILE], vp[hp].t[:, :], [vp[hp]], [vf_dep[i]])
                        else:
                            cp(anyeng(), vb4[hp].t[:, :], vp[hp].t[:, :], [vp[hp]], [vb4[hp]])
                    if not first_global:
                        lo = big.get()

                        def f(e, lo=lo, vb4=vb4):
                            ins = None
                            for hp in range(4):
                                ins = e.matmul(lo.t[0:32, :], vdnb.t[:, hp, :], vb4[hp].t[:, :], start=(hp == 0),
                                               stop=(hp == 3))
                            return ins
                        S.op("pe", f, [vdnb] + vb4, [lo])
                        cp("act", lob.t[:, :], lo.t[0:32, :], [lo], [lob])
                        for hp in range(4):
                            gps_ = big.get()
                            mm(gps_.t[:, :], vupb.t[0:32, hp * 128:(hp + 1) * 128], lob.t[0:32, :], [vupb, lob], [gps_])
                            sgv = tmp.get()
                            act(sgv.t[:, 0:512], gps_.t[:, :], AF.Sigmoid, [gps_, colt], [sgv], bias=colv(O_VB, hp))
                            vfl = tmp.get()
                            dma(vfl.t[:, 0:512], vf_d[hp, :, t0:t0 + TILE], [vf_dep[i]], [vfl])
                            tt("pool", vfl.t[:, 0:512], vfl.t[:, 0:512], vp[hp].t[:, :], ALU.subtract, [vfl, vp[hp]], [vfl])
                            tt("dve", vfl.t[:, 0:512], vfl.t[:, 0:512], sgv.t[:, 0:512], ALU.mult, [vfl, sgv], [vfl])
                            tt("pool", vp[hp].t[:, :], vp[hp].t[:, :], vfl.t[:, 0:512], ALU.add, [vfl, vp[hp]], [vp[hp]])


                stage(4)
                v3 = lambda ap: ap.rearrange("p (c t) -> p c t", t=128)

                def A_steps(hp):
                    hs = slice(hp * 128, (hp + 1) * 128)
                    st = []
                    st.append(lambda: tokshift(proj(hp), hp, rp))
                    st.append(lambda: tokshift(proj(4 + hp), 4 + hp, kp))
                    st.append(lambda: tokshift(proj(12 + hp), 12 + hp, gp))

                    def s_w():
                        wps = big.get()
                        mm(wps.t[:, :], lora.t[0:64, hs], zl.t[0:64, :], [lora, zl], [wps])
                        act(sigw.t[:, :], wps.t[:, :], AF.Sigmoid, [wps, colt], [sigw], bias=colv(O_DB, hp))
                        S.op("dve", lambda e: e.tensor_tensor_scan(out=cl.t[:, :], data0=SCANM, data1=sigw.t[:, :],
                                                                   initial=0.0, op0=ALU.mult, op1=ALU.add),
                             [scm, sigw], [cl])
                        act(Wt.t[:, :], cl.t[:, :], AF.Exp, [cl], [Wt], scale=-C0)
                        act(Winv.t[:, :], cl.t[:, :], AF.Exp, [cl], [Winv], scale=C0)
                    st.append(s_w)

                    def s_a():
                        aps = big.get()
                        mm(aps.t[:, :], lora.t[64:128, hs], zl.t[64:128, :], [lora, zl], [aps])
                        act(aa.t[:, :], aps.t[:, :], AF.Sigmoid, [aps, colt], [aa], bias=colv(O_AB, hp))
                    st.append(s_a)

                    def s_kk():
                        sq = tb16.get()
                        act(sq.t[:, :], kp.t[:, :], AF.Square, [kp, colt], [sq], scale=colv(O_KK, hp))
                        ssp = big.get()
                        mm(ssp.t[:, :], onesb.t[:, :], sq.t[:, :], [onesb, sq], [ssp])
                        rn = tmp.get()
                        rsqrt(rn.t[:, 0:512], ssp.t[:, :], 1e-24, [ssp], [rn])
                        tsc("pool", kkn.t[:, :], kp.t[:, :], colv(O_KK, hp), 0.0, ALU.mult, ALU.add, [kp, colt], [kkn])
                        tt("pool", kkn.t[:, :], kkn.t[:, :], rn.t[:, 0:512], ALU.mult, [kkn, rn], [kkn])
                    st.append(s_kk)

                    def s_kb():
                        t1 = tmp.get()
                        tsc("pool", t1.t[:, 0:512], aa.t[:, :], colv(O_KA, hp), omka.t[:, hp:hp + 1], ALU.mult, ALU.add,
                            [aa, colt, omka], [t1])
                        tt("pool", kf.t[:, :], kp.t[:, :], t1.t[:, 0:512], ALU.mult, [kp, t1], [kf])
                        tt("pool", bb.t[:, :], kkn.t[:, :], aa.t[:, :], ALU.mult, [kkn, aa], [bb])
                    st.append(s_kb)
                    return st

                def B(hp):
                    ex = tmp.get()
                    tt("pool", ex.t[:, 0:512], cl.t[:, :], sigw.t[:, :], ALU.subtract, [cl, sigw], [ex])
                    Wprev = tmp.get()
                    act(Wprev.t[:, 0:512], ex.t[:, 0:512], AF.Exp, [ex], [Wprev], scale=-C0)
                    tt("dve", AR.t[:, :, 128:256], v3(rp.t[:, :]), v3(Wt.t[:, :]), ALU.mult, [rp, Wt], [AR])
                    stt(AR.t[:, :, 0:128], v3(kkn.t[:, :]), -1.0, v3(Wprev.t[:, 0:512]), ALU.mult, ALU.mult,
                        [kkn, Wprev], [AR])
                    tt("pool", Kt.t[:, :], kf.t[:, :], Winv.t[:, :], ALU.mult, [kf, Winv], [Kt])
                    tt("pool", Bt.t[:, :], bb.t[:, :], Winv.t[:, :], ALU.mult, [bb, Winv], [Bt])
                    for c in range(4):
                        cs = slice(c * 128, (c + 1) * 128)
                        wc = Wt.t[:, c * 128 + 127:c * 128 + 128]
                        tsc("dve", Kh.t[:, cs], Kt.t[:, cs], wc, None, ALU.mult, None, [Kt, Wt], [Kh])
                        tsc("dve", Bh.t[:, cs], Bt.t[:, cs], wc, None, ALU.mult, None, [Bt, Wt], [Bh])
                        cp("pool", wcs.t[:, c:c + 1], wc, [Wt], [wcs])
                    Vb = tb16.get()
                    cp("pool", Vb.t[:, :], vp[hp].t[:, :], [vp[hp]], [Vb])

                    def trs2(e):
                        ins = None
                        for c in range(4):
                            cs = slice(c * 128, (c + 1) * 128)
                            ins = e.transpose(trps.t[:, c * 128:(c + 1) * 128], Kh.t[:, cs], ident.t[:, :])
                            ins = e.transpose(trps.t[:, 512 + c * 128:512 + (c + 1) * 128], Bh.t[:, cs], ident.t[:, :])
                        return ins
                    S.op("pe", trs2, [Kh, Bh, ident], [trps])
                    cp("act", BKT.t[:, :], trps.t[:, :], [trps], [BKT])

                    def trs3(e, Vb=Vb):
                        ins = None
                        for c in range(4):
                            cs = slice(c * 128, (c + 1) * 128)
                            ins = e.transpose(trps.t[:, c * 128:(c + 1) * 128], Vb.t[:, cs], ident.t[:, :])
                        return ins
                    S.op("pe", trs3, [Vb, ident], [trps])
                    cp("dve", VT.t[:, :], trps.t[:, 0:512], [trps], [VT])
                    rk = tmp.get()
                    tt("pool", rk.t[:, 0:512], rp.t[:, :], kf.t[:, :], ALU.mult, [rp, kf], [rk])
                    rkb = tb16.get()
                    act(rkb.t[:, :], rk.t[:, 0:512], AF.Copy, [rk, colt], [rkb], scale=colv(O_RK, hp))
                    bsp = big.get()
                    mm(bsp.t[:, :], onesb.t[:, :], rkb.t[:, :], [onesb, rkb], [bsp])
                    tt("dve", bonb.t[:, :], bsp.t[:, :], vp[hp].t[:, :], ALU.mult, [bsp, vp[hp]], [bonb])
                    act(sgb.t[:, :], gp.t[:, :], AF.Silu, [gp], [sgb])

                    chains = [(c, hh) for c in range(4) for hh in range(2)]
                    for ci, (c, hh) in enumerate(chains):
                        cs = slice(c * 128, (c + 1) * 128)
                        R_ = slice(hh * 64, hh * 64 + 64)
                        gm, NL, Y = GmC[ci], NLC[ci], YC[ci]
                        gb = ipool.get()
                        g2 = ipool.get()

                        def fG(e, R_=R_, cs=cs, c=c, gb=gb):
                            e.matmul(gb.t[:, 0:256], Bt.t[R_, cs], AR.t[R_, c, :], start=True, stop=True)
                            return e.matmul(gb.t[:, 256:512], Kt.t[R_, cs], AR.t[R_, c, :], start=True, stop=True)
                        S.op("pe", fG, [Bt, Kt, AR], [gb])
                        mm(g2.t[:, 0:128], AR.t[R_, c, 0:128], Bt.t[R_, cs], [AR, Bt], [g2])
                        tt("dve", gm.t[:, :], gb.t[:, :], MASKB2, ALU.mult, [gb, mkb], [gm])
                        tt("dve", NL.t[:, 128:256], g2.t[:, 0:128], MASKL, ALU.mult, [g2, mkl], [NL])
                        cp("pool", NL.t[:, 0:128], gm.t[:, 0:128], [gm], [NL])
                        tt("pool", Y.t[:, :], gm.t[:, 0:128], ident.t[:, :], ALU.add, [gm, ident], [Y])
                    for lev in range(NLEV):
                        last = (lev == NLEV - 1)
                        sqs, yls = {}, {}
                        for step in range(8 + 3):
                            if step < 8:
                                NL = NLC[step]
                                sq = ipool.get()
                                sqs[step] = sq

                                def fsq(e, NL=NL, last=last, sq=sq):
                                    ins = e.matmul(sq.t[:, 128:256], NL.t[:, 0:128], NL.t[:, 128:256], start=True, stop=True)
                                    if not last:
                                        ins = e.matmul(sq.t[:, 0:128], NL.t[:, 128:256], NL.t[:, 0:128], start=True,
                                                       stop=True)
                                    return ins
                                S.op("pe", fsq, [NL], [sq])
                            ci = step - 1
                            if 0 <= ci < 8:
                                NL, sq = NLC[ci], sqs[ci]
                                if last:
                                    cp("act", NL.t[:, 128:256], sq.t[:, 128:256], [sq], [NL])
                                else:
                                    cp("act", NL.t[:, :], sq.t[:, 0:256], [sq], [NL])
                            ci = step - 2
                            if 0 <= ci < 8:
                                NL, Y = NLC[ci], YC[ci]
                                yl = ipool.get()
                                yls[ci] = yl
                                mm(yl.t[:, 0:128], NL.t[:, 128:256], Y.t[:, :], [NL, Y], [yl])
                            ci = step - 3
                            if 0 <= ci < 8:
                                Y, yl = YC[ci], yls[ci]
                                tt("dve", Y.t[:, :], yl.t[:, 0:128], Y.t[:, :], ALU.add, [yl, Y], [Y])

                def C_steps(hp):
                    st = []
                    for c in range(4):
                        cs = slice(c * 128, (c + 1) * 128)
                        info = []
                        for hh in range(2):
                            ci = c * 2 + hh
                            info.append(dict(ci=ci, hh=hh, h=2 * hp + hh, R_=slice(hh * 64, hh * 64 + 64),
                                             vs=slice(c * 128 + hh * 64, c * 128 + hh * 64 + 64), gm=GmC[ci], TT=YC[ci],
                                             p1ap=small[:, hh * 64:hh * 64 + 64],
                                             uap=small[:, 128 + hh * 64:128 + hh * 64 + 64],
                                             sap=small[hh * 64:hh * 64 + 64, 256 + hh * 64:256 + hh * 64 + 64]))

                        def s1(info=info, c=c):
                            for d in info:
                                def fP1(e, d=d):
                                    e.matmul(d["p1ap"], AR.t[d["R_"], c, 0:128], Sbf[d["R_"], hp, :], start=True, stop=False)
                                    return e.matmul(d["p1ap"], d["gm"].t[:, 256:384], VT.t[:, d["vs"]], start=False, stop=True)
                                S.op("pe", fP1, [AR, Sbdep[d["h"]], d["gm"], VT], [smalldep])
                            for d in info:
                                cp("dve", P1b[d["hh"]].t[:, :], d["p1ap"], [smalldep], [P1b[d["hh"]]])
                        st.append(s1)

                        def s2(info=info, c=c):
                            for d in info:
                                mm(d["uap"], d["TT"].t[:, :], P1b[d["hh"]].t[:, :], [d["TT"], P1b[d["hh"]]], [smalldep])
                            for d in info:
                                cp("dve", Ub[d["hh"]].t[:, :], d["uap"], [smalldep], [Ub[d["hh"]]])
                        st.append(s2)

                        def s3(info=info, c=c, cs=cs):
                            for d in info:
                                def fS(e, d=d):
                                    vs = d["vs"]
                                    e.matmul(d["sap"], BKT.t[:, 512 + vs.start:512 + vs.stop], Ub[d["hh"]].t[:, :],
                                             start=True, stop=False)
                                    return e.matmul(d["sap"], BKT.t[:, vs], VT.t[:, vs], start=False, stop=True)
                                S.op("pe", fS, [BKT, Ub[d["hh"]], VT], [smalldep])
                            for d in info:
                                def fY(e, d=d):
                                    R_ = d["R_"]
                                    e.matmul(Ytps.t[R_, cs], Sbf[R_, hp, :], AR.t[R_, c, 128:256], start=True, stop=False)
                                    e.matmul(Ytps.t[R_, cs], Ub[d["hh"]].t[:, :], d["gm"].t[:, 128:256], start=False, stop=False)
                                    return e.matmul(Ytps.t[R_, cs], VT.t[:, d["vs"]], d["gm"].t[:, 384:512], start=False,
                                                    stop=True)
                                S.op("pe", fY, [Sbdep[d["h"]], AR, Ub[d["hh"]], d["gm"], VT], [Ytps])
                            for d in info:
                                R_ = d["R_"]
                                stt(S32[R_, hp, :], S32[R_, hp, :], wcs.t[R_, c:c + 1], d["sap"], ALU.mult, ALU.add,
                                    [Sdep[d["h"]], wcs, smalldep], [Sdep[d["h"]]])
                            for d in info:
                                R_ = d["R_"]
                                cp("dve", Sbf[R_, hp, :], S32[R_, hp, :], [Sdep[d["h"]]], [Sbdep[d["h"]]])
                        st.append(s3)

                    def gn():
                        ysb = tmp.get()
                        cp("act", ysb.t[:, 0:512], Ytps.t[:, :], [Ytps], [ysb])
                        ybf = tb16.get()
                        cp("dve", ybf.t[:, :], ysb.t[:, 0:512], [ysb], [ybf])
                        mps = big.get()
                        mm(mps.t[:, :], ones64.t[:, :], ybf.t[:, :], [ones64, ybf], [mps])
                        dd = tmp.get()
                        tt("dve", dd.t[:, 0:512], ysb.t[:, 0:512], mps.t[:, :], ALU.subtract, [ysb, mps], [dd])
                        dsq = tb16.get()
                        act(dsq.t[:, :], dd.t[:, 0:512], AF.Square, [dd], [dsq])
                        vps = big.get()
                        mm(vps.t[:, :], ones64.t[:, :], dsq.t[:, :], [ones64, dsq], [vps])
                        rstd = tmp.get()
                        rsqrt(rstd.t[:, 0:512], vps.t[:, :], GN_EPS, [vps], [rstd])
                        tt("pool", dd.t[:, 0:512], dd.t[:, 0:512], rstd.t[:, 0:512], ALU.mult, [dd, rstd], [dd])
                        tsc("dve", dd.t[:, 0:512], dd.t[:, 0:512], colv(O_LG, hp), colv(O_LB, hp), ALU.mult, ALU.add,
                            [dd, colt], [dd])
                        tt("pool", dd.t[:, 0:512], dd.t[:, 0:512], bonb.t[:, :], ALU.add, [dd, bonb], [dd])
                        tt("pool", ycat.t[:, hp, :], dd.t[:, 0:512], sgb.t[:, :], ALU.mult, [dd, sgb], [ycat])
                    st.append(gn)
                    return st

                def conv_steps():
                    st = []
                    for cpi in range(4):
                        hold = {}

                        def c1(cpi=cpi, hold=hold):
                            pC = proj(21 + cpi)
                            Csb = tmp.get()
                            cp("act", Csb.t[:, 0:512], pC.t[:, :], [pC], [Csb])
                            pH = proj(25 + cpi)
                            u = tmp.get()
                            cp("pool", u.t[:, 0:2], ucarry.t[:, cpi, :], [ucarry], [u])
                            tt("dve", u.t[:, 2:514], Csb.t[:, 0:512], pH.t[:, :], ALU.mult, [Csb, pH], [u])
                            cp("pool", ucarry.t[:, cpi, :], u.t[:, 512:514], [u], [ucarry])
                            acc = tmp.get()
                            tsc("pool", acc.t[:, 0:512], u.t[:, 0:512], colv(O_CW, 0 * 4 + cpi), 0.0, ALU.mult, ALU.add,
                                [u, colt], [acc])
                            acc2 = tmp.get()
                            stt(acc2.t[:, 0:512], u.t[:, 1:513], colv(O_CW, 1 * 4 + cpi), acc.t[:, 0:512], ALU.mult, ALU.add,
                                [u, colt, acc], [acc2])
                            stt(acc.t[:, 0:512], u.t[:, 2:514], colv(O_CW, 2 * 4 + cpi), acc2.t[:, 0:512], ALU.mult, ALU.add,
                                [u, colt, acc2], [acc])
                            pB = proj(17 + cpi)
                            tt("dve", acc2.t[:, 0:512], pB.t[:, :], acc.t[:, 0:512], ALU.mult, [pB, acc], [acc2])
                            pG = proj(29 + cpi)
                            sg = tmp.get()
                            act(sg.t[:, 0:512], pG.t[:, :], AF.Silu, [pG], [sg])
                            tt("pool", ycat.t[:, 4 + cpi, :], acc2.t[:, 0:512], sg.t[:, 0:512], ALU.mult, [acc2, sg], [ycat])
                        st.append(c1)
                    return st

                def interleave(cs_, as_):
                    ia = 0
                    for k, cstep in enumerate(cs_):
                        cstep()
                        want = ((k + 1) * len(as_) + len(cs_) - 1) // len(cs_)
                        while ia < min(want, len(as_)):
                            as_[ia]()
                            ia += 1
                    while ia < len(as_):
                        as_[ia]()
                        ia += 1

                def mid():
                    for hp in range(3):
                        nxt = A_steps(hp + 1)
                        if hp == 1 and has_next:
                            nxt = nxt + prologue_steps(g + 1)
                        interleave(C_steps(hp), nxt)
                        B(hp + 1)

                def outproj_steps():
                    st = []
                    for j in range(4):
                        def oj(j=j):
                            for half in range(2):
                                pb = big.get()

                                def f(e, pb=pb, j=j, half=half):
                                    ins = None
                                    for kc in range(8):
                                        ins = e.matmul(pb.t[:, :], ycat.t[:, kc, j * 128:(j + 1) * 128],
                                                       Wob.t[:, kc, half * 512:(half + 1) * 512], start=(kc == 0),
                                                       stop=(kc == 7))
                                    return ins
                                S.op("pe", f, [ycat, Wob], [pb])
                                tt("dve", xt.t[:, j, half * 512:(half + 1) * 512], xt.t[:, j, half * 512:(half + 1) * 512],
                                   pb.t[:, :], ALU.add, [xjd[j], pb], [xjd[j]])
                            if last_global and final_norm:
                                junk = tmp.get()
                                act(junk.t[:, 0:512], xt.t[:, j, 0:512], AF.Square, [xjd[j]], [junk, ss], accum=ss.t[:, 0:1])
                                act(junk.t[:, 0:512], xt.t[:, j, 512:1024], AF.Square, [xjd[j]], [junk, ss],
                                    accum=ss.t[:, 1:2])
                                tt("dve", rs.t[:, 0:1], ss.t[:, 0:1], ss.t[:, 1:2], ALU.add, [ss], [rs])
                                rsqrt(rs.t[:, 0:1], rs.t[:, 0:1], 1024.0 * NORM_EPS, [rs], [rs])
                                stt(xt.t[:, j, :], xt.t[:, j, :], rs.t[:, 0:1], fgb.t[:, :], ALU.mult, ALU.mult,
                                    [xjd[j], rs, fgb], [xjd[j]])
                            r0 = t0 + j * 128
                            dma(xout_d[r0:r0 + 128, :], xt.t[:, j, :], [xjd[j]],
                                [xmid_dep[i]] if li < len(layers) - 1 else [])
                        st.append(oj)
                    return st

                return types.SimpleNamespace(g=g, has_next=has_next, head1=lambda: [s_zl, s_vall],
                                             head2=lambda: A_steps(0) + [lambda: B(0)], mid=mid,
                                             c3=lambda: C_steps(3), conv=conv_steps, outproj=outproj_steps,
                                             interleave=interleave)

            ctxs = [tile_ctx(i) for i in range(NT)]
            for f_ in ctxs[0].head1() + ctxs[0].head2():
                f_()
            for i in range(NT):
                c_ = ctxs[i]
                n_ = ctxs[i + 1] if i + 1 < NT else None
                if c_.has_next:
                    load_x(c_.g + 1)
                c_.mid()
                c_.interleave(c_.c3(), c_.conv() + (n_.head1() if n_ else []))
                c_.interleave(c_.outproj(), n_.head2() if n_ else [])

        all_pools.extend([tmp, tb16, wstp, big, ipool])
        emit_all(True)
        emit_all(False)
        sp = S.engs["sp"]
        S.final = [(sp.dsems[k], sp.dvals[k]) for k in range(len(sp.dsems)) if sp.dvals[k] > 0]

        with nc.Block() as block:
            @block.sync
            def _(e):
                S.replay("sp", e)

            @block.tensor
            def _(e):
                S.replay("pe", e)

            @block.scalar
            def _(e):
                S.replay("act", e)

            @block.vector
            def _(e):
                S.replay("dve", e)

            @block.gpsimd
            def _(e):
                S.replay("pool", e)
    return nc


def make_consts():
    c = np.zeros((128, 1408), np.float32)
    c[:, 0:128] = np.eye(128, dtype=np.float32)
    blk = np.zeros((128, 128), np.float32)
    blk[0:64, 0:64] = 1.0
    blk[64:128, 64:128] = 1.0
    c[:, 128:256] = blk
    s = np.arange(128)[:, None]
    t = np.arange(128)[None, :]
    strict = (s < t).astype(np.float32)
    incl = (s <= t).astype(np.float32)
    c[:, 256:384] = strict
    c[:, 384:512] = incl
    c[:, 512:640] = strict
    c[:, 640:768] = incl
    c[:, 768:896] = (s > t).astype(np.float32)
    m = np.ones((128, 512), np.float32)
    m[:, 0::128] = 0.0
    c[:, 896:1408] = m
    return c


def pack_cols(l, shift_mu, decay_bias, aaa_bias, k_k, k_a, r_k, ln_gain, ln_bias, v_bias, conv_w, norm_gain):
    c = np.zeros((128, NCOLS), np.float32)
    pc = lambda v: np.ascontiguousarray(v.reshape(-1, 128).T)
    c[:, O_MU:O_MU + 17] = pc(shift_mu[l])
    c[:, O_DB:O_DB + 4] = pc(decay_bias[l])
    c[:, O_AB:O_AB + 4] = pc(aaa_bias[l])
    c[:, O_KK:O_KK + 4] = pc(k_k[l])
    c[:, O_KA:O_KA + 4] = pc(k_a[l])
    c[:, O_RK:O_RK + 4] = pc(r_k[l].reshape(-1))
    c[:, O_LG:O_LG + 4] = pc(ln_gain[l])
    c[:, O_LB:O_LB + 4] = pc(ln_bias[l])
    if l >= 1:
        c[:, O_VB:O_VB + 4] = pc(v_bias[l - 1])
    for k in range(3):
        c[:, O_CW + 4 * k:O_CW + 4 * k + 4] = pc(conv_w[l, k])
    c[:, O_NG:O_NG + 8] = pc(norm_gain[l])
    return c


_NC_CACHE = {}


def _get_nc(key, **kw):
    if key not in _NC_CACHE:
        _NC_CACHE[key] = build(**kw)
    return _NC_CACHE[key]


def host_prep(inp, layers):
    f = lambda a: np.ascontiguousarray(np.asarray(a, dtype=np.float32))
    p = {k: f(v) for k, v in inp.items() if k != "x"}
    cols = np.stack([pack_cols(l, p["shift_mu"], p["decay_bias"], p["aaa_bias"], p["k_k"], p["k_a"], p["r_k"],
                               p["ln_gain"], p["ln_bias"], p["v_bias"], p["conv_w"], p["norm_gain"]) for l in layers])
    lora = np.stack([np.concatenate([p["decay_up"][l], p["aaa_up"][l]], axis=0) for l in layers])
    return {
        "w_in": f(p["w_in"][layers]),
        "w_out": f(p["w_out"][layers]),
        "cols": f(cols),
        "lora": f(lora),
        "v_down": f(p["v_down"][0]),
        "v_up": f(p["v_up"][0]),
        "final_gain": f(p["final_gain"].reshape(1, -1)),
        "consts": make_consts(),
    }


FUSED = True


def kernel(**inputs):
    x = np.ascontiguousarray(np.asarray(inputs["x"], dtype=np.float32))
    B, T, _ = x.shape
    n_layers = np.asarray(inputs["w_in"]).shape[0]
    cores = list(range(B))
    if FUSED:
        nc = _get_nc(("fused", T), T=T, layers=list(range(n_layers)), n_layers_total=n_layers)
        shared = host_prep(inputs, list(range(n_layers)))
        in_maps = [dict(shared, x=x[b]) for b in range(B)]
        res = run_bass_kernel_spmd(nc, in_maps, core_ids=cores)
        return np.stack([np.asarray(r["out"]) for r in res.results]).astype(np.float32)
    cur = [x[b] for b in range(B)]
    vf = None
    for l in range(n_layers):
        nc = _get_nc(("layer", T, l, n_layers), T=T, layers=[l], n_layers_total=n_layers,
                     vf_in=(l > 0), vf_out=(l == 0), final_norm=(l == n_layers - 1))
        shared = host_prep(inputs, [l])
        in_maps = []
        for b in range(B):
            m = dict(shared, x=cur[b])
            if l > 0:
                m["vf"] = vf[b]
            in_maps.append(m)
        res = run_bass_kernel_spmd(nc, in_maps, core_ids=cores)
        cur = [np.asarray(r["out"]) for r in res.results]
        if l == 0:
            vf = [np.asarray(r["vf"]) for r in res.results]
    return np.stack(cur).astype(np.float32)
```

```python
import contextlib
import types
import numpy as np
import concourse.bass as bass
import concourse.mybir as mybir
from concourse.bass_utils import run_bass_kernel_spmd

F32 = mybir.dt.float32
BF16 = mybir.dt.bfloat16
ALU = mybir.AluOpType
AF = mybir.ActivationFunctionType

D_MODEL = 1024
IN_COLS = 4224
NCH = 33
TILE = 512
CH = 128
C0 = float(np.exp(-0.5))
NORM_EPS = 1e-6
GN_EPS = 64e-5
NLEV = 6

O_MU, O_DB, O_AB, O_KK, O_KA, O_RK, O_LG, O_LB, O_VB, O_CW, O_NG = 0, 17, 21, 25, 29, 33, 37, 41, 45, 49, 61
NCOLS = 69


class StopBuild(Exception):
    pass


STAGE_LIMIT = [None]


def stage(n):
    if STAGE_LIMIT[0] is not None and STAGE_LIMIT[0] == n:
        raise StopBuild()


class Dep:
    __slots__ = ("lw", "rd", "name", "psum")

    def __init__(self, name="", psum=False):
        self.lw = {}
        self.rd = {}
        self.name = name
        self.psum = psum


class Buf:
    def __init__(self, t, name=""):
        self.t = t
        self.dep = Dep(name)

    def __getitem__(self, k):
        return self.t[k]


class Eng:
    def __init__(self, name):
        self.name = name
        self.ops = []
        self.count = 0
        self.seen = {}
        self.sem = None
        self.dsems = []
        self.dvals = []
        self.dnext = 0


def _dep(x):
    return x.dep if isinstance(x, Buf) else x


class Sched:
    def __init__(self, nc, es, ndma=None):
        self.nc = nc
        ndma = ndma or {"sp": 8, "pool": 4, "act": 2}
        self.engs = {}
        for n in ("pe", "act", "dve", "pool", "sp"):
            e = Eng(n)
            e.sem = es.enter_context(nc.semaphore("s_" + n))
            for k in range(ndma.get(n, 0)):
                e.dsems.append(es.enter_context(nc.semaphore("d_%s%d" % (n, k))))
                e.dvals.append(0)
            self.engs[n] = e
        self.final = []

    def op(self, eng, fn, R=(), W=(), dma=False):
        if getattr(self, "dry", False):
            return ("dry", 0)
        E = self.engs[eng]
        waits = {}

        def need(sem, val):
            if sem is E.sem and eng == "pe":
                return
            if E.seen.get(sem, 0) >= val:
                return
            if waits.get(sem, 0) < val:
                waits[sem] = val

        for b in R:
            d = _dep(b)
            for s, v in d.lw.items():
                need(s, v)
            if d.psum:
                for s, v in d.rd.items():
                    if s is not E.sem:
                        need(s, v)
        for b in W:
            d = _dep(b)
            for s, v in d.lw.items():
                need(s, v)
            for s, v in d.rd.items():
                need(s, v)
        if dma:
            k = E.dnext
            E.dnext = (k + 1) % len(E.dsems)
            sem = E.dsems[k]
            if E.dvals[k] > 0:
                need(sem, E.dvals[k])
            E.dvals[k] += 16
            tok = (sem, E.dvals[k])
            inc = 16
        else:
            E.count += 1
            tok = (E.sem, E.count)
            inc = 1
        for s, v in waits.items():
            E.seen[s] = v
        E.ops.append((list(waits.items()), fn, tok[0], inc))
        for b in R:
            d = _dep(b)
            if d.rd.get(tok[0], 0) < tok[1]:
                d.rd[tok[0]] = tok[1]
        for b in W:
            d = _dep(b)
            if d.lw.get(tok[0], 0) < tok[1]:
                d.lw[tok[0]] = tok[1]
        return tok

    def replay(self, name, e):
        E = self.engs[name]
        for waits, fn, sem, inc in E.ops:
            for s, v in waits:
                e.wait_ge(s, v)
            ins = fn(e)
            ins.then_inc(sem, inc)
        if name == "sp":
            for s, v in self.final:
                e.wait_ge(s, v)


def build(T, layers, n_layers_total, x_kind="ExternalInput", out_kind="ExternalOutput",
          vf_in=False, vf_out=False, final_norm=True, dbg=None):
    NT = T // TILE
    nc = bass.Bass("TRN2", target_bir_lowering=False)
    dr = lambda name, shape, kind: nc.dram_tensor(name, shape, F32, kind=kind).ap()
    x_d = dr("x", [T, D_MODEL], "ExternalInput")
    out_d = dr("out", [T, D_MODEL], "ExternalOutput")
    w_in_d = dr("w_in", [len(layers), D_MODEL, IN_COLS], "ExternalInput")
    w_out_d = dr("w_out", [len(layers), D_MODEL, D_MODEL], "ExternalInput")
    cols_d = dr("cols", [len(layers), 128, NCOLS], "ExternalInput")
    lora_d = dr("lora", [len(layers), 128, 512], "ExternalInput")
    vdn_d = dr("v_down", [512, 32], "ExternalInput")
    vup_d = dr("v_up", [32, 512], "ExternalInput")
    fg_d = dr("final_gain", [1, D_MODEL], "ExternalInput")
    cst_d = dr("consts", [128, 1408], "ExternalInput")
    if vf_in:
        vf_d = dr("vf", [4, 128, T], "ExternalInput")
    elif vf_out:
        vf_d = dr("vf", [4, 128, T], "ExternalOutput")
    else:
        vf_d = dr("vf", [4, 128, T], "Internal")
    xmid_d = dr("xmid", [T, D_MODEL], "Internal") if len(layers) > 1 else None
    wscr_d = nc.dram_tensor("wscr", [len(layers), NCH, 128, 1024], BF16, kind="Internal").ap()
    dbg_d = {}
    if dbg:
        for k, shp in dbg.items():
            dbg_d[k] = dr("dbg_" + k, shp, "ExternalOutput")

    es = contextlib.ExitStack()
    with es:
        es.enter_context(nc.allow_low_precision("bf16 matmul operands, fp32 accumulation"))
        S = Sched(nc, es)

        def sb(name, shape, dt=F32):
            return Buf(es.enter_context(nc.sbuf_tensor("sb_" + name, shape, dt)), name)

        def ps(name, shape, dt=F32):
            b = Buf(es.enter_context(nc.psum_tensor("ps_" + name, shape, dt)), name)
            b.dep.psum = True
            return b

        WD = 6
        wring = [sb("wr%d" % k, [128, 1024], BF16) for k in range(WD)]
        wscr_dep = [Dep("wscr%d" % k) for k in range(len(layers))]
        TSEQ = [16, 8, 9, 10, 11]
        for hp_ in range(4):
            TSEQ += [hp_, 4 + hp_, 12 + hp_]
        for c_ in range(4):
            TSEQ += [21 + c_, 25 + c_, 17 + c_, 29 + c_]
        Wob = sb("Wob", [128, 8, D_MODEL], BF16)
        lora = sb("lora", [128, 512])
        colt = sb("colt", [128, NCOLS])
        omka = sb("omka", [128, 4])
        vdnb = sb("vdnb", [128, 4, 32], BF16)
        vupb = sb("vupb", [32, 512], BF16)
        mkb = sb("mkb", [128, 512], BF16)
        mkl = sb("mkl", [128, 128], BF16)
        scm = sb("scm", [128, 512])
        ident = sb("ident", [128, 128], BF16)
        onesb = sb("onesb", [128, 128], BF16)
        ones64 = sb("ones64", [128, 128], BF16)
        fgb = sb("fgb", [128, D_MODEL])
        MASKB2 = mkb.t[:, :]
        MASKL = mkl.t[:, :]
        SCANM = scm.t[:, :]

        xs = [sb("xs%d" % k, [128, 4, D_MODEL]) for k in range(2)]
        xjds = [[Dep("xj%d_%d" % (q, k)) for k in range(4)] for q in range(2)]
        hTs = [sb("hT%d" % k, [128, 8, TILE], BF16) for k in range(2)]
        hb = [sb("hb%d" % k, [128, D_MODEL], BF16) for k in range(2)]
        ycat = sb("ycat", [128, 8, TILE], BF16)
        ss = sb("ss", [128, 8])
        rs = sb("rs", [128, 8])
        carry = sb("carry", [128, 17])
        ucarry = sb("ucarry", [128, 4, 2])
        S32 = es.enter_context(nc.sbuf_tensor("S32", [128, 4, 64], F32))
        Sbf = es.enter_context(nc.sbuf_tensor("Sbf", [128, 4, 64], BF16))
        Sdep = [Dep("S%d" % h) for h in range(8)]
        Sbdep = [Dep("Sb%d" % h) for h in range(8)]

        class Pool:
            def __init__(self, name, n, shape, dt=F32, mk=sb):
                self.b = [mk("%s%d" % (name, k), shape, dt) for k in range(n)]
                self.i = 0

            def get(self):
                b = self.b[self.i]
                self.i = (self.i + 1) % len(self.b)
                return b

        tmp = Pool("tmp", 7, [128, 514])
        tb16 = Pool("tb", 4, [128, TILE], BF16)
        wstp = Pool("wst", 2, [128, 4, 1024], BF16)
        zl = sb("zl", [128, TILE])
        vp = [sb("vp%d" % k, [128, TILE]) for k in range(4)]
        rp, kp, gp = sb("rp", [128, TILE]), sb("kp", [128, TILE]), sb("gp", [128, TILE])
        sigw, cl, aa = sb("sigw", [128, TILE]), sb("cl", [128, TILE]), sb("aa", [128, TILE])
        Wt, Winv = sb("Wt", [128, TILE]), sb("Winv", [128, TILE])
        kkn, kf, bb = sb("kkn", [128, TILE]), sb("kf", [128, TILE]), sb("bb", [128, TILE])
        AR = sb("AR", [128, 4, 256], BF16)
        Bt, Kt = sb("Bt", [128, TILE], BF16), sb("Kt", [128, TILE], BF16)
        Bh, Kh = sb("Bh", [128, TILE], BF16), sb("Kh", [128, TILE], BF16)
        BKT = sb("BKT", [128, 1024], BF16)
        VT = sb("VT", [128, TILE], BF16)
        lob = sb("lob", [32, TILE], BF16)
        wcs = sb("wcs", [128, 4])
        bonb = sb("bonb", [128, TILE])
        sgb = sb("sgb", [128, TILE], BF16)
        GmC = [sb("Gm%d" % k, [128, 512], BF16) for k in range(8)]
        NLC = [sb("NL%d" % k, [128, 256], BF16) for k in range(8)]
        YC = [sb("Yv%d" % k, [128, 128], BF16) for k in range(8)]
        P1b = [sb("P1b%d" % k, [128, 64], BF16) for k in range(2)]
        Ub = [sb("Ub%d" % k, [128, 64], BF16) for k in range(2)]

        big = Pool("pbig", 2, [128, 512], F32, mk=ps)
        trps = ps("ptr", [128, 1024], BF16)
        smallb = ps("psmall", [128, 512])
        small = smallb.t
        smalldep = smallb.dep
        P1ps = [smalldep for k in range(2)]
        Ups = [smalldep for k in range(2)]
        Stps = [smalldep for k in range(2)]
        invbs = [ps("pinv%d" % k, [128, 512]) for k in range(3)]
        Ytps = ps("pYt", [128, 512])

        class RR:
            def __init__(self, lst):
                self.b = lst
                self.i = 0

            def get(self):
                b = self.b[self.i]
                self.i = (self.i + 1) % len(self.b)
                return b
        ipool = RR(invbs + [smallb, Ytps])

        xmid_dep = [Dep("xmid%d" % k) for k in range(NT)]
        vf_dep = [Dep("vf%d" % k) for k in range(NT)]
        def act(out, in_, func, R, W, bias=0.0, scale=1.0, accum=None):
            kw = {}
            if accum is not None:
                kw["accum_out"] = accum
            S.op("act", lambda e: e.activation(out=out, in_=in_, func=func, bias=bias, scale=scale, **kw), R, W)

        def tt(eng, out, in0, in1, op, R, W):
            S.op(eng, lambda e: e.tensor_tensor(out=out, in0=in0, in1=in1, op=op), R, W)

        def tsc(eng, out, in0, s1, s2, op0, op1, R, W):
            if op1 is None:
                S.op(eng, lambda e: e.tensor_scalar(out=out, in0=in0, scalar1=s1, scalar2=None, op0=op0), R, W)
            else:
                S.op(eng, lambda e: e.tensor_scalar(out=out, in0=in0, scalar1=s1, scalar2=s2, op0=op0, op1=op1), R, W)

        def stt(out, in0, scalar, in1, op0, op1, R, W):
            S.op("dve", lambda e: e.scalar_tensor_tensor(out=out, in0=in0, scalar=scalar, in1=in1, op0=op0, op1=op1), R, W)

        def rsqrt(out, in_, eps, R, W):
            act(out, in_, AF.Sqrt, R, W, bias=eps)
            S.op("dve", lambda e: e.reciprocal(out=out, in_=out), W, W)

        def cp(eng, out, in_, R, W):
            if eng == "act":
                S.op("act", lambda e: e.copy(out=out, in_=in_), R, W)
            else:
                S.op(eng, lambda e: e.tensor_copy(out=out, in_=in_), R, W)

        def mm(out, lhsT, rhs, R, W, start=True, stop=True):
            S.op("pe", lambda e: e.matmul(out, lhsT, rhs, start=start, stop=stop), R, W)

        def dma(out, in_, R, W, eng="sp"):
            return S.op(eng, lambda e: e.dma_start(out=out, in_=in_), R, W, dma=True)

        rr = {"i": 0}

        def anyeng(choices=("dve", "pool")):
            rr["i"] += 1
            return choices[rr["i"] % len(choices)]

        c0_ = tmp.get()
        dma(c0_.t[:, 0:256], cst_d[:, 0:256], [], [c0_])
        cp("dve", ident.t[:, :], c0_.t[:, 0:128], [c0_], [ident])
        cp("dve", onesb.t[:, :], c0_.t[:, 128:256], [c0_], [onesb])
        tsc("dve", ones64.t[:, :], c0_.t[:, 128:256], 1.0 / 64.0, None, ALU.mult, None, [c0_], [ones64])
        c1_ = tmp.get()
        dma(c1_.t[:, 0:512], cst_d[:, 256:768], [], [c1_])
        cp("dve", mkb.t[:, :], c1_.t[:, 0:512], [c1_], [mkb])
        c2_ = tmp.get()
        dma(c2_.t[:, 0:128], cst_d[:, 768:896], [], [c2_])
        cp("dve", mkl.t[:, :], c2_.t[:, 0:128], [c2_], [mkl])
        c3_ = tmp.get()
        dma(c3_.t[:, 0:512], cst_d[:, 896:1408], [], [c3_])
        cp("dve", scm.t[:, :], c3_.t[:, 0:512], [c3_], [scm])
        if final_norm:
            dma(fgb.t[:, :], fg_d[0:1, :].partition_broadcast(128), [], [fgb])
            tsc("dve", fgb.t[:, :], fgb.t[:, :], 32.0, None, ALU.mult, None, [fgb], [fgb])

        def colv(off, j=0):
            return colt.t[:, off + j:off + j + 1]

        proj_log = {}
        all_pools = []

        def emit_all(dry):
          S.dry = dry
          for p_ in all_pools:
              p_.i = 0
          rr["i"] = 0
          for li, l in enumerate(layers):
            first_global = (l == 0)
            last_global = (l == n_layers_total - 1)
            xin_d = x_d if li == 0 else xmid_d
            xout_d = out_d if li == len(layers) - 1 else xmid_d

            dma(colt.t[:, :], cols_d[li], [], [colt])
            dma(lora.t[:, :], lora_d[li], [], [lora])
            tsc("dve", omka.t[:, :], colt.t[:, O_KA:O_KA + 4], -1.0, 1.0, ALU.mult, ALU.add, [colt], [omka])
            S.op("pool", lambda e: e.memset(carry.t[:, :], 0.0), [], [carry])
            S.op("pool", lambda e: e.memset(ucarry.t[:, :, :], 0.0), [], [ucarry])
            S.op("pool", lambda e: e.memset(S32[:, :, :], 0.0), [], Sdep)
            S.op("pool", lambda e: e.memset(Sbf[:, :, :], 0.0), [], Sbdep)
            k3 = 0
            for c0 in range(0, IN_COLS, 512):
                w = min(512, IN_COLS - c0)
                cc0, nch = c0 // 128, w // 128
                stg = wstp.get()
                for kc in range(8):
                    st = tmp.get()
                    dma(st.t[:, 0:w], w_in_d[li, kc * 128:(kc + 1) * 128, c0:c0 + w], [], [st])
                    eng = ("act", "dve", "pool")[k3 % 3]
                    k3 += 1
                    dst = stg.t[:, 0:nch, kc * 128:(kc + 1) * 128]
                    src = st.t[:, 0:w].rearrange("p (c m) -> p c m", m=128)
                    if eng == "act":
                        act(dst, src, AF.Copy, [st, colt], [stg], scale=colv(O_NG, kc))
                    else:
                        tsc(eng, dst, src, colv(O_NG, kc), None, ALU.mult, None, [st, colt], [stg])
                dma(wscr_d[li, cc0:cc0 + nch, :, :].rearrange("c p m -> p c m"), stg.t[:, 0:nch, :], [stg],
                    [wscr_dep[li]])
            for kc in range(8):
                for c0 in range(0, D_MODEL, 512):
                    st = tmp.get()
                    dma(st.t[:, 0:512], w_out_d[li, kc * 128:(kc + 1) * 128, c0:c0 + 512], [], [st])
                    eng = ("act", "dve", "pool")[k3 % 3]
                    k3 += 1
                    cp(eng, Wob.t[:, kc, c0:c0 + 512], st.t[:, 0:512], [st], [Wob])
            if not first_global:
                st = tmp.get()
                dma(st.t[:, 0:128].rearrange("p (h m) -> p h m", m=32),
                    vdn_d.rearrange("(h p) m -> p h m", p=128), [], [st])
                cp("dve", vdnb.t[:, :, :], st.t[:, 0:128].rearrange("p (h m) -> p h m", m=32), [st], [vdnb])
                st = tmp.get()
                dma(st.t[0:32, 0:512], vup_d[:, :], [], [st])
                cp("dve", vupb.t[:, :], st.t[0:32, 0:512], [st], [vupb])

            if dry:
                proj_log[li] = []
            uses = proj_log[li]
            wstate = {"u": 0, "l": 0}

            def ensure_loads(upto, li=li, uses=uses, wstate=wstate):
                while wstate["l"] < min(upto, len(uses)):
                    k = wstate["l"]
                    slot = wring[k % WD]
                    dma(slot.t[:, :], wscr_d[li, uses[k]], [wscr_dep[li]], [slot])
                    wstate["l"] = k + 1

            stage(1)
            def xsrc(g):
                lj, ij = divmod(g, NT)
                src = x_d if lj == 0 else xmid_d
                return src, ([xmid_dep[ij]] if lj > 0 else []), ij

            def load_x(g):
                src, deps, ij = xsrc(g)
                for j in range(4):
                    r0 = ij * TILE + j * 128
                    dma(xs[g % 2].t[:, j, :], src[r0:r0 + 128, :], deps, [xjds[g % 2][j]])

            def prologue_steps(g):
                xt_, xd_, hT_ = xs[g % 2], xjds[g % 2], hTs[g % 2]
                st = []
                for j in range(4):
                    def pj(j=j):
                        junk = tmp.get()
                        hbj = hb[j % 2]
                        act(junk.t[:, 0:512], xt_.t[:, j, 0:512], AF.Square, [xd_[j]], [junk, ss],
                            accum=ss.t[:, 2 * j:2 * j + 1])
                        act(junk.t[:, 0:512], xt_.t[:, j, 512:1024], AF.Square, [xd_[j]], [junk, ss],
                            accum=ss.t[:, 2 * j + 1:2 * j + 2])
                        tt("dve", rs.t[:, 2 * j:2 * j + 1], ss.t[:, 2 * j:2 * j + 1], ss.t[:, 2 * j + 1:2 * j + 2], ALU.add,
                           [ss], [rs])
                        rsqrt(rs.t[:, 2 * j:2 * j + 1], rs.t[:, 2 * j:2 * j + 1], 1024.0 * NORM_EPS, [rs], [rs])
                        tsc(anyeng(), hbj.t[:, :], xt_.t[:, j, :], rs.t[:, 2 * j:2 * j + 1], 32.0, ALU.mult, ALU.mult,
                            [xd_[j], rs], [hbj])

                        def trs(e, hbj=hbj):
                            ins = None
                            for kc in range(8):
                                ins = e.transpose(trps.t[:, kc * 128:(kc + 1) * 128], hbj.t[:, kc * 128:(kc + 1) * 128],
                                                  ident.t[:, :])
                            return ins
                        S.op("pe", trs, [hbj, ident], [trps])
                        cp(anyeng(("act", "dve")), hT_.t[:, :, j * 128:(j + 1) * 128],
                           trps.t[:, :].rearrange("p (k t) -> p k t", t=128), [trps], [hT_])
                    st.append(pj)
                return st

            if li == 0:
                load_x(0)
                for f_ in prologue_steps(0):
                    f_()
            def tile_ctx(i):
                g = li * NT + i
                xt = xs[g % 2]
                xjd = xjds[g % 2]
                hT = hTs[g % 2]
                t0 = i * TILE
                has_next = (g + 1 < len(layers) * NT)

                stage(2)

                def proj(cc):
                    if dry:
                        uses.append(cc)
                        return big.get()
                    u = wstate["u"]
                    assert uses[u] == cc, (u, uses[u], cc)
                    ensure_loads(u + WD)
                    slot = wring[u % WD]
                    wstate["u"] = u + 1
                    pb = big.get()

                    def f(e, pb=pb, slot=slot, hT=hT):
                        ins = None
                        for kc in range(8):
                            ins = e.matmul(pb.t[:, :], slot.t[:, kc * 128:(kc + 1) * 128], hT.t[:, kc, :],
                                           start=(kc == 0), stop=(kc == 7))
                        return ins
                    S.op("pe", f, [slot, hT], [pb])
                    return pb

                def tokshift(pb, cc, out):
                    zs = tmp.get()
                    cp("pool", zs.t[:, 0:1], carry.t[:, cc:cc + 1], [carry], [zs])
                    cp("act", zs.t[:, 1:513], pb.t[:, :], [pb], [zs])
                    cp("pool", carry.t[:, cc:cc + 1], zs.t[:, 512:513], [zs], [carry])
                    dd = tmp.get()
                    tt("pool", dd.t[:, 0:512], zs.t[:, 0:512], zs.t[:, 1:513], ALU.subtract, [zs], [dd])
                    stt(out.t[:, :], dd.t[:, 0:512], colv(O_MU, cc), zs.t[:, 1:513], ALU.mult, ALU.add,
                        [dd, zs, colt], [out])

                def s_zl():
                    tokshift(proj(16), 16, zl)
                    act(zl.t[0:64, :], zl.t[0:64, :], AF.Tanh, [zl], [zl])

                def s_vall():
                    vb4 = [tb16.get() for _ in range(4)] if not first_global else None
                    for hp in range(4):
                        tokshift(proj(8 + hp), 8 + hp, vp[hp])
                        if first_global:
                            dma(vf_d[hp, :, t0:t0 + TILE], vp[hp].t[:, :], [vp[hp]], [vf_dep[i]])
                        else:
                            cp(anyeng(), vb4[hp].t[:, :], vp[hp].t[:, :], [vp[hp]], [vb4[hp]])
                    if not first_global:
                        lo = big.get()

                        def f(e, lo=lo, vb4=vb4):
                            ins = None
                            for hp in range(4):
                                ins = e.matmul(lo.t[0:32, :], vdnb.t[:, hp, :], vb4[hp].t[:, :], start=(hp == 0),
                                               stop=(hp == 3))
                            return ins
                        S.op("pe", f, [vdnb] + vb4, [lo])
                        cp("act", lob.t[:, :], lo.t[0:32, :], [lo], [lob])
                        for hp in range(4):
                            gps_ = big.get()
                            mm(gps_.t[:, :], vupb.t[0:32, hp * 128:(hp + 1) * 128], lob.t[0:32, :], [vupb, lob], [gps_])
                            sgv = tmp.get()
                            act(sgv.t[:, 0:512], gps_.t[:, :], AF.Sigmoid, [gps_, colt], [sgv], bias=colv(O_VB, hp))
                            vfl = tmp.get()
                            dma(vfl.t[:, 0:512], vf_d[hp, :, t0:t0 + TILE], [vf_dep[i]], [vfl])
                            tt("pool", vfl.t[:, 0:512], vfl.t[:, 0:512], vp[hp].t[:, :], ALU.subtract, [vfl, vp[hp]], [vfl])
                            tt("dve", vfl.t[:, 0:512], vfl.t[:, 0:512], sgv.t[:, 0:512], ALU.mult, [vfl, sgv], [vfl])
                            tt("pool", vp[hp].t[:, :], vp[hp].t[:, :], vfl.t[:, 0:512], ALU.add, [vfl, vp[hp]], [vp[hp]])


                stage(4)
                v3 = lambda ap: ap.rearrange("p (c t) -> p c t", t=128)

                def A_steps(hp):
                    hs = slice(hp * 128, (hp + 1) * 128)
                    st = []
                    st.append(lambda: tokshift(proj(hp), hp, rp))
                    st.append(lambda: tokshift(proj(4 + hp), 4 + hp, kp))
                    st.append(lambda: tokshift(proj(12 + hp), 12 + hp, gp))

                    def s_w():
                        wps = big.get()
                        mm(wps.t[:, :], lora.t[0:64, hs], zl.t[0:64, :], [lora, zl], [wps])
                        act(sigw.t[:, :], wps.t[:, :], AF.Sigmoid, [wps, colt], [sigw], bias=colv(O_DB, hp))
                        S.op("dve", lambda e: e.tensor_tensor_scan(out=cl.t[:, :], data0=SCANM, data1=sigw.t[:, :],
                                                                   initial=0.0, op0=ALU.mult, op1=ALU.add),
                             [scm, sigw], [cl])
                        act(Wt.t[:, :], cl.t[:, :], AF.Exp, [cl], [Wt], scale=-C0)
                        act(Winv.t[:, :], cl.t[:, :], AF.Exp, [cl], [Winv], scale=C0)
                    st.append(s_w)

                    def s_a():
                        aps = big.get()
                        mm(aps.t[:, :], lora.t[64:128, hs], zl.t[64:128, :], [lora, zl], [aps])
                        act(aa.t[:, :], aps.t[:, :], AF.Sigmoid, [aps, colt], [aa], bias=colv(O_AB, hp))
                    st.append(s_a)

                    def s_kk():
                        sq = tb16.get()
                        act(sq.t[:, :], kp.t[:, :], AF.Square, [kp, colt], [sq], scale=colv(O_KK, hp))
                        ssp = big.get()
                        mm(ssp.t[:, :], onesb.t[:, :], sq.t[:, :], [onesb, sq], [ssp])
                        rn = tmp.get()
                        rsqrt(rn.t[:, 0:512], ssp.t[:, :], 1e-24, [ssp], [rn])
                        tsc("pool", kkn.t[:, :], kp.t[:, :], colv(O_KK, hp), 0.0, ALU.mult, ALU.add, [kp, colt], [kkn])
                        tt("pool", kkn.t[:, :], kkn.t[:, :], rn.t[:, 0:512], ALU.mult, [kkn, rn], [kkn])
                    st.append(s_kk)

                    def s_kb():
                        t1 = tmp.get()
                        tsc("pool", t1.t[:, 0:512], aa.t[:, :], colv(O_KA, hp), omka.t[:, hp:hp + 1], ALU.mult, ALU.add,
                            [aa, colt, omka], [t1])
                        tt("pool", kf.t[:, :], kp.t[:, :], t1.t[:, 0:512], ALU.mult, [kp, t1], [kf])
                        tt("pool", bb.t[:, :], kkn.t[:, :], aa.t[:, :], ALU.mult, [kkn, aa], [bb])
                    st.append(s_kb)
                    return st

                def B(hp):
                    ex = tmp.get()
                    tt("pool", ex.t[:, 0:512], cl.t[:, :], sigw.t[:, :], ALU.subtract, [cl, sigw], [ex])
                    Wprev = tmp.get()
                    act(Wprev.t[:, 0:512], ex.t[:, 0:512], AF.Exp, [ex], [Wprev], scale=-C0)
                    tt("dve", AR.t[:, :, 128:256], v3(rp.t[:, :]), v3(Wt.t[:, :]), ALU.mult, [rp, Wt], [AR])
                    stt(AR.t[:, :, 0:128], v3(kkn.t[:, :]), -1.0, v3(Wprev.t[:, 0:512]), ALU.mult, ALU.mult,
                        [kkn, Wprev], [AR])
                    tt("pool", Kt.t[:, :], kf.t[:, :], Winv.t[:, :], ALU.mult, [kf, Winv], [Kt])
                    tt("pool", Bt.t[:, :], bb.t[:, :], Winv.t[:, :], ALU.mult, [bb, Winv], [Bt])
                    for c in range(4):
                        cs = slice(c * 128, (c + 1) * 128)
                        wc = Wt.t[:, c * 128 + 127:c * 128 + 128]
                        tsc("dve", Kh.t[:, cs], Kt.t[:, cs], wc, None, ALU.mult, None, [Kt, Wt], [Kh])
                        tsc("dve", Bh.t[:, cs], Bt.t[:, cs], wc, None, ALU.mult, None, [Bt, Wt], [Bh])
                        cp("pool", wcs.t[:, c:c + 1], wc, [Wt], [wcs])
                    Vb = tb16.get()
                    cp("pool", Vb.t[:, :], vp[hp].t[:, :], [vp[hp]], [Vb])

                    def trs2(e):
                        ins = None
                        for c in range(4):
                            cs = slice(c * 128, (c + 1) * 128)
                            ins = e.transpose(trps.t[:, c * 128:(c + 1) * 128], Kh.t[:, cs], ident.t[:, :])
                            ins = e.transpose(trps.t[:, 512 + c * 128:512 + (c + 1) * 128], Bh.t[:, cs], ident.t[:, :])
                        return ins
                    S.op("pe", trs2, [Kh, Bh, ident], [trps])
                    cp("act", BKT.t[:, :], trps.t[:, :], [trps], [BKT])

                    def trs3(e, Vb=Vb):
                        ins = None
                        for c in range(4):
                            cs = slice(c * 128, (c + 1) * 128)
                            ins = e.transpose(trps.t[:, c * 128:(c + 1) * 128], Vb.t[:, cs], ident.t[:, :])
                        return ins
                    S.op("pe", trs3, [Vb, ident], [trps])
                    cp("dve", VT.t[:, :], trps.t[:, 0:512], [trps], [VT])
                    rk = tmp.get()
                    tt("pool", rk.t[:, 0:512], rp.t[:, :], kf.t[:, :], ALU.mult, [rp, kf], [rk])
                    rkb = tb16.get()
                    act(rkb.t[:, :], rk.t[:, 0:512], AF.Copy, [rk, colt], [rkb], scale=colv(O_RK, hp))
                    bsp = big.get()
                    mm(bsp.t[:, :], onesb.t[:, :], rkb.t[:, :], [onesb, rkb], [bsp])
                    tt("dve", bonb.t[:, :], bsp.t[:, :], vp[hp].t[:, :], ALU.mult, [bsp, vp[hp]], [bonb])
                    act(sgb.t[:, :], gp.t[:, :], AF.Silu, [gp], [sgb])

                    chains = [(c, hh) for c in range(4) for hh in range(2)]
                    for ci, (c, hh) in enumerate(chains):
                        cs = slice(c * 128, (c + 1) * 128)
                        R_ = slice(hh * 64, hh * 64 + 64)
                        gm, NL, Y = GmC[ci], NLC[ci], YC[ci]
                        gb = ipool.get()
                        g2 = ipool.get()

                        def fG(e, R_=R_, cs=cs, c=c, gb=gb):
                            e.matmul(gb.t[:, 0:256], Bt.t[R_, cs], AR.t[R_, c, :], start=True, stop=True)
                            return e.matmul(gb.t[:, 256:512], Kt.t[R_, cs], AR.t[R_, c, :], start=True, stop=True)
                        S.op("pe", fG, [Bt, Kt, AR], [gb])
                        mm(g2.t[:, 0:128], AR.t[R_, c, 0:128], Bt.t[R_, cs], [AR, Bt], [g2])
                        tt("dve", gm.t[:, :], gb.t[:, :], MASKB2, ALU.mult, [gb, mkb], [gm])
                        tt("dve", NL.t[:, 128:256], g2.t[:, 0:128], MASKL, ALU.mult, [g2, mkl], [NL])
                        cp("pool", NL.t[:, 0:128], gm.t[:, 0:128], [gm], [NL])
                        tt("pool", Y.t[:, :], gm.t[:, 0:128], ident.t[:, :], ALU.add, [gm, ident], [Y])
                    for lev in range(NLEV):
                        last = (lev == NLEV - 1)
                        sqs, yls = {}, {}
                        for step in range(8 + 3):
                            if step < 8:
                                NL = NLC[step]
                                sq = ipool.get()
                                sqs[step] = sq

                                def fsq(e, NL=NL, last=last, sq=sq):
                                    ins = e.matmul(sq.t[:, 128:256], NL.t[:, 0:128], NL.t[:, 128:256], start=True, stop=True)
                                    if not last:
                                        ins = e.matmul(sq.t[:, 0:128], NL.t[:, 128:256], NL.t[:, 0:128], start=True,
                                                       stop=True)
                                    return ins
                                S.op("pe", fsq, [NL], [sq])
                            ci = step - 1
                            if 0 <= ci < 8:
                                NL, sq = NLC[ci], sqs[ci]
                                if last:
                                    cp("act", NL.t[:, 128:256], sq.t[:, 128:256], [sq], [NL])
                                else:
                                    cp("act", NL.t[:, :], sq.t[:, 0:256], [sq], [NL])
                            ci = step - 2
                            if 0 <= ci < 8:
                                NL, Y = NLC[ci], YC[ci]
                                yl = ipool.get()
                                yls[ci] = yl
                                mm(yl.t[:, 0:128], NL.t[:, 128:256], Y.t[:, :], [NL, Y], [yl])
                            ci = step - 3
                            if 0 <= ci < 8:
                                Y, yl = YC[ci], yls[ci]
                                tt("dve", Y.t[:, :], yl.t[:, 0:128], Y.t[:, :], ALU.add, [yl, Y], [Y])

                def C_steps(hp):
                    st = []
                    for c in range(4):
                        cs = slice(c * 128, (c + 1) * 128)
                        info = []
                        for hh in range(2):
                            ci = c * 2 + hh
                            info.append(dict(ci=ci, hh=hh, h=2 * hp + hh, R_=slice(hh * 64, hh * 64 + 64),
                                             vs=slice(c * 128 + hh * 64, c * 128 + hh * 64 + 64), gm=GmC[ci], TT=YC[ci],
                                             p1ap=small[:, hh * 64:hh * 64 + 64],
                                             uap=small[:, 128 + hh * 64:128 + hh * 64 + 64],
                                             sap=small[hh * 64:hh * 64 + 64, 256 + hh * 64:256 + hh * 64 + 64]))

                        def s1(info=info, c=c):
                            for d in info:
                                def fP1(e, d=d):
                                    e.matmul(d["p1ap"], AR.t[d["R_"], c, 0:128], Sbf[d["R_"], hp, :], start=True, stop=False)
                                    return e.matmul(d["p1ap"], d["gm"].t[:, 256:384], VT.t[:, d["vs"]], start=False, stop=True)
                                S.op("pe", fP1, [AR, Sbdep[d["h"]], d["gm"], VT], [smalldep])
                            for d in info:
                                cp("dve", P1b[d["hh"]].t[:, :], d["p1ap"], [smalldep], [P1b[d["hh"]]])
                        st.append(s1)

                        def s2(info=info, c=c):
                            for d in info:
                                mm(d["uap"], d["TT"].t[:, :], P1b[d["hh"]].t[:, :], [d["TT"], P1b[d["hh"]]], [smalldep])
                            for d in info:
                                cp("dve", Ub[d["hh"]].t[:, :], d["uap"], [smalldep], [Ub[d["hh"]]])
                        st.append(s2)

                        def s3(info=info, c=c, cs=cs):
                            for d in info:
                                def fS(e, d=d):
                                    vs = d["vs"]
                                    e.matmul(d["sap"], BKT.t[:, 512 + vs.start:512 + vs.stop], Ub[d["hh"]].t[:, :],
                                             start=True, stop=False)
                                    return e.matmul(d["sap"], BKT.t[:, vs], VT.t[:, vs], start=False, stop=True)
                                S.op("pe", fS, [BKT, Ub[d["hh"]], VT], [smalldep])
                            for d in info:
                                def fY(e, d=d):
                                    R_ = d["R_"]
                                    e.matmul(Ytps.t[R_, cs], Sbf[R_, hp, :], AR.t[R_, c, 128:256], start=True, stop=False)
                                    e.matmul(Ytps.t[R_, cs], Ub[d["hh"]].t[:, :], d["gm"].t[:, 128:256], start=False, stop=False)
                                    return e.matmul(Ytps.t[R_, cs], VT.t[:, d["vs"]], d["gm"].t[:, 384:512], start=False,
                                                    stop=True)
                                S.op("pe", fY, [Sbdep[d["h"]], AR, Ub[d["hh"]], d["gm"], VT], [Ytps])
                            for d in info:
                                R_ = d["R_"]
                                stt(S32[R_, hp, :], S32[R_, hp, :], wcs.t[R_, c:c + 1], d["sap"], ALU.mult, ALU.add,
                                    [Sdep[d["h"]], wcs, smalldep], [Sdep[d["h"]]])
                            for d in info:
                                R_ = d["R_"]
                                cp("dve", Sbf[R_, hp, :], S32[R_, hp, :], [Sdep[d["h"]]], [Sbdep[d["h"]]])
                        st.append(s3)

                    def gn():
                        ysb = tmp.get()
                        cp("act", ysb.t[:, 0:512], Ytps.t[:, :], [Ytps], [ysb])
                        ybf = tb16.get()
                        cp("dve", ybf.t[:, :], ysb.t[:, 0:512], [ysb], [ybf])
                        mps = big.get()
                        mm(mps.t[:, :], ones64.t[:, :], ybf.t[:, :], [ones64, ybf], [mps])
                        dd = tmp.get()
                        tt("dve", dd.t[:, 0:512], ysb.t[:, 0:512], mps.t[:, :], ALU.subtract, [ysb, mps], [dd])
                        dsq = tb16.get()
                        act(dsq.t[:, :], dd.t[:, 0:512], AF.Square, [dd], [dsq])
                        vps = big.get()
                        mm(vps.t[:, :], ones64.t[:, :], dsq.t[:, :], [ones64, dsq], [vps])
                        rstd = tmp.get()
                        rsqrt(rstd.t[:, 0:512], vps.t[:, :], GN_EPS, [vps], [rstd])
                        tt("pool", dd.t[:, 0:512], dd.t[:, 0:512], rstd.t[:, 0:512], ALU.mult, [dd, rstd], [dd])
                        tsc("dve", dd.t[:, 0:512], dd.t[:, 0:512], colv(O_LG, hp), colv(O_LB, hp), ALU.mult, ALU.add,
                            [dd, colt], [dd])
                        tt("pool", dd.t[:, 0:512], dd.t[:, 0:512], bonb.t[:, :], ALU.add, [dd, bonb], [dd])
                        tt("pool", ycat.t[:, hp, :], dd.t[:, 0:512], sgb.t[:, :], ALU.mult, [dd, sgb], [ycat])
                    st.append(gn)
                    return st

                def conv_steps():
                    st = []
                    for cpi in range(4):
                        hold = {}

                        def c1(cpi=cpi, hold=hold):
                            pC = proj(21 + cpi)
                            Csb = tmp.get()
                            cp("act", Csb.t[:, 0:512], pC.t[:, :], [pC], [Csb])
                            pH = proj(25 + cpi)
                            u = tmp.get()
                            cp("pool", u.t[:, 0:2], ucarry.t[:, cpi, :], [ucarry], [u])
                            tt("dve", u.t[:, 2:514], Csb.t[:, 0:512], pH.t[:, :], ALU.mult, [Csb, pH], [u])
                            cp("pool", ucarry.t[:, cpi, :], u.t[:, 512:514], [u], [ucarry])
                            acc = tmp.get()
                            tsc("pool", acc.t[:, 0:512], u.t[:, 0:512], colv(O_CW, 0 * 4 + cpi), 0.0, ALU.mult, ALU.add,
                                [u, colt], [acc])
                            acc2 = tmp.get()
                            stt(acc2.t[:, 0:512], u.t[:, 1:513], colv(O_CW, 1 * 4 + cpi), acc.t[:, 0:512], ALU.mult, ALU.add,
                                [u, colt, acc], [acc2])
                            stt(acc.t[:, 0:512], u.t[:, 2:514], colv(O_CW, 2 * 4 + cpi), acc2.t[:, 0:512], ALU.mult, ALU.add,
                                [u, colt, acc2], [acc])
                            pB = proj(17 + cpi)
                            tt("dve", acc2.t[:, 0:512], pB.t[:, :], acc.t[:, 0:512], ALU.mult, [pB, acc], [acc2])
                            pG = proj(29 + cpi)
                            sg = tmp.get()
                            act(sg.t[:, 0:512], pG.t[:, :], AF.Silu, [pG], [sg])
                            tt("pool", ycat.t[:, 4 + cpi, :], acc2.t[:, 0:512], sg.t[:, 0:512], ALU.mult, [acc2, sg], [ycat])
                        st.append(c1)
                    return st

                def interleave(cs_, as_):
                    ia = 0
                    for k, cstep in enumerate(cs_):
                        cstep()
                        want = ((k + 1) * len(as_) + len(cs_) - 1) // len(cs_)
                        while ia < min(want, len(as_)):
                            as_[ia]()
                            ia += 1
                    while ia < len(as_):
                        as_[ia]()
                        ia += 1

                def mid():
                    for hp in range(3):
                        nxt = A_steps(hp + 1)
                        if hp == 1 and has_next:
                            nxt = nxt + prologue_steps(g + 1)
                        interleave(C_steps(hp), nxt)
                        B(hp + 1)

                def outproj_steps():
                    st = []
                    for j in range(4):
                        def oj(j=j):
                            for half in range(2):
                                pb = big.get()

                                def f(e, pb=pb, j=j, half=half):
                                    ins = None
                                    for kc in range(8):
                                        ins = e.matmul(pb.t[:, :], ycat.t[:, kc, j * 128:(j + 1) * 128],
                                                       Wob.t[:, kc, half * 512:(half + 1) * 512], start=(kc == 0),
                                                       stop=(kc == 7))
                                    return ins
                                S.op("pe", f, [ycat, Wob], [pb])
                                tt("dve", xt.t[:, j, half * 512:(half + 1) * 512], xt.t[:, j, half * 512:(half + 1) * 512],
                                   pb.t[:, :], ALU.add, [xjd[j], pb], [xjd[j]])
                            if last_global and final_norm:
                                junk = tmp.get()
                                act(junk.t[:, 0:512], xt.t[:, j, 0:512], AF.Square, [xjd[j]], [junk, ss], accum=ss.t[:, 0:1])
                                act(junk.t[:, 0:512], xt.t[:, j, 512:1024], AF.Square, [xjd[j]], [junk, ss],
                                    accum=ss.t[:, 1:2])
                                tt("dve", rs.t[:, 0:1], ss.t[:, 0:1], ss.t[:, 1:2], ALU.add, [ss], [rs])
                                rsqrt(rs.t[:, 0:1], rs.t[:, 0:1], 1024.0 * NORM_EPS, [rs], [rs])
                                stt(xt.t[:, j, :], xt.t[:, j, :], rs.t[:, 0:1], fgb.t[:, :], ALU.mult, ALU.mult,
                                    [xjd[j], rs, fgb], [xjd[j]])
                            r0 = t0 + j * 128
                            dma(xout_d[r0:r0 + 128, :], xt.t[:, j, :], [xjd[j]],
                                [xmid_dep[i]] if li < len(layers) - 1 else [])
                        st.append(oj)
                    return st

                return types.SimpleNamespace(g=g, has_next=has_next, head1=lambda: [s_zl, s_vall],
                                             head2=lambda: A_steps(0) + [lambda: B(0)], mid=mid,
                                             c3=lambda: C_steps(3), conv=conv_steps, outproj=outproj_steps,
                                             interleave=interleave)

            ctxs = [tile_ctx(i) for i in range(NT)]
            for f_ in ctxs[0].head1() + ctxs[0].head2():
                f_()
            for i in range(NT):
                c_ = ctxs[i]
                n_ = ctxs[i + 1] if i + 1 < NT else None
                if c_.has_next:
                    load_x(c_.g + 1)
                c_.mid()
                c_.interleave(c_.c3(), c_.conv() + (n_.head1() if n_ else []))
                c_.interleave(c_.outproj(), n_.head2() if n_ else [])

        all_pools.extend([tmp, tb16, wstp, big, ipool])
        emit_all(True)
        emit_all(False)
        sp = S.engs["sp"]
        S.final = [(sp.dsems[k], sp.dvals[k]) for k in range(len(sp.dsems)) if sp.dvals[k] > 0]

        with nc.Block() as block:
            @block.sync
            def _(e):
                S.replay("sp", e)

            @block.tensor
            def _(e):
                S.replay("pe", e)

            @block.scalar
            def _(e):
                S.replay("act", e)

            @block.vector
            def _(e):
                S.replay("dve", e)

            @block.gpsimd
            def _(e):
                S.replay("pool", e)
    return nc


def make_consts():
    c = np.zeros((128, 1408), np.float32)
    c[:, 0:128] = np.eye(128, dtype=np.float32)
    blk = np.zeros((128, 128), np.float32)
    blk[0:64, 0:64] = 1.0
    blk[64:128, 64:128] = 1.0
    c[:, 128:256] = blk
    s = np.arange(128)[:, None]
    t = np.arange(128)[None, :]
    strict = (s < t).astype(np.float32)
    incl = (s <= t).astype(np.float32)
    c[:, 256:384] = strict
    c[:, 384:512] = incl
    c[:, 512:640] = strict
    c[:, 640:768] = incl
    c[:, 768:896] = (s > t).astype(np.float32)
    m = np.ones((128, 512), np.float32)
    m[:, 0::128] = 0.0
    c[:, 896:1408] = m
    return c


def pack_cols(l, shift_mu, decay_bias, aaa_bias, k_k, k_a, r_k, ln_gain, ln_bias, v_bias, conv_w, norm_gain):
    c = np.zeros((128, NCOLS), np.float32)
    pc = lambda v: np.ascontiguousarray(v.reshape(-1, 128).T)
    c[:, O_MU:O_MU + 17] = pc(shift_mu[l])
    c[:, O_DB:O_DB + 4] = pc(decay_bias[l])
    c[:, O_AB:O_AB + 4] = pc(aaa_bias[l])
    c[:, O_KK:O_KK + 4] = pc(k_k[l])
    c[:, O_KA:O_KA + 4] = pc(k_a[l])
    c[:, O_RK:O_RK + 4] = pc(r_k[l].reshape(-1))
    c[:, O_LG:O_LG + 4] = pc(ln_gain[l])
    c[:, O_LB:O_LB + 4] = pc(ln_bias[l])
    if l >= 1:
        c[:, O_VB:O_VB + 4] = pc(v_bias[l - 1])
    for k in range(3):
        c[:, O_CW + 4 * k:O_CW + 4 * k + 4] = pc(conv_w[l, k])
    c[:, O_NG:O_NG + 8] = pc(norm_gain[l])
    return c


_NC_CACHE = {}


def _get_nc(key, **kw):
    if key not in _NC_CACHE:
        _NC_CACHE[key] = build(**kw)
    return _NC_CACHE[key]


def host_prep(inp, layers):
    f = lambda a: np.ascontiguousarray(np.asarray(a, dtype=np.float32))
    p = {k: f(v) for k, v in inp.items() if k != "x"}
    cols = np.stack([pack_cols(l, p["shift_mu"], p["decay_bias"], p["aaa_bias"], p["k_k"], p["k_a"], p["r_k"],
                               p["ln_gain"], p["ln_bias"], p["v_bias"], p["conv_w"], p["norm_gain"]) for l in layers])
    lora = np.stack([np.concatenate([p["decay_up"][l], p["aaa_up"][l]], axis=0) for l in layers])
    return {
        "w_in": f(p["w_in"][layers]),
        "w_out": f(p["w_out"][layers]),
        "cols": f(cols),
        "lora": f(lora),
        "v_down": f(p["v_down"][0]),
        "v_up": f(p["v_up"][0]),
        "final_gain": f(p["final_gain"].reshape(1, -1)),
        "consts": make_consts(),
    }


FUSED = True


def kernel(**inputs):
    x = np.ascontiguousarray(np.asarray(inputs["x"], dtype=np.float32))
    B, T, _ = x.shape
    n_layers = np.asarray(inputs["w_in"]).shape[0]
    cores = list(range(B))
    if FUSED:
        nc = _get_nc(("fused", T), T=T, layers=list(range(n_layers)), n_layers_total=n_layers)
        shared = host_prep(inputs, list(range(n_layers)))
        in_maps = [dict(shared, x=x[b]) for b in range(B)]
        res = run_bass_kernel_spmd(nc, in_maps, core_ids=cores)
        return np.stack([np.asarray(r["out"]) for r in res.results]).astype(np.float32)
    cur = [x[b] for b in range(B)]
    vf = None
    for l in range(n_layers):
        nc = _get_nc(("layer", T, l, n_layers), T=T, layers=[l], n_layers_total=n_layers,
                     vf_in=(l > 0), vf_out=(l == 0), final_norm=(l == n_layers - 1))
        shared = host_prep(inputs, [l])
        in_maps = []
        for b in range(B):
            m = dict(shared, x=cur[b])
            if l > 0:
                m["vf"] = vf[b]
            in_maps.append(m)
        res = run_bass_kernel_spmd(nc, in_maps, core_ids=cores)
        cur = [np.asarray(r["out"]) for r in res.results]
        if l == 0:
            vf = [np.asarray(r["vf"]) for r in res.results]
    return np.stack(cur).astype(np.float32)
```

```python
import contextlib
import types
import numpy as np
import concourse.bass as bass
import concourse.mybir as mybir
from concourse.bass_utils import run_bass_kernel_spmd

F32 = mybir.dt.float32
BF16 = mybir.dt.bfloat16
ALU = mybir.AluOpType
AF = mybir.ActivationFunctionType

D_MODEL = 1024
IN_COLS = 4224
NCH = 33
TILE = 512
CH = 128
C0 = float(np.exp(-0.5))
NORM_EPS = 1e-6
GN_EPS = 64e-5
NLEV = 6

O_MU, O_DB, O_AB, O_KK, O_KA, O_RK, O_LG, O_LB, O_VB, O_CW, O_NG = 0, 17, 21, 25, 29, 33, 37, 41, 45, 49, 61
NCOLS = 69


class StopBuild(Exception):
    pass


STAGE_LIMIT = [None]


def stage(n):
    if STAGE_LIMIT[0] is not None and STAGE_LIMIT[0] == n:
        raise StopBuild()


class Dep:
    __slots__ = ("lw", "rd", "name", "psum")

    def __init__(self, name="", psum=False):
        self.lw = {}
        self.rd = {}
        self.name = name
        self.psum = psum


class Buf:
    def __init__(self, t, name=""):
        self.t = t
        self.dep = Dep(name)

    def __getitem__(self, k):
        return self.t[k]


class Eng:
    def __init__(self, name):
        self.name = name
        self.ops = []
        self.count = 0
        self.seen = {}
        self.sem = None
        self.dsems = []
        self.dvals = []
        self.dnext = 0


def _dep(x):
    return x.dep if isinstance(x, Buf) else x


class Sched:
    def __init__(self, nc, es, ndma=None):
        self.nc = nc
        ndma = ndma or {"sp": 8, "pool": 4, "act": 2}
        self.engs = {}
        for n in ("pe", "act", "dve", "pool", "sp"):
            e = Eng(n)
            e.sem = es.enter_context(nc.semaphore("s_" + n))
            for k in range(ndma.get(n, 0)):
                e.dsems.append(es.enter_context(nc.semaphore("d_%s%d" % (n, k))))
                e.dvals.append(0)
            self.engs[n] = e
        self.final = []

    def op(self, eng, fn, R=(), W=(), dma=False):
        if getattr(self, "dry", False):
            return ("dry", 0)
        E = self.engs[eng]
        waits = {}

        def need(sem, val):
            if sem is E.sem and eng == "pe":
                return
            if E.seen.get(sem, 0) >= val:
                return
            if waits.get(sem, 0) < val:
                waits[sem] = val

        for b in R:
            d = _dep(b)
            for s, v in d.lw.items():
                need(s, v)
            if d.psum:
                for s, v in d.rd.items():
                    if s is not E.sem:
                        need(s, v)
        for b in W:
            d = _dep(b)
            for s, v in d.lw.items():
                need(s, v)
            for s, v in d.rd.items():
                need(s, v)
        if dma:
            k = E.dnext
            E.dnext = (k + 1) % len(E.dsems)
            sem = E.dsems[k]
            if E.dvals[k] > 0:
                need(sem, E.dvals[k])
            E.dvals[k] += 16
            tok = (sem, E.dvals[k])
            inc = 16
        else:
            E.count += 1
            tok = (E.sem, E.count)
            inc = 1
        for s, v in waits.items():
            E.seen[s] = v
        E.ops.append((list(waits.items()), fn, tok[0], inc))
        for b in R:
            d = _dep(b)
            if d.rd.get(tok[0], 0) < tok[1]:
                d.rd[tok[0]] = tok[1]
        for b in W:
            d = _dep(b)
            if d.lw.get(tok[0], 0) < tok[1]:
                d.lw[tok[0]] = tok[1]
        return tok

    def replay(self, name, e):
        E = self.engs[name]
        for waits, fn, sem, inc in E.ops:
            for s, v in waits:
                e.wait_ge(s, v)
            ins = fn(e)
            ins.then_inc(sem, inc)
        if name == "sp":
            for s, v in self.final:
                e.wait_ge(s, v)


def build(T, layers, n_layers_total, x_kind="ExternalInput", out_kind="ExternalOutput",
          vf_in=False, vf_out=False, final_norm=True, dbg=None):
    NT = T // TILE
    nc = bass.Bass("TRN2", target_bir_lowering=False)
    dr = lambda name, shape, kind: nc.dram_tensor(name, shape, F32, kind=kind).ap()
    x_d = dr("x", [T, D_MODEL], "ExternalInput")
    out_d = dr("out", [T, D_MODEL], "ExternalOutput")
    w_in_d = dr("w_in", [len(layers), D_MODEL, IN_COLS], "ExternalInput")
    w_out_d = dr("w_out", [len(layers), D_MODEL, D_MODEL], "ExternalInput")
    cols_d = dr("cols", [len(layers), 128, NCOLS], "ExternalInput")
    lora_d = dr("lora", [len(layers), 128, 512], "ExternalInput")
    vdn_d = dr("v_down", [512, 32], "ExternalInput")
    vup_d = dr("v_up", [32, 512], "ExternalInput")
    fg_d = dr("final_gain", [1, D_MODEL], "ExternalInput")
    cst_d = dr("consts", [128, 1408], "ExternalInput")
    if vf_in:
        vf_d = dr("vf", [4, 128, T], "ExternalInput")
    elif vf_out:
        vf_d = dr("vf", [4, 128, T], "ExternalOutput")
    else:
        vf_d = dr("vf", [4, 128, T], "Internal")
    xmid_d = dr("xmid", [T, D_MODEL], "Internal") if len(layers) > 1 else None
    wscr_d = nc.dram_tensor("wscr", [len(layers), NCH, 128, 1024], BF16, kind="Internal").ap()
    dbg_d = {}
    if dbg:
        for k, shp in dbg.items():
            dbg_d[k] = dr("dbg_" + k, shp, "ExternalOutput")

    es = contextlib.ExitStack()
    with es:
        es.enter_context(nc.allow_low_precision("bf16 matmul operands, fp32 accumulation"))
        S = Sched(nc, es)

        def sb(name, shape, dt=F32):
            return Buf(es.enter_context(nc.sbuf_tensor("sb_" + name, shape, dt)), name)

        def ps(name, shape, dt=F32):
            b = Buf(es.enter_context(nc.psum_tensor("ps_" + name, shape, dt)), name)
            b.dep.psum = True
            return b

        WD = 6
        wring = [sb("wr%d" % k, [128, 1024], BF16) for k in range(WD)]
        wscr_dep = [[Dep("wscr%d_%d" % (k, gq)) for gq in range(9)] for k in range(len(layers))]
        TSEQ = [16, 8, 9, 10, 11]
        for hp_ in range(4):
            TSEQ += [hp_, 4 + hp_, 12 + hp_]
        for c_ in range(4):
            TSEQ += [21 + c_, 25 + c_, 17 + c_, 29 + c_]
        Wob = sb("Wob", [128, 8, D_MODEL], BF16)
        lora = sb("lora", [128, 512])
        colt = sb("colt", [128, NCOLS])
        omka = sb("omka", [128, 4])
        vdnb = sb("vdnb", [128, 4, 32], BF16)
        vupb = sb("vupb", [32, 512], BF16)
        mkb = sb("mkb", [128, 512], BF16)
        mkl = sb("mkl", [128, 128], BF16)
        scm = sb("scm", [128, 512])
        ident = sb("ident", [128, 128], BF16)
        onesb = sb("onesb", [128, 128], BF16)
        ones64 = sb("ones64", [128, 128], BF16)
        fgb = sb("fgb", [128, D_MODEL])
        MASKB2 = mkb.t[:, :]
        MASKL = mkl.t[:, :]
        SCANM = scm.t[:, :]

        xs = [sb("xs%d" % k, [128, 4, D_MODEL]) for k in range(2)]
        xjds = [[Dep("xj%d_%d" % (q, k)) for k in range(4)] for q in range(2)]
        hTs = [sb("hT%d" % k, [128, 8, TILE], BF16) for k in range(2)]
        hb = [sb("hb%d" % k, [128, D_MODEL], BF16) for k in range(2)]
        ycat = sb("ycat", [128, 8, TILE], BF16)
        ss = sb("ss", [128, 8])
        rs = sb("rs", [128, 8])
        carry = sb("carry", [128, 17])
        ucarry = sb("ucarry", [128, 4, 2])
        S32 = es.enter_context(nc.sbuf_tensor("S32", [128, 4, 64], F32))
        Sbf = es.enter_context(nc.sbuf_tensor("Sbf", [128, 4, 64], BF16))
        Sdep = [Dep("S%d" % h) for h in range(8)]
        Sbdep = [Dep("Sb%d" % h) for h in range(8)]

        class Pool:
            def __init__(self, name, n, shape, dt=F32, mk=sb):
                self.b = [mk("%s%d" % (name, k), shape, dt) for k in range(n)]
                self.i = 0

            def get(self):
                b = self.b[self.i]
                self.i = (self.i + 1) % len(self.b)
                return b

        tmp = Pool("tmp", 7, [128, 514])
        tb16 = Pool("tb", 4, [128, TILE], BF16)
        wstp = Pool("wst", 2, [128, 4, 1024], BF16)
        zl = sb("zl", [128, TILE])
        vp = [sb("vp%d" % k, [128, TILE]) for k in range(4)]
        rp, kp, gp = sb("rp", [128, TILE]), sb("kp", [128, TILE]), sb("gp", [128, TILE])
        sigw, cl, aa = sb("sigw", [128, TILE]), sb("cl", [128, TILE]), sb("aa", [128, TILE])
        Wt, Winv = sb("Wt", [128, TILE]), sb("Winv", [128, TILE])
        kkn, kf, bb = sb("kkn", [128, TILE]), sb("kf", [128, TILE]), sb("bb", [128, TILE])
        AR = sb("AR", [128, 4, 256], BF16)
        Bt, Kt = sb("Bt", [128, TILE], BF16), sb("Kt", [128, TILE], BF16)
        Bh, Kh = sb("Bh", [128, TILE], BF16), sb("Kh", [128, TILE], BF16)
        BKT = sb("BKT", [128, 1024], BF16)
        VT = sb("VT", [128, TILE], BF16)
        lob = sb("lob", [32, TILE], BF16)
        wcs = sb("wcs", [128, 4])
        bonb = sb("bonb", [128, TILE])
        sgb = sb("sgb", [128, TILE], BF16)
        GmC = [sb("Gm%d" % k, [128, 512], BF16) for k in range(8)]
        NLC = [sb("NL%d" % k, [128, 256], BF16) for k in range(8)]
        YC = [sb("Yv%d" % k, [128, 128], BF16) for k in range(8)]
        P1b = [sb("P1b%d" % k, [128, 64], BF16) for k in range(2)]
        Ub = [sb("Ub%d" % k, [128, 64], BF16) for k in range(2)]

        big = Pool("pbig", 2, [128, 512], F32, mk=ps)
        trps = ps("ptr", [128, 1024], BF16)
        smallb = ps("psmall", [128, 512])
        small = smallb.t
        smalldep = smallb.dep
        P1ps = [smalldep for k in range(2)]
        Ups = [smalldep for k in range(2)]
        Stps = [smalldep for k in range(2)]
        invbs = [ps("pinv%d" % k, [128, 512]) for k in range(3)]
        Ytps = ps("pYt", [128, 512])

        class RR:
            def __init__(self, lst):
                self.b = lst
                self.i = 0

            def get(self):
                b = self.b[self.i]
                self.i = (self.i + 1) % len(self.b)
                return b
        ipool = RR(invbs + [smallb, Ytps])

        xmid_dep = [Dep("xmid%d" % k) for k in range(NT)]
        vf_dep = [Dep("vf%d" % k) for k in range(NT)]
        def act(out, in_, func, R, W, bias=0.0, scale=1.0, accum=None):
            kw = {}
            if accum is not None:
                kw["accum_out"] = accum
            S.op("act", lambda e: e.activation(out=out, in_=in_, func=func, bias=bias, scale=scale, **kw), R, W)

        def tt(eng, out, in0, in1, op, R, W):
            S.op(eng, lambda e: e.tensor_tensor(out=out, in0=in0, in1=in1, op=op), R, W)

        def tsc(eng, out, in0, s1, s2, op0, op1, R, W):
            if op1 is None:
                S.op(eng, lambda e: e.tensor_scalar(out=out, in0=in0, scalar1=s1, scalar2=None, op0=op0), R, W)
            else:
                S.op(eng, lambda e: e.tensor_scalar(out=out, in0=in0, scalar1=s1, scalar2=s2, op0=op0, op1=op1), R, W)

        def stt(out, in0, scalar, in1, op0, op1, R, W):
            S.op("dve", lambda e: e.scalar_tensor_tensor(out=out, in0=in0, scalar=scalar, in1=in1, op0=op0, op1=op1), R, W)

        def rsqrt(out, in_, eps, R, W):
            act(out, in_, AF.Sqrt, R, W, bias=eps)
            S.op("dve", lambda e: e.reciprocal(out=out, in_=out), W, W)

        def cp(eng, out, in_, R, W):
            if eng == "act":
                S.op("act", lambda e: e.copy(out=out, in_=in_), R, W)
            else:
                S.op(eng, lambda e: e.tensor_copy(out=out, in_=in_), R, W)

        def mm(out, lhsT, rhs, R, W, start=True, stop=True):
            S.op("pe", lambda e: e.matmul(out, lhsT, rhs, start=start, stop=stop), R, W)

        def dma(out, in_, R, W, eng="sp"):
            return S.op(eng, lambda e: e.dma_start(out=out, in_=in_), R, W, dma=True)

        rr = {"i": 0}

        def anyeng(choices=("dve", "pool")):
            rr["i"] += 1
            return choices[rr["i"] % len(choices)]

        c0_ = tmp.get()
        dma(c0_.t[:, 0:256], cst_d[:, 0:256], [], [c0_])
        cp("dve", ident.t[:, :], c0_.t[:, 0:128], [c0_], [ident])
        cp("dve", onesb.t[:, :], c0_.t[:, 128:256], [c0_], [onesb])
        tsc("dve", ones64.t[:, :], c0_.t[:, 128:256], 1.0 / 64.0, None, ALU.mult, None, [c0_], [ones64])
        c1_ = tmp.get()
        dma(c1_.t[:, 0:512], cst_d[:, 256:768], [], [c1_])
        cp("dve", mkb.t[:, :], c1_.t[:, 0:512], [c1_], [mkb])
        c2_ = tmp.get()
        dma(c2_.t[:, 0:128], cst_d[:, 768:896], [], [c2_])
        cp("dve", mkl.t[:, :], c2_.t[:, 0:128], [c2_], [mkl])
        c3_ = tmp.get()
        dma(c3_.t[:, 0:512], cst_d[:, 896:1408], [], [c3_])
        cp("dve", scm.t[:, :], c3_.t[:, 0:512], [c3_], [scm])
        if final_norm:
            dma(fgb.t[:, :], fg_d[0:1, :].partition_broadcast(128), [], [fgb])
            tsc("dve", fgb.t[:, :], fgb.t[:, :], 32.0, None, ALU.mult, None, [fgb], [fgb])

        def colv(off, j=0):
            return colt.t[:, off + j:off + j + 1]

        proj_log = {}
        all_pools = []

        def emit_all(dry):
          S.dry = dry
          for p_ in all_pools:
              p_.i = 0
          rr["i"] = 0
          for li, l in enumerate(layers):
            first_global = (l == 0)
            last_global = (l == n_layers_total - 1)
            xin_d = x_d if li == 0 else xmid_d
            xout_d = out_d if li == len(layers) - 1 else xmid_d

            dma(colt.t[:, :], cols_d[li], [], [colt])
            dma(lora.t[:, :], lora_d[li], [], [lora])
            tsc("dve", omka.t[:, :], colt.t[:, O_KA:O_KA + 4], -1.0, 1.0, ALU.mult, ALU.add, [colt], [omka])
            S.op("pool", lambda e: e.memset(carry.t[:, :], 0.0), [], [carry])
            S.op("pool", lambda e: e.memset(ucarry.t[:, :, :], 0.0), [], [ucarry])
            S.op("pool", lambda e: e.memset(S32[:, :, :], 0.0), [], Sdep)
            S.op("pool", lambda e: e.memset(Sbf[:, :, :], 0.0), [], Sbdep)
            k3c = {'v': 0}
            prep_steps = []
            for gq in (4, 2, 0, 1, 3, 5, 6, 7, 8):
                def pstep(gq=gq):
                    c0 = gq * 512
                    w = min(512, IN_COLS - c0)
                    cc0, nch = c0 // 128, w // 128
                    stg = wstp.get()
                    for kc in range(8):
                        st = tmp.get()
                        dma(st.t[:, 0:w], w_in_d[li, kc * 128:(kc + 1) * 128, c0:c0 + w], [], [st])
                        eng = ("act", "dve")[k3c['v'] % 2]
                        k3c['v'] += 1
                        dst = stg.t[:, 0:nch, kc * 128:(kc + 1) * 128]
                        src = st.t[:, 0:w].rearrange("p (c m) -> p c m", m=128)
                        if eng == "act":
                            act(dst, src, AF.Copy, [st, colt], [stg], scale=colv(O_NG, kc))
                        else:
                            tsc(eng, dst, src, colv(O_NG, kc), None, ALU.mult, None, [st, colt], [stg])
                    dma(wscr_d[li, cc0:cc0 + nch, :, :].rearrange("c p m -> p c m"), stg.t[:, 0:nch, :], [stg],
                        [wscr_dep[li][gq]])

                prep_steps.append(pstep)

            def wout_prep():
                for kc in range(8):
                    for c0 in range(0, D_MODEL, 512):
                        st = tmp.get()
                        dma(st.t[:, 0:512], w_out_d[li, kc * 128:(kc + 1) * 128, c0:c0 + 512], [], [st])
                        eng = ("act", "dve")[k3c['v'] % 2]
                        k3c['v'] += 1
                        cp(eng, Wob.t[:, kc, c0:c0 + 512], st.t[:, 0:512], [st], [Wob])

            if not first_global:
                st = tmp.get()
                dma(st.t[:, 0:128].rearrange("p (h m) -> p h m", m=32),
                    vdn_d.rearrange("(h p) m -> p h m", p=128), [], [st])
                cp("dve", vdnb.t[:, :, :], st.t[:, 0:128].rearrange("p (h m) -> p h m", m=32), [st], [vdnb])
                st = tmp.get()
                dma(st.t[0:32, 0:512], vup_d[:, :], [], [st])
                cp("dve", vupb.t[:, :], st.t[0:32, 0:512], [st], [vupb])

            if dry:
                proj_log[li] = []
            uses = proj_log[li]
            wstate = {"u": 0, "l": 0}

            def ensure_loads(upto, li=li, uses=uses, wstate=wstate):
                while wstate["l"] < min(upto, len(uses)):
                    k = wstate["l"]
                    slot = wring[k % WD]
                    assert wscr_dep[li][uses[k] // 4].lw, ("weight group not prepared yet", uses[k])
                    dma(slot.t[:, :], wscr_d[li, uses[k]], [wscr_dep[li][uses[k] // 4]], [slot])
                    wstate["l"] = k + 1

            stage(1)
            def xsrc(g):
                lj, ij = divmod(g, NT)
                src = x_d if lj == 0 else xmid_d
                return src, ([xmid_dep[ij]] if lj > 0 else []), ij

            def load_x(g):
                src, deps, ij = xsrc(g)
                for j in range(4):
                    r0 = ij * TILE + j * 128
                    dma(xs[g % 2].t[:, j, :], src[r0:r0 + 128, :], deps, [xjds[g % 2][j]])

            def prologue_steps(g):
                xt_, xd_, hT_ = xs[g % 2], xjds[g % 2], hTs[g % 2]
                st = []
                for j in range(4):
                    def pj(j=j):
                        junk = tmp.get()
                        hbj = hb[j % 2]
                        act(junk.t[:, 0:512], xt_.t[:, j, 0:512], AF.Square, [xd_[j]], [junk, ss],
                            accum=ss.t[:, 2 * j:2 * j + 1])
                        act(junk.t[:, 0:512], xt_.t[:, j, 512:1024], AF.Square, [xd_[j]], [junk, ss],
                            accum=ss.t[:, 2 * j + 1:2 * j + 2])
                        tt("dve", rs.t[:, 2 * j:2 * j + 1], ss.t[:, 2 * j:2 * j + 1], ss.t[:, 2 * j + 1:2 * j + 2], ALU.add,
                           [ss], [rs])
                        rsqrt(rs.t[:, 2 * j:2 * j + 1], rs.t[:, 2 * j:2 * j + 1], 1024.0 * NORM_EPS, [rs], [rs])
                        tsc(anyeng(), hbj.t[:, :], xt_.t[:, j, :], rs.t[:, 2 * j:2 * j + 1], 32.0, ALU.mult, ALU.mult,
                            [xd_[j], rs], [hbj])

                        def trs(e, hbj=hbj):
                            ins = None
                            for kc in range(8):
                                ins = e.transpose(trps.t[:, kc * 128:(kc + 1) * 128], hbj.t[:, kc * 128:(kc + 1) * 128],
                                                  ident.t[:, :])
                            return ins
                        S.op("pe", trs, [hbj, ident], [trps])
                        cp(anyeng(("act", "dve")), hT_.t[:, :, j * 128:(j + 1) * 128],
                           trps.t[:, :].rearrange("p (k t) -> p k t", t=128), [trps], [hT_])
                    st.append(pj)
                return st

            if li == 0:
                load_x(0)
                for f_ in prologue_steps(0):
                    f_()
            def tile_ctx(i):
                g = li * NT + i
                xt = xs[g % 2]
                xjd = xjds[g % 2]
                hT = hTs[g % 2]
                t0 = i * TILE
                has_next = (g + 1 < len(layers) * NT)

                stage(2)

                def proj(cc):
                    if dry:
                        uses.append(cc)
                        return big.get()
                    u = wstate["u"]
                    assert uses[u] == cc, (u, uses[u], cc)
                    ensure_loads(u + WD)
                    slot = wring[u % WD]
                    wstate["u"] = u + 1
                    pb = big.get()

                    def f(e, pb=pb, slot=slot, hT=hT):
                        ins = None
                        for kc in range(8):
                            ins = e.matmul(pb.t[:, :], slot.t[:, kc * 128:(kc + 1) * 128], hT.t[:, kc, :],
                                           start=(kc == 0), stop=(kc == 7))
                        return ins
                    S.op("pe", f, [slot, hT], [pb])
                    return pb

                def tokshift(pb, cc, out):
                    zs = tmp.get()
                    cp("pool", zs.t[:, 0:1], carry.t[:, cc:cc + 1], [carry], [zs])
                    cp("act", zs.t[:, 1:513], pb.t[:, :], [pb], [zs])
                    cp("pool", carry.t[:, cc:cc + 1], zs.t[:, 512:513], [zs], [carry])
                    dd = tmp.get()
                    tt("pool", dd.t[:, 0:512], zs.t[:, 0:512], zs.t[:, 1:513], ALU.subtract, [zs], [dd])
                    stt(out.t[:, :], dd.t[:, 0:512], colv(O_MU, cc), zs.t[:, 1:513], ALU.mult, ALU.add,
                        [dd, zs, colt], [out])

                def s_zl():
                    tokshift(proj(16), 16, zl)
                    act(zl.t[0:64, :], zl.t[0:64, :], AF.Tanh, [zl], [zl])

                def s_vall():
                    vb4 = [tb16.get() for _ in range(4)] if not first_global else None
                    for hp in range(4):
                        tokshift(proj(8 + hp), 8 + hp, vp[hp])
                        if first_global:
                            dma(vf_d[hp, :, t0:t0 + TILE], vp[hp].t[:, :], [vp[hp]], [vf_dep[i]])
                        else:
                            cp(anyeng(), vb4[hp].t[:, :], vp[hp].t[:, :], [vp[hp]], [vb4[hp]])
                    if not first_global:
                        lo = big.get()

                        def f(e, lo=lo, vb4=vb4):
                            ins = None
                            for hp in range(4):
                                ins = e.matmul(lo.t[0:32, :], vdnb.t[:, hp, :], vb4[hp].t[:, :], start=(hp == 0),
                                               stop=(hp == 3))
                            return ins
                        S.op("pe", f, [vdnb] + vb4, [lo])
                        cp("act", lob.t[:, :], lo.t[0:32, :], [lo], [lob])
                        for hp in range(4):
                            gps_ = big.get()
                            mm(gps_.t[:, :], vupb.t[0:32, hp * 128:(hp + 1) * 128], lob.t[0:32, :], [vupb, lob], [gps_])
                            sgv = tmp.get()
                            act(sgv.t[:, 0:512], gps_.t[:, :], AF.Sigmoid, [gps_, colt], [sgv], bias=colv(O_VB, hp))
                            vfl = tmp.get()
                            dma(vfl.t[:, 0:512], vf_d[hp, :, t0:t0 + TILE], [vf_dep[i]], [vfl])
                            tt("pool", vfl.t[:, 0:512], vfl.t[:, 0:512], vp[hp].t[:, :], ALU.subtract, [vfl, vp[hp]], [vfl])
                            tt("dve", vfl.t[:, 0:512], vfl.t[:, 0:512], sgv.t[:, 0:512], ALU.mult, [vfl, sgv], [vfl])
                            tt("pool", vp[hp].t[:, :], vp[hp].t[:, :], vfl.t[:, 0:512], ALU.add, [vfl, vp[hp]], [vp[hp]])


                stage(4)
                v3 = lambda ap: ap.rearrange("p (c t) -> p c t", t=128)

                def A_steps(hp):
                    hs = slice(hp * 128, (hp + 1) * 128)
                    st = []
                    st.append(lambda: tokshift(proj(hp), hp, rp))
                    st.append(lambda: tokshift(proj(4 + hp), 4 + hp, kp))
                    st.append(lambda: tokshift(proj(12 + hp), 12 + hp, gp))

                    def s_w():
                        wps = big.get()
                        mm(wps.t[:, :], lora.t[0:64, hs], zl.t[0:64, :], [lora, zl], [wps])
                        act(sigw.t[:, :], wps.t[:, :], AF.Sigmoid, [wps, colt], [sigw], bias=colv(O_DB, hp))
                        S.op("dve", lambda e: e.tensor_tensor_scan(out=cl.t[:, :], data0=SCANM, data1=sigw.t[:, :],
                                                                   initial=0.0, op0=ALU.mult, op1=ALU.add),
                             [scm, sigw], [cl])
                        act(Wt.t[:, :], cl.t[:, :], AF.Exp, [cl], [Wt], scale=-C0)
                        act(Winv.t[:, :], cl.t[:, :], AF.Exp, [cl], [Winv], scale=C0)
                    st.append(s_w)

                    def s_a():
                        aps = big.get()
                        mm(aps.t[:, :], lora.t[64:128, hs], zl.t[64:128, :], [lora, zl], [aps])
                        act(aa.t[:, :], aps.t[:, :], AF.Sigmoid, [aps, colt], [aa], bias=colv(O_AB, hp))
                    st.append(s_a)

                    def s_kk():
                        sq = tb16.get()
                        act(sq.t[:, :], kp.t[:, :], AF.Square, [kp, colt], [sq], scale=colv(O_KK, hp))
                        ssp = big.get()
                        mm(ssp.t[:, :], onesb.t[:, :], sq.t[:, :], [onesb, sq], [ssp])
                        rn = tmp.get()
                        rsqrt(rn.t[:, 0:512], ssp.t[:, :], 1e-24, [ssp], [rn])
                        tsc("pool", kkn.t[:, :], kp.t[:, :], colv(O_KK, hp), 0.0, ALU.mult, ALU.add, [kp, colt], [kkn])
                        tt("pool", kkn.t[:, :], kkn.t[:, :], rn.t[:, 0:512], ALU.mult, [kkn, rn], [kkn])
                    st.append(s_kk)

                    def s_kb():
                        t1 = tmp.get()
                        tsc("pool", t1.t[:, 0:512], aa.t[:, :], colv(O_KA, hp), omka.t[:, hp:hp + 1], ALU.mult, ALU.add,
                            [aa, colt, omka], [t1])
                        tt("pool", kf.t[:, :], kp.t[:, :], t1.t[:, 0:512], ALU.mult, [kp, t1], [kf])
                        tt("pool", bb.t[:, :], kkn.t[:, :], aa.t[:, :], ALU.mult, [kkn, aa], [bb])
                    st.append(s_kb)
                    return st

                def B(hp):
                    ex = tmp.get()
                    tt("pool", ex.t[:, 0:512], cl.t[:, :], sigw.t[:, :], ALU.subtract, [cl, sigw], [ex])
                    Wprev = tmp.get()
                    act(Wprev.t[:, 0:512], ex.t[:, 0:512], AF.Exp, [ex], [Wprev], scale=-C0)
                    tt("dve", AR.t[:, :, 128:256], v3(rp.t[:, :]), v3(Wt.t[:, :]), ALU.mult, [rp, Wt], [AR])
                    stt(AR.t[:, :, 0:128], v3(kkn.t[:, :]), -1.0, v3(Wprev.t[:, 0:512]), ALU.mult, ALU.mult,
                        [kkn, Wprev], [AR])
                    tt("pool", Kt.t[:, :], kf.t[:, :], Winv.t[:, :], ALU.mult, [kf, Winv], [Kt])
                    tt("pool", Bt.t[:, :], bb.t[:, :], Winv.t[:, :], ALU.mult, [bb, Winv], [Bt])
                    for c in range(4):
                        cs = slice(c * 128, (c + 1) * 128)
                        wc = Wt.t[:, c * 128 + 127:c * 128 + 128]
                        tsc("dve", Kh.t[:, cs], Kt.t[:, cs], wc, None, ALU.mult, None, [Kt, Wt], [Kh])
                        tsc("dve", Bh.t[:, cs], Bt.t[:, cs], wc, None, ALU.mult, None, [Bt, Wt], [Bh])
                        cp("pool", wcs.t[:, c:c + 1], wc, [Wt], [wcs])
                    Vb = tb16.get()
                    cp("pool", Vb.t[:, :], vp[hp].t[:, :], [vp[hp]], [Vb])

                    def trs2(e):
                        ins = None
                        for c in range(4):
                            cs = slice(c * 128, (c + 1) * 128)
                            ins = e.transpose(trps.t[:, c * 128:(c + 1) * 128], Kh.t[:, cs], ident.t[:, :])
                            ins = e.transpose(trps.t[:, 512 + c * 128:512 + (c + 1) * 128], Bh.t[:, cs], ident.t[:, :])
                        return ins
                    S.op("pe", trs2, [Kh, Bh, ident], [trps])
                    cp("act", BKT.t[:, :], trps.t[:, :], [trps], [BKT])

                    def trs3(e, Vb=Vb):
                        ins = None
                        for c in range(4):
                            cs = slice(c * 128, (c + 1) * 128)
                            ins = e.transpose(trps.t[:, c * 128:(c + 1) * 128], Vb.t[:, cs], ident.t[:, :])
                        return ins
                    S.op("pe", trs3, [Vb, ident], [trps])
                    cp("dve", VT.t[:, :], trps.t[:, 0:512], [trps], [VT])
                    rk = tmp.get()
                    tt("pool", rk.t[:, 0:512], rp.t[:, :], kf.t[:, :], ALU.mult, [rp, kf], [rk])
                    rkb = tb16.get()
                    act(rkb.t[:, :], rk.t[:, 0:512], AF.Copy, [rk, colt], [rkb], scale=colv(O_RK, hp))
                    bsp = big.get()
                    mm(bsp.t[:, :], onesb.t[:, :], rkb.t[:, :], [onesb, rkb], [bsp])
                    tt("dve", bonb.t[:, :], bsp.t[:, :], vp[hp].t[:, :], ALU.mult, [bsp, vp[hp]], [bonb])
                    act(sgb.t[:, :], gp.t[:, :], AF.Silu, [gp], [sgb])

                    chains = [(c, hh) for c in range(4) for hh in range(2)]
                    for ci, (c, hh) in enumerate(chains):
                        cs = slice(c * 128, (c + 1) * 128)
                        R_ = slice(hh * 64, hh * 64 + 64)
                        gm, NL, Y = GmC[ci], NLC[ci], YC[ci]
                        gb = ipool.get()
                        g2 = ipool.get()

                        def fG(e, R_=R_, cs=cs, c=c, gb=gb):
                            e.matmul(gb.t[:, 0:256], Bt.t[R_, cs], AR.t[R_, c, :], start=True, stop=True)
                            return e.matmul(gb.t[:, 256:512], Kt.t[R_, cs], AR.t[R_, c, :], start=True, stop=True)
                        S.op("pe", fG, [Bt, Kt, AR], [gb])
                        mm(g2.t[:, 0:128], AR.t[R_, c, 0:128], Bt.t[R_, cs], [AR, Bt], [g2])
                        tt("dve", gm.t[:, :], gb.t[:, :], MASKB2, ALU.mult, [gb, mkb], [gm])
                        tt("dve", NL.t[:, 128:256], g2.t[:, 0:128], MASKL, ALU.mult, [g2, mkl], [NL])
                        cp("pool", NL.t[:, 0:128], gm.t[:, 0:128], [gm], [NL])
                        tt("pool", Y.t[:, :], gm.t[:, 0:128], ident.t[:, :], ALU.add, [gm, ident], [Y])
                    for lev in range(NLEV):
                        last = (lev == NLEV - 1)
                        sqs, yls = {}, {}
                        for step in range(8 + 3):
                            if step < 8:
                                NL = NLC[step]
                                sq = ipool.get()
                                sqs[step] = sq

                                def fsq(e, NL=NL, last=last, sq=sq):
                                    ins = e.matmul(sq.t[:, 128:256], NL.t[:, 0:128], NL.t[:, 128:256], start=True, stop=True)
                                    if not last:
                                        ins = e.matmul(sq.t[:, 0:128], NL.t[:, 128:256], NL.t[:, 0:128], start=True,
                                                       stop=True)
                                    return ins
                                S.op("pe", fsq, [NL], [sq])
                            ci = step - 1
                            if 0 <= ci < 8:
                                NL, sq = NLC[ci], sqs[ci]
                                if last:
                                    cp("act", NL.t[:, 128:256], sq.t[:, 128:256], [sq], [NL])
                                else:
                                    cp("act", NL.t[:, :], sq.t[:, 0:256], [sq], [NL])
                            ci = step - 2
                            if 0 <= ci < 8:
                                NL, Y = NLC[ci], YC[ci]
                                yl = ipool.get()
                                yls[ci] = yl
                                mm(yl.t[:, 0:128], NL.t[:, 128:256], Y.t[:, :], [NL, Y], [yl])
                            ci = step - 3
                            if 0 <= ci < 8:
                                Y, yl = YC[ci], yls[ci]
                                tt("dve", Y.t[:, :], yl.t[:, 0:128], Y.t[:, :], ALU.add, [yl, Y], [Y])

                def C_steps(hp):
                    st = []
                    for c in range(4):
                        cs = slice(c * 128, (c + 1) * 128)
                        info = []
                        for hh in range(2):
                            ci = c * 2 + hh
                            info.append(dict(ci=ci, hh=hh, h=2 * hp + hh, R_=slice(hh * 64, hh * 64 + 64),
                                             vs=slice(c * 128 + hh * 64, c * 128 + hh * 64 + 64), gm=GmC[ci], TT=YC[ci],
                                             p1ap=small[:, hh * 64:hh * 64 + 64],
                                             uap=small[:, 128 + hh * 64:128 + hh * 64 + 64],
                                             sap=small[hh * 64:hh * 64 + 64, 256 + hh * 64:256 + hh * 64 + 64]))

                        def s1(info=info, c=c):
                            for d in info:
                                def fP1(e, d=d):
                                    e.matmul(d["p1ap"], AR.t[d["R_"], c, 0:128], Sbf[d["R_"], hp, :], start=True, stop=False)
                                    return e.matmul(d["p1ap"], d["gm"].t[:, 256:384], VT.t[:, d["vs"]], start=False, stop=True)
                                S.op("pe", fP1, [AR, Sbdep[d["h"]], d["gm"], VT], [smalldep])
                            for d in info:
                                cp("dve", P1b[d["hh"]].t[:, :], d["p1ap"], [smalldep], [P1b[d["hh"]]])
                        st.append(s1)

                        def s2(info=info, c=c):
                            for d in info:
                                mm(d["uap"], d["TT"].t[:, :], P1b[d["hh"]].t[:, :], [d["TT"], P1b[d["hh"]]], [smalldep])
                            for d in info:
                                cp("dve", Ub[d["hh"]].t[:, :], d["uap"], [smalldep], [Ub[d["hh"]]])
                        st.append(s2)

                        def s3(info=info, c=c, cs=cs):
                            for d in info:
                                def fS(e, d=d):
                                    vs = d["vs"]
                                    e.matmul(d["sap"], BKT.t[:, 512 + vs.start:512 + vs.stop], Ub[d["hh"]].t[:, :],
                                             start=True, stop=False)
                                    return e.matmul(d["sap"], BKT.t[:, vs], VT.t[:, vs], start=False, stop=True)
                                S.op("pe", fS, [BKT, Ub[d["hh"]], VT], [smalldep])
                            for d in info:
                                def fY(e, d=d):
                                    R_ = d["R_"]
                                    e.matmul(Ytps.t[R_, cs], Sbf[R_, hp, :], AR.t[R_, c, 128:256], start=True, stop=False)
                                    e.matmul(Ytps.t[R_, cs], Ub[d["hh"]].t[:, :], d["gm"].t[:, 128:256], start=False, stop=False)
                                    return e.matmul(Ytps.t[R_, cs], VT.t[:, d["vs"]], d["gm"].t[:, 384:512], start=False,
                                                    stop=True)
                                S.op("pe", fY, [Sbdep[d["h"]], AR, Ub[d["hh"]], d["gm"], VT], [Ytps])
                            for d in info:
                                R_ = d["R_"]
                                stt(S32[R_, hp, :], S32[R_, hp, :], wcs.t[R_, c:c + 1], d["sap"], ALU.mult, ALU.add,
                                    [Sdep[d["h"]], wcs, smalldep], [Sdep[d["h"]]])
                            for d in info:
                                R_ = d["R_"]
                                cp("dve", Sbf[R_, hp, :], S32[R_, hp, :], [Sdep[d["h"]]], [Sbdep[d["h"]]])
                        st.append(s3)

                    def gn():
                        ysb = tmp.get()
                        cp("act", ysb.t[:, 0:512], Ytps.t[:, :], [Ytps], [ysb])
                        ybf = tb16.get()
                        cp("dve", ybf.t[:, :], ysb.t[:, 0:512], [ysb], [ybf])
                        mps = big.get()
                        mm(mps.t[:, :], ones64.t[:, :], ybf.t[:, :], [ones64, ybf], [mps])
                        dd = tmp.get()
                        tt("dve", dd.t[:, 0:512], ysb.t[:, 0:512], mps.t[:, :], ALU.subtract, [ysb, mps], [dd])
                        dsq = tb16.get()
                        act(dsq.t[:, :], dd.t[:, 0:512], AF.Square, [dd], [dsq])
                        vps = big.get()
                        mm(vps.t[:, :], ones64.t[:, :], dsq.t[:, :], [ones64, dsq], [vps])
                        rstd = tmp.get()
                        rsqrt(rstd.t[:, 0:512], vps.t[:, :], GN_EPS, [vps], [rstd])
                        tt("pool", dd.t[:, 0:512], dd.t[:, 0:512], rstd.t[:, 0:512], ALU.mult, [dd, rstd], [dd])
                        tsc("dve", dd.t[:, 0:512], dd.t[:, 0:512], colv(O_LG, hp), colv(O_LB, hp), ALU.mult, ALU.add,
                            [dd, colt], [dd])
                        tt("pool", dd.t[:, 0:512], dd.t[:, 0:512], bonb.t[:, :], ALU.add, [dd, bonb], [dd])
                        tt("pool", ycat.t[:, hp, :], dd.t[:, 0:512], sgb.t[:, :], ALU.mult, [dd, sgb], [ycat])
                    st.append(gn)
                    return st

                def conv_steps():
                    st = []
                    for cpi in range(4):
                        hold = {}

                        def c1(cpi=cpi, hold=hold):
                            pC = proj(21 + cpi)
                            Csb = tmp.get()
                            cp("act", Csb.t[:, 0:512], pC.t[:, :], [pC], [Csb])
                            pH = proj(25 + cpi)
                            u = tmp.get()
                            cp("pool", u.t[:, 0:2], ucarry.t[:, cpi, :], [ucarry], [u])
                            tt("dve", u.t[:, 2:514], Csb.t[:, 0:512], pH.t[:, :], ALU.mult, [Csb, pH], [u])
                            cp("pool", ucarry.t[:, cpi, :], u.t[:, 512:514], [u], [ucarry])
                            acc = tmp.get()
                            tsc("pool", acc.t[:, 0:512], u.t[:, 0:512], colv(O_CW, 0 * 4 + cpi), 0.0, ALU.mult, ALU.add,
                                [u, colt], [acc])
                            acc2 = tmp.get()
                            stt(acc2.t[:, 0:512], u.t[:, 1:513], colv(O_CW, 1 * 4 + cpi), acc.t[:, 0:512], ALU.mult, ALU.add,
                                [u, colt, acc], [acc2])
                            stt(acc.t[:, 0:512], u.t[:, 2:514], colv(O_CW, 2 * 4 + cpi), acc2.t[:, 0:512], ALU.mult, ALU.add,
                                [u, colt, acc2], [acc])
                            pB = proj(17 + cpi)
                            tt("dve", acc2.t[:, 0:512], pB.t[:, :], acc.t[:, 0:512], ALU.mult, [pB, acc], [acc2])
                            pG = proj(29 + cpi)
                            sg = tmp.get()
                            act(sg.t[:, 0:512], pG.t[:, :], AF.Silu, [pG], [sg])
                            tt("pool", ycat.t[:, 4 + cpi, :], acc2.t[:, 0:512], sg.t[:, 0:512], ALU.mult, [acc2, sg], [ycat])
                        st.append(c1)
                    return st

                def interleave(cs_, as_):
                    ia = 0
                    for k, cstep in enumerate(cs_):
                        cstep()
                        want = ((k + 1) * len(as_) + len(cs_) - 1) // len(cs_)
                        while ia < min(want, len(as_)):
                            as_[ia]()
                            ia += 1
                    while ia < len(as_):
                        as_[ia]()
                        ia += 1

                def mid():
                    for hp in range(3):
                        nxt = A_steps(hp + 1)
                        if hp == 1 and has_next:
                            nxt = nxt + prologue_steps(g + 1)
                        interleave(C_steps(hp), nxt)
                        B(hp + 1)

                def outproj_steps():
                    st = []
                    for j in range(4):
                        def oj(j=j):
                            for half in range(2):
                                pb = big.get()

                                def f(e, pb=pb, j=j, half=half):
                                    ins = None
                                    for kc in range(8):
                                        ins = e.matmul(pb.t[:, :], ycat.t[:, kc, j * 128:(j + 1) * 128],
                                                       Wob.t[:, kc, half * 512:(half + 1) * 512], start=(kc == 0),
                                                       stop=(kc == 7))
                                    return ins
                                S.op("pe", f, [ycat, Wob], [pb])
                                tt("dve", xt.t[:, j, half * 512:(half + 1) * 512], xt.t[:, j, half * 512:(half + 1) * 512],
                                   pb.t[:, :], ALU.add, [xjd[j], pb], [xjd[j]])
                            if last_global and final_norm:
                                junk = tmp.get()
                                act(junk.t[:, 0:512], xt.t[:, j, 0:512], AF.Square, [xjd[j]], [junk, ss], accum=ss.t[:, 0:1])
                                act(junk.t[:, 0:512], xt.t[:, j, 512:1024], AF.Square, [xjd[j]], [junk, ss],
                                    accum=ss.t[:, 1:2])
                                tt("dve", rs.t[:, 0:1], ss.t[:, 0:1], ss.t[:, 1:2], ALU.add, [ss], [rs])
                                rsqrt(rs.t[:, 0:1], rs.t[:, 0:1], 1024.0 * NORM_EPS, [rs], [rs])
                                stt(xt.t[:, j, :], xt.t[:, j, :], rs.t[:, 0:1], fgb.t[:, :], ALU.mult, ALU.mult,
                                    [xjd[j], rs, fgb], [xjd[j]])
                            r0 = t0 + j * 128
                            dma(xout_d[r0:r0 + 128, :], xt.t[:, j, :], [xjd[j]],
                                [xmid_dep[i]] if li < len(layers) - 1 else [])
                        st.append(oj)
                    return st

                return types.SimpleNamespace(g=g, has_next=has_next, head1=lambda: [s_zl, s_vall],
                                             head2=lambda: A_steps(0) + [lambda: B(0)], mid=mid,
                                             c3=lambda: C_steps(3), conv=conv_steps, outproj=outproj_steps,
                                             interleave=interleave)

            ctxs = [tile_ctx(i) for i in range(NT)]
            for f_ in prep_steps[0:3]:
                f_()
            h1_ = ctxs[0].head1()
            h1_[0]()
            prep_steps[3]()
            prep_steps[4]()
            h1_[1]()
            prep_steps[5]()
            prep_steps[6]()
            h2_ = ctxs[0].head2()
            ctxs[0].interleave(h2_, prep_steps[7:9] + [wout_prep])
            for i in range(NT):
                c_ = ctxs[i]
                n_ = ctxs[i + 1] if i + 1 < NT else None
                if c_.has_next:
                    load_x(c_.g + 1)
                c_.mid()
                c_.interleave(c_.c3(), c_.conv() + (n_.head1() if n_ else []))
                c_.interleave(c_.outproj(), n_.head2() if n_ else [])

        all_pools.extend([tmp, tb16, wstp, big, ipool])
        emit_all(True)
        emit_all(False)
        sp = S.engs["sp"]
        S.final = [(sp.dsems[k], sp.dvals[k]) for k in range(len(sp.dsems)) if sp.dvals[k] > 0]

        with nc.Block() as block:
            @block.sync
            def _(e):
                S.replay("sp", e)

            @block.tensor
            def _(e):
                S.replay("pe", e)

            @block.scalar
            def _(e):
                S.replay("act", e)

            @block.vector
            def _(e):
                S.replay("dve", e)

            @block.gpsimd
            def _(e):
                S.replay("pool", e)
    return nc


def make_consts():
    c = np.zeros((128, 1408), np.float32)
    c[:, 0:128] = np.eye(128, dtype=np.float32)
    blk = np.zeros((128, 128), np.float32)
    blk[0:64, 0:64] = 1.0
    blk[64:128, 64:128] = 1.0
    c[:, 128:256] = blk
    s = np.arange(128)[:, None]
    t = np.arange(128)[None, :]
    strict = (s < t).astype(np.float32)
    incl = (s <= t).astype(np.float32)
    c[:, 256:384] = strict
    c[:, 384:512] = incl
    c[:, 512:640] = strict
    c[:, 640:768] = incl
    c[:, 768:896] = (s > t).astype(np.float32)
    m = np.ones((128, 512), np.float32)
    m[:, 0::128] = 0.0
    c[:, 896:1408] = m
    return c


def pack_cols(l, shift_mu, decay_bias, aaa_bias, k_k, k_a, r_k, ln_gain, ln_bias, v_bias, conv_w, norm_gain):
    c = np.zeros((128, NCOLS), np.float32)
    pc = lambda v: np.ascontiguousarray(v.reshape(-1, 128).T)
    c[:, O_MU:O_MU + 17] = pc(shift_mu[l])
    c[:, O_DB:O_DB + 4] = pc(decay_bias[l])
    c[:, O_AB:O_AB + 4] = pc(aaa_bias[l])
    c[:, O_KK:O_KK + 4] = pc(k_k[l])
    c[:, O_KA:O_KA + 4] = pc(k_a[l])
    c[:, O_RK:O_RK + 4] = pc(r_k[l].reshape(-1))
    c[:, O_LG:O_LG + 4] = pc(ln_gain[l])
    c[:, O_LB:O_LB + 4] = pc(ln_bias[l])
    if l >= 1:
        c[:, O_VB:O_VB + 4] = pc(v_bias[l - 1])
    for k in range(3):
        c[:, O_CW + 4 * k:O_CW + 4 * k + 4] = pc(conv_w[l, k])
    c[:, O_NG:O_NG + 8] = pc(norm_gain[l])
    return c


_NC_CACHE = {}


def _get_nc(key, **kw):
    if key not in _NC_CACHE:
        _NC_CACHE[key] = build(**kw)
    return _NC_CACHE[key]


def host_prep(inp, layers):
    f = lambda a: np.ascontiguousarray(np.asarray(a, dtype=np.float32))
    p = {k: f(v) for k, v in inp.items() if k != "x"}
    cols = np.stack([pack_cols(l, p["shift_mu"], p["decay_bias"], p["aaa_bias"], p["k_k"], p["k_a"], p["r_k"],
                               p["ln_gain"], p["ln_bias"], p["v_bias"], p["conv_w"], p["norm_gain"]) for l in layers])
    lora = np.stack([np.concatenate([p["decay_up"][l], p["aaa_up"][l]], axis=0) for l in layers])
    return {
        "w_in": f(p["w_in"][layers]),
        "w_out": f(p["w_out"][layers]),
        "cols": f(cols),
        "lora": f(lora),
        "v_down": f(p["v_down"][0]),
        "v_up": f(p["v_up"][0]),
        "final_gain": f(p["final_gain"].reshape(1, -1)),
        "consts": make_consts(),
    }


FUSED = True


def kernel(**inputs):
    x = np.ascontiguousarray(np.asarray(inputs["x"], dtype=np.float32))
    B, T, _ = x.shape
    n_layers = np.asarray(inputs["w_in"]).shape[0]
    cores = list(range(B))
    if FUSED:
        nc = _get_nc(("fused", T), T=T, layers=list(range(n_layers)), n_layers_total=n_layers)
        shared = host_prep(inputs, list(range(n_layers)))
        in_maps = [dict(shared, x=x[b]) for b in range(B)]
        res = run_bass_kernel_spmd(nc, in_maps, core_ids=cores)
        return np.stack([np.asarray(r["out"]) for r in res.results]).astype(np.float32)
    cur = [x[b] for b in range(B)]
    vf = None
    for l in range(n_layers):
        nc = _get_nc(("layer", T, l, n_layers), T=T, layers=[l], n_layers_total=n_layers,
                     vf_in=(l > 0), vf_out=(l == 0), final_norm=(l == n_layers - 1))
        shared = host_prep(inputs, [l])
        in_maps = []
        for b in range(B):
            m = dict(shared, x=cur[b])
            if l > 0:
                m["vf"] = vf[b]
            in_maps.append(m)
        res = run_bass_kernel_spmd(nc, in_maps, core_ids=cores)
        cur = [np.asarray(r["out"]) for r in res.results]
        if l == 0:
            vf = [np.asarray(r["vf"]) for r in res.results]
    return np.stack(cur).astype(np.float32)
```

```python
import contextlib
import types
import numpy as np
import concourse.bass as bass
import concourse.mybir as mybir
from concourse.bass_utils import run_bass_kernel_spmd

F32 = mybir.dt.float32
BF16 = mybir.dt.bfloat16
ALU = mybir.AluOpType
AF = mybir.ActivationFunctionType

D_MODEL = 1024
IN_COLS = 4224
NCH = 33
TILE = 512
CH = 128
C0 = float(np.exp(-0.5))
NORM_EPS = 1e-6
GN_EPS = 64e-5
NLEV = 6

O_MU, O_DB, O_AB, O_KK, O_KA, O_RK, O_LG, O_LB, O_VB, O_CW, O_NG = 0, 17, 21, 25, 29, 33, 37, 41, 45, 49, 61
NCOLS = 69


class StopBuild(Exception):
    pass


STAGE_LIMIT = [None]


def stage(n):
    if STAGE_LIMIT[0] is not None and STAGE_LIMIT[0] == n:
        raise StopBuild()


class Dep:
    __slots__ = ("lw", "rd", "name", "psum")

    def __init__(self, name="", psum=False):
        self.lw = {}
        self.rd = {}
        self.name = name
        self.psum = psum


class Buf:
    def __init__(self, t, name=""):
        self.t = t
        self.dep = Dep(name)

    def __getitem__(self, k):
        return self.t[k]


class Eng:
    def __init__(self, name):
        self.name = name
        self.ops = []
        self.count = 0
        self.seen = {}
        self.sem = None
        self.dsems = []
        self.dvals = []
        self.dnext = 0


def _dep(x):
    return x.dep if isinstance(x, Buf) else x


class Sched:
    def __init__(self, nc, es, ndma=None):
        self.nc = nc
        ndma = ndma or {"sp": 8, "pool": 4, "act": 2}
        self.engs = {}
        for n in ("pe", "act", "dve", "pool", "sp"):
            e = Eng(n)
            e.sem = es.enter_context(nc.semaphore("s_" + n))
            for k in range(ndma.get(n, 0)):
                e.dsems.append(es.enter_context(nc.semaphore("d_%s%d" % (n, k))))
                e.dvals.append(0)
            self.engs[n] = e
        self.final = []

    def op(self, eng, fn, R=(), W=(), dma=False):
        if getattr(self, "dry", False):
            return ("dry", 0)
        E = self.engs[eng]
        waits = {}

        def need(sem, val):
            if sem is E.sem and eng == "pe":
                return
            if E.seen.get(sem, 0) >= val:
                return
            if waits.get(sem, 0) < val:
                waits[sem] = val

        for b in R:
            d = _dep(b)
            for s, v in d.lw.items():
                need(s, v)
            if d.psum:
                for s, v in d.rd.items():
                    if s is not E.sem:
                        need(s, v)
        for b in W:
            d = _dep(b)
            for s, v in d.lw.items():
                need(s, v)
            for s, v in d.rd.items():
                need(s, v)
        if dma:
            k = E.dnext
            E.dnext = (k + 1) % len(E.dsems)
            sem = E.dsems[k]
            if E.dvals[k] > 0:
                need(sem, E.dvals[k])
            E.dvals[k] += 16
            tok = (sem, E.dvals[k])
            inc = 16
        else:
            E.count += 1
            tok = (E.sem, E.count)
            inc = 1
        for s, v in waits.items():
            E.seen[s] = v
        E.ops.append((list(waits.items()), fn, tok[0], inc))
        for b in R:
            d = _dep(b)
            if d.rd.get(tok[0], 0) < tok[1]:
                d.rd[tok[0]] = tok[1]
        for b in W:
            d = _dep(b)
            if d.lw.get(tok[0], 0) < tok[1]:
                d.lw[tok[0]] = tok[1]
        return tok

    def replay(self, name, e):
        E = self.engs[name]
        for waits, fn, sem, inc in E.ops:
            for s, v in waits:
                e.wait_ge(s, v)
            ins = fn(e)
            ins.then_inc(sem, inc)
        if name == "sp":
            for s, v in self.final:
                e.wait_ge(s, v)


def build(T, layers, n_layers_total, x_kind="ExternalInput", out_kind="ExternalOutput",
          vf_in=False, vf_out=False, final_norm=True, dbg=None):
    NT = T // TILE
    nc = bass.Bass("TRN2", target_bir_lowering=False)
    dr = lambda name, shape, kind: nc.dram_tensor(name, shape, F32, kind=kind).ap()
    x_d = dr("x", [T, D_MODEL], "ExternalInput")
    out_d = dr("out", [T, D_MODEL], "ExternalOutput")
    w_in_d = dr("w_in", [len(layers), D_MODEL, IN_COLS], "ExternalInput")
    w_out_d = dr("w_out", [len(layers), D_MODEL, D_MODEL], "ExternalInput")
    cols_d = dr("cols", [len(layers), 128, NCOLS], "ExternalInput")
    lora_d = dr("lora", [len(layers), 128, 512], "ExternalInput")
    vdn_d = dr("v_down", [512, 32], "ExternalInput")
    vup_d = dr("v_up", [32, 512], "ExternalInput")
    fg_d = dr("final_gain", [1, D_MODEL], "ExternalInput")
    cst_d = dr("consts", [128, 1408], "ExternalInput")
    if vf_in:
        vf_d = dr("vf", [4, 128, T], "ExternalInput")
    elif vf_out:
        vf_d = dr("vf", [4, 128, T], "ExternalOutput")
    else:
        vf_d = dr("vf", [4, 128, T], "Internal")
    xmid_d = dr("xmid", [T, D_MODEL], "Internal") if len(layers) > 1 else None
    wscr_d = nc.dram_tensor("wscr", [len(layers), NCH, 128, 1024], BF16, kind="Internal").ap()
    dbg_d = {}
    if dbg:
        for k, shp in dbg.items():
            dbg_d[k] = dr("dbg_" + k, shp, "ExternalOutput")

    es = contextlib.ExitStack()
    with es:
        es.enter_context(nc.allow_low_precision("bf16 matmul operands, fp32 accumulation"))
        S = Sched(nc, es)

        def sb(name, shape, dt=F32):
            return Buf(es.enter_context(nc.sbuf_tensor("sb_" + name, shape, dt)), name)

        def ps(name, shape, dt=F32):
            b = Buf(es.enter_context(nc.psum_tensor("ps_" + name, shape, dt)), name)
            b.dep.psum = True
            return b

        WD = 6
        wring = [sb("wr%d" % k, [128, 1024], BF16) for k in range(WD)]
        ngs = [sb("ngs%d" % k, [128, 8]) for k in range(2)]
        wscr_dep = [[Dep("wscr%d_%d" % (k, gq)) for gq in range(9)] for k in range(len(layers))]
        TSEQ = [16, 8, 9, 10, 11]
        for hp_ in range(4):
            TSEQ += [hp_, 4 + hp_, 12 + hp_]
        for c_ in range(4):
            TSEQ += [21 + c_, 25 + c_, 17 + c_, 29 + c_]
        Wob = sb("Wob", [128, 8, D_MODEL], BF16)
        lora = sb("lora", [128, 512])
        colt = sb("colt", [128, NCOLS])
        omka = sb("omka", [128, 4])
        vdnb = sb("vdnb", [128, 4, 32], BF16)
        vupb = sb("vupb", [32, 512], BF16)
        mkb = sb("mkb", [128, 512], BF16)
        mkl = sb("mkl", [128, 128], BF16)
        scm = sb("scm", [128, 512])
        ident = sb("ident", [128, 128], BF16)
        onesb = sb("onesb", [128, 128], BF16)
        ones64 = sb("ones64", [128, 128], BF16)
        fgb = sb("fgb", [128, D_MODEL])
        MASKB2 = mkb.t[:, :]
        MASKL = mkl.t[:, :]
        SCANM = scm.t[:, :]

        xs = [sb("xs%d" % k, [128, 4, D_MODEL]) for k in range(2)]
        xjds = [[Dep("xj%d_%d" % (q, k)) for k in range(4)] for q in range(2)]
        hTs = [sb("hT%d" % k, [128, 8, TILE], BF16) for k in range(2)]
        hb = [sb("hb%d" % k, [128, D_MODEL], BF16) for k in range(2)]
        ycat = sb("ycat", [128, 8, TILE], BF16)
        ss = sb("ss", [128, 8])
        rs = sb("rs", [128, 8])
        carry = sb("carry", [128, 17])
        ucarry = sb("ucarry", [128, 4, 2])
        S32 = es.enter_context(nc.sbuf_tensor("S32", [128, 4, 64], F32))
        Sbf = es.enter_context(nc.sbuf_tensor("Sbf", [128, 4, 64], BF16))
        Sdep = [Dep("S%d" % h) for h in range(8)]
        Sbdep = [Dep("Sb%d" % h) for h in range(8)]

        class Pool:
            def __init__(self, name, n, shape, dt=F32, mk=sb):
                self.b = [mk("%s%d" % (name, k), shape, dt) for k in range(n)]
                self.i = 0

            def get(self):
                b = self.b[self.i]
                self.i = (self.i + 1) % len(self.b)
                return b

        tmp = Pool("tmp", 7, [128, 514])
        tb16 = Pool("tb", 4, [128, TILE], BF16)
        wstp = Pool("wst", 2, [128, 4, 1024], BF16)
        zl = sb("zl", [128, TILE])
        vp = [sb("vp%d" % k, [128, TILE]) for k in range(4)]
        rp, kp, gp = sb("rp", [128, TILE]), sb("kp", [128, TILE]), sb("gp", [128, TILE])
        sigw, cl, aa = sb("sigw", [128, TILE]), sb("cl", [128, TILE]), sb("aa", [128, TILE])
        Wt, Winv = sb("Wt", [128, TILE]), sb("Winv", [128, TILE])
        kkn, kf, bb = sb("kkn", [128, TILE]), sb("kf", [128, TILE]), sb("bb", [128, TILE])
        AR = sb("AR", [128, 4, 256], BF16)
        Bt, Kt = sb("Bt", [128, TILE], BF16), sb("Kt", [128, TILE], BF16)
        Bh, Kh = sb("Bh", [128, TILE], BF16), sb("Kh", [128, TILE], BF16)
        BKT = sb("BKT", [128, 1024], BF16)
        VT = sb("VT", [128, TILE], BF16)
        lob = sb("lob", [32, TILE], BF16)
        wcs = sb("wcs", [128, 4])
        bonb = sb("bonb", [128, TILE])
        sgb = sb("sgb", [128, TILE], BF16)
        GmC = [sb("Gm%d" % k, [128, 512], BF16) for k in range(8)]
        NLC = [sb("NL%d" % k, [128, 256], BF16) for k in range(8)]
        YC = [sb("Yv%d" % k, [128, 128], BF16) for k in range(8)]
        P1b = [sb("P1b%d" % k, [128, 64], BF16) for k in range(2)]
        Ub = [sb("Ub%d" % k, [128, 64], BF16) for k in range(2)]

        big = Pool("pbig", 2, [128, 512], F32, mk=ps)
        trps = ps("ptr", [128, 1024], BF16)
        smallb = ps("psmall", [128, 512])
        small = smallb.t
        smalldep = smallb.dep
        P1ps = [smalldep for k in range(2)]
        Ups = [smalldep for k in range(2)]
        Stps = [smalldep for k in range(2)]
        invbs = [ps("pinv%d" % k, [128, 512]) for k in range(3)]
        Ytps = ps("pYt", [128, 512])

        class RR:
            def __init__(self, lst):
                self.b = lst
                self.i = 0

            def get(self):
                b = self.b[self.i]
                self.i = (self.i + 1) % len(self.b)
                return b
        ipool = RR(invbs + [smallb, Ytps])

        xmid_dep = [Dep("xmid%d" % k) for k in range(NT)]
        vf_dep = [Dep("vf%d" % k) for k in range(NT)]
        def act(out, in_, func, R, W, bias=0.0, scale=1.0, accum=None):
            kw = {}
            if accum is not None:
                kw["accum_out"] = accum
            S.op("act", lambda e: e.activation(out=out, in_=in_, func=func, bias=bias, scale=scale, **kw), R, W)

        def tt(eng, out, in0, in1, op, R, W):
            S.op(eng, lambda e: e.tensor_tensor(out=out, in0=in0, in1=in1, op=op), R, W)

        def tsc(eng, out, in0, s1, s2, op0, op1, R, W):
            if op1 is None:
                S.op(eng, lambda e: e.tensor_scalar(out=out, in0=in0, scalar1=s1, scalar2=None, op0=op0), R, W)
            else:
                S.op(eng, lambda e: e.tensor_scalar(out=out, in0=in0, scalar1=s1, scalar2=s2, op0=op0, op1=op1), R, W)

        def stt(out, in0, scalar, in1, op0, op1, R, W):
            S.op("dve", lambda e: e.scalar_tensor_tensor(out=out, in0=in0, scalar=scalar, in1=in1, op0=op0, op1=op1), R, W)

        def rsqrt(out, in_, eps, R, W):
            act(out, in_, AF.Sqrt, R, W, bias=eps)
            S.op("dve", lambda e: e.reciprocal(out=out, in_=out), W, W)

        def cp(eng, out, in_, R, W):
            if eng == "act":
                S.op("act", lambda e: e.copy(out=out, in_=in_), R, W)
            else:
                S.op(eng, lambda e: e.tensor_copy(out=out, in_=in_), R, W)

        def mm(out, lhsT, rhs, R, W, start=True, stop=True):
            S.op("pe", lambda e: e.matmul(out, lhsT, rhs, start=start, stop=stop), R, W)

        def dma(out, in_, R, W, eng="sp"):
            return S.op(eng, lambda e: e.dma_start(out=out, in_=in_), R, W, dma=True)

        rr = {"i": 0}

        def anyeng(choices=("dve", "pool")):
            rr["i"] += 1
            return choices[rr["i"] % len(choices)]

        c0_ = tmp.get()
        dma(c0_.t[:, 0:256], cst_d[:, 0:256], [], [c0_])
        cp("dve", ident.t[:, :], c0_.t[:, 0:128], [c0_], [ident])
        cp("dve", onesb.t[:, :], c0_.t[:, 128:256], [c0_], [onesb])
        tsc("dve", ones64.t[:, :], c0_.t[:, 128:256], 1.0 / 64.0, None, ALU.mult, None, [c0_], [ones64])
        c1_ = tmp.get()
        dma(c1_.t[:, 0:512], cst_d[:, 256:768], [], [c1_])
        cp("dve", mkb.t[:, :], c1_.t[:, 0:512], [c1_], [mkb])
        c2_ = tmp.get()
        dma(c2_.t[:, 0:128], cst_d[:, 768:896], [], [c2_])
        cp("dve", mkl.t[:, :], c2_.t[:, 0:128], [c2_], [mkl])
        c3_ = tmp.get()
        dma(c3_.t[:, 0:512], cst_d[:, 896:1408], [], [c3_])
        cp("dve", scm.t[:, :], c3_.t[:, 0:512], [c3_], [scm])
        if final_norm:
            dma(fgb.t[:, :], fg_d[0:1, :].partition_broadcast(128), [], [fgb])
            tsc("dve", fgb.t[:, :], fgb.t[:, :], 32.0, None, ALU.mult, None, [fgb], [fgb])

        def colv(off, j=0):
            return colt.t[:, off + j:off + j + 1]

        proj_log = {}
        all_pools = []

        pending_prep = []

        def emit_all(dry):
          S.dry = dry
          del pending_prep[:]
          for p_ in all_pools:
              p_.i = 0
          rr["i"] = 0
          for li, l in enumerate(layers):
            first_global = (l == 0)
            last_global = (l == n_layers_total - 1)
            xin_d = x_d if li == 0 else xmid_d
            xout_d = out_d if li == len(layers) - 1 else xmid_d

            dma(colt.t[:, :], cols_d[li], [], [colt])
            dma(lora.t[:, :], lora_d[li], [], [lora])
            tsc("dve", omka.t[:, :], colt.t[:, O_KA:O_KA + 4], -1.0, 1.0, ALU.mult, ALU.add, [colt], [omka])
            S.op("pool", lambda e: e.memset(carry.t[:, :], 0.0), [], [carry])
            S.op("pool", lambda e: e.memset(ucarry.t[:, :, :], 0.0), [], [ucarry])
            S.op("pool", lambda e: e.memset(S32[:, :, :], 0.0), [], Sdep)
            S.op("pool", lambda e: e.memset(Sbf[:, :, :], 0.0), [], Sbdep)
            k3c = {'v': 0}
            def make_prep(lj):
                ngt = ngs[lj % 2]
                steps = []

                def ld():
                    dma(ngt.t[:, :], cols_d[lj][:, O_NG:O_NG + 8], [], [ngt])
                steps.append(ld)
                for gq in (4, 2, 0, 1, 3, 5, 6, 7, 8):
                    def pstep(gq=gq):
                        c0 = gq * 512
                        w = min(512, IN_COLS - c0)
                        cc0, nch = c0 // 128, w // 128
                        stg = wstp.get()
                        for kc in range(8):
                            st = tmp.get()
                            dma(st.t[:, 0:w], w_in_d[lj, kc * 128:(kc + 1) * 128, c0:c0 + w], [], [st])
                            eng = ("act", "dve")[k3c['v'] % 2]
                            k3c['v'] += 1
                            dst = stg.t[:, 0:nch, kc * 128:(kc + 1) * 128]
                            src = st.t[:, 0:w].rearrange("p (c m) -> p c m", m=128)
                            if eng == "act":
                                act(dst, src, AF.Copy, [st, ngt], [stg], scale=ngt.t[:, kc:kc + 1])
                            else:
                                tsc(eng, dst, src, ngt.t[:, kc:kc + 1], None, ALU.mult, None, [st, ngt], [stg])
                        dma(wscr_d[lj, cc0:cc0 + nch, :, :].rearrange("c p m -> p c m"), stg.t[:, 0:nch, :], [stg],
                            [wscr_dep[lj][gq]])
                    steps.append(pstep)
                return steps

            if li == 0:
                prep_steps = make_prep(0)
                prep_steps.pop(0)()
            else:
                while pending_prep:
                    pending_prep.pop(0)()
                prep_steps = []
            if li + 1 < len(layers):
                pending_prep.extend(make_prep(li + 1))

            def wout_prep():
                for kc in range(8):
                    for c0 in range(0, D_MODEL, 512):
                        st = tmp.get()
                        dma(st.t[:, 0:512], w_out_d[li, kc * 128:(kc + 1) * 128, c0:c0 + 512], [], [st])
                        eng = ("act", "dve")[k3c['v'] % 2]
                        k3c['v'] += 1
                        cp(eng, Wob.t[:, kc, c0:c0 + 512], st.t[:, 0:512], [st], [Wob])

            if not first_global:
                st = tmp.get()
                dma(st.t[:, 0:128].rearrange("p (h m) -> p h m", m=32),
                    vdn_d.rearrange("(h p) m -> p h m", p=128), [], [st])
                cp("dve", vdnb.t[:, :, :], st.t[:, 0:128].rearrange("p (h m) -> p h m", m=32), [st], [vdnb])
                st = tmp.get()
                dma(st.t[0:32, 0:512], vup_d[:, :], [], [st])
                cp("dve", vupb.t[:, :], st.t[0:32, 0:512], [st], [vupb])

            if dry:
                proj_log[li] = []
            uses = proj_log[li]
            wstate = {"u": 0, "l": 0}

            def ensure_loads(upto, li=li, uses=uses, wstate=wstate):
                while wstate["l"] < min(upto, len(uses)):
                    k = wstate["l"]
                    slot = wring[k % WD]
                    assert wscr_dep[li][uses[k] // 4].lw, ("weight group not prepared yet", uses[k])
                    dma(slot.t[:, :], wscr_d[li, uses[k]], [wscr_dep[li][uses[k] // 4]], [slot])
                    wstate["l"] = k + 1

            stage(1)
            def xsrc(g):
                lj, ij = divmod(g, NT)
                src = x_d if lj == 0 else xmid_d
                return src, ([xmid_dep[ij]] if lj > 0 else []), ij

            def load_x(g):
                src, deps, ij = xsrc(g)
                for j in range(4):
                    r0 = ij * TILE + j * 128
                    dma(xs[g % 2].t[:, j, :], src[r0:r0 + 128, :], deps, [xjds[g % 2][j]])

            def prologue_steps(g):
                xt_, xd_, hT_ = xs[g % 2], xjds[g % 2], hTs[g % 2]
                st = []
                for j in range(4):
                    def pj(j=j):
                        junk = tmp.get()
                        hbj = hb[j % 2]
                        act(junk.t[:, 0:512], xt_.t[:, j, 0:512], AF.Square, [xd_[j]], [junk, ss],
                            accum=ss.t[:, 2 * j:2 * j + 1])
                        act(junk.t[:, 0:512], xt_.t[:, j, 512:1024], AF.Square, [xd_[j]], [junk, ss],
                            accum=ss.t[:, 2 * j + 1:2 * j + 2])
                        tt("dve", rs.t[:, 2 * j:2 * j + 1], ss.t[:, 2 * j:2 * j + 1], ss.t[:, 2 * j + 1:2 * j + 2], ALU.add,
                           [ss], [rs])
                        rsqrt(rs.t[:, 2 * j:2 * j + 1], rs.t[:, 2 * j:2 * j + 1], 1024.0 * NORM_EPS, [rs], [rs])
                        tsc(anyeng(), hbj.t[:, :], xt_.t[:, j, :], rs.t[:, 2 * j:2 * j + 1], 32.0, ALU.mult, ALU.mult,
                            [xd_[j], rs], [hbj])

                        def trs(e, hbj=hbj):
                            ins = None
                            for kc in range(8):
                                ins = e.transpose(trps.t[:, kc * 128:(kc + 1) * 128], hbj.t[:, kc * 128:(kc + 1) * 128],
                                                  ident.t[:, :])
                            return ins
                        S.op("pe", trs, [hbj, ident], [trps])
                        cp(anyeng(("act", "dve")), hT_.t[:, :, j * 128:(j + 1) * 128],
                           trps.t[:, :].rearrange("p (k t) -> p k t", t=128), [trps], [hT_])
                    st.append(pj)
                return st

            if li == 0:
                load_x(0)
                for f_ in prologue_steps(0):
                    f_()
            def tile_ctx(i):
                g = li * NT + i
                xt = xs[g % 2]
                xjd = xjds[g % 2]
                hT = hTs[g % 2]
                t0 = i * TILE
                has_next = (g + 1 < len(layers) * NT)

                stage(2)

                def proj(cc):
                    if dry:
                        uses.append(cc)
                        return big.get()
                    u = wstate["u"]
                    assert uses[u] == cc, (u, uses[u], cc)
                    ensure_loads(u + WD)
                    slot = wring[u % WD]
                    wstate["u"] = u + 1
                    pb = big.get()

                    def f(e, pb=pb, slot=slot, hT=hT):
                        ins = None
                        for kc in range(8):
                            ins = e.matmul(pb.t[:, :], slot.t[:, kc * 128:(kc + 1) * 128], hT.t[:, kc, :],
                                           start=(kc == 0), stop=(kc == 7))
                        return ins
                    S.op("pe", f, [slot, hT], [pb])
                    return pb

                def tokshift(pb, cc, out):
                    zs = tmp.get()
                    cp("pool", zs.t[:, 0:1], carry.t[:, cc:cc + 1], [carry], [zs])
                    cp("act", zs.t[:, 1:513], pb.t[:, :], [pb], [zs])
                    cp("pool", carry.t[:, cc:cc + 1], zs.t[:, 512:513], [zs], [carry])
                    dd = tmp.get()
                    tt("pool", dd.t[:, 0:512], zs.t[:, 0:512], zs.t[:, 1:513], ALU.subtract, [zs], [dd])
                    stt(out.t[:, :], dd.t[:, 0:512], colv(O_MU, cc), zs.t[:, 1:513], ALU.mult, ALU.add,
                        [dd, zs, colt], [out])

                def s_zl():
                    tokshift(proj(16), 16, zl)
                    act(zl.t[0:64, :], zl.t[0:64, :], AF.Tanh, [zl], [zl])

                def s_vall():
                    vb4 = [tb16.get() for _ in range(4)] if not first_global else None
                    for hp in range(4):
                        tokshift(proj(8 + hp), 8 + hp, vp[hp])
                        if first_global:
                            dma(vf_d[hp, :, t0:t0 + TILE], vp[hp].t[:, :], [vp[hp]], [vf_dep[i]])
                        else:
                            cp(anyeng(), vb4[hp].t[:, :], vp[hp].t[:, :], [vp[hp]], [vb4[hp]])
                    if not first_global:
                        lo = big.get()

                        def f(e, lo=lo, vb4=vb4):
                            ins = None
                            for hp in range(4):
                                ins = e.matmul(lo.t[0:32, :], vdnb.t[:, hp, :], vb4[hp].t[:, :], start=(hp == 0),
                                               stop=(hp == 3))
                            return ins
                        S.op("pe", f, [vdnb] + vb4, [lo])
                        cp("act", lob.t[:, :], lo.t[0:32, :], [lo], [lob])
                        for hp in range(4):
                            gps_ = big.get()
                            mm(gps_.t[:, :], vupb.t[0:32, hp * 128:(hp + 1) * 128], lob.t[0:32, :], [vupb, lob], [gps_])
                            sgv = tmp.get()
                            act(sgv.t[:, 0:512], gps_.t[:, :], AF.Sigmoid, [gps_, colt], [sgv], bias=colv(O_VB, hp))
                            vfl = tmp.get()
                            dma(vfl.t[:, 0:512], vf_d[hp, :, t0:t0 + TILE], [vf_dep[i]], [vfl])
                            tt("pool", vfl.t[:, 0:512], vfl.t[:, 0:512], vp[hp].t[:, :], ALU.subtract, [vfl, vp[hp]], [vfl])
                            tt("dve", vfl.t[:, 0:512], vfl.t[:, 0:512], sgv.t[:, 0:512], ALU.mult, [vfl, sgv], [vfl])
                            tt("pool", vp[hp].t[:, :], vp[hp].t[:, :], vfl.t[:, 0:512], ALU.add, [vfl, vp[hp]], [vp[hp]])


                stage(4)
                v3 = lambda ap: ap.rearrange("p (c t) -> p c t", t=128)

                def A_steps(hp):
                    hs = slice(hp * 128, (hp + 1) * 128)
                    st = []
                    st.append(lambda: tokshift(proj(hp), hp, rp))
                    st.append(lambda: tokshift(proj(4 + hp), 4 + hp, kp))
                    st.append(lambda: tokshift(proj(12 + hp), 12 + hp, gp))

                    def s_w():
                        wps = big.get()
                        mm(wps.t[:, :], lora.t[0:64, hs], zl.t[0:64, :], [lora, zl], [wps])
                        act(sigw.t[:, :], wps.t[:, :], AF.Sigmoid, [wps, colt], [sigw], bias=colv(O_DB, hp))
                        S.op("dve", lambda e: e.tensor_tensor_scan(out=cl.t[:, :], data0=SCANM, data1=sigw.t[:, :],
                                                                   initial=0.0, op0=ALU.mult, op1=ALU.add),
                             [scm, sigw], [cl])
                        act(Wt.t[:, :], cl.t[:, :], AF.Exp, [cl], [Wt], scale=-C0)
                        act(Winv.t[:, :], cl.t[:, :], AF.Exp, [cl], [Winv], scale=C0)
                    st.append(s_w)

                    def s_a():
                        aps = big.get()
                        mm(aps.t[:, :], lora.t[64:128, hs], zl.t[64:128, :], [lora, zl], [aps])
                        act(aa.t[:, :], aps.t[:, :], AF.Sigmoid, [aps, colt], [aa], bias=colv(O_AB, hp))
                    st.append(s_a)

                    def s_kk():
                        sq = tb16.get()
                        act(sq.t[:, :], kp.t[:, :], AF.Square, [kp, colt], [sq], scale=colv(O_KK, hp))
                        ssp = big.get()
                        mm(ssp.t[:, :], onesb.t[:, :], sq.t[:, :], [onesb, sq], [ssp])
                        rn = tmp.get()
                        rsqrt(rn.t[:, 0:512], ssp.t[:, :], 1e-24, [ssp], [rn])
                        tsc("pool", kkn.t[:, :], kp.t[:, :], colv(O_KK, hp), 0.0, ALU.mult, ALU.add, [kp, colt], [kkn])
                        tt("pool", kkn.t[:, :], kkn.t[:, :], rn.t[:, 0:512], ALU.mult, [kkn, rn], [kkn])
                    st.append(s_kk)

                    def s_kb():
                        t1 = tmp.get()
                        tsc("pool", t1.t[:, 0:512], aa.t[:, :], colv(O_KA, hp), omka.t[:, hp:hp + 1], ALU.mult, ALU.add,
                            [aa, colt, omka], [t1])
                        tt("pool", kf.t[:, :], kp.t[:, :], t1.t[:, 0:512], ALU.mult, [kp, t1], [kf])
                        tt("pool", bb.t[:, :], kkn.t[:, :], aa.t[:, :], ALU.mult, [kkn, aa], [bb])
                    st.append(s_kb)
                    return st

                def B(hp):
                    ex = tmp.get()
                    tt("pool", ex.t[:, 0:512], cl.t[:, :], sigw.t[:, :], ALU.subtract, [cl, sigw], [ex])
                    Wprev = tmp.get()
                    act(Wprev.t[:, 0:512], ex.t[:, 0:512], AF.Exp, [ex], [Wprev], scale=-C0)
                    tt("dve", AR.t[:, :, 128:256], v3(rp.t[:, :]), v3(Wt.t[:, :]), ALU.mult, [rp, Wt], [AR])
                    stt(AR.t[:, :, 0:128], v3(kkn.t[:, :]), -1.0, v3(Wprev.t[:, 0:512]), ALU.mult, ALU.mult,
                        [kkn, Wprev], [AR])
                    tt("pool", Kt.t[:, :], kf.t[:, :], Winv.t[:, :], ALU.mult, [kf, Winv], [Kt])
                    tt("pool", Bt.t[:, :], bb.t[:, :], Winv.t[:, :], ALU.mult, [bb, Winv], [Bt])
                    for c in range(4):
                        cs = slice(c * 128, (c + 1) * 128)
                        wc = Wt.t[:, c * 128 + 127:c * 128 + 128]
                        tsc("dve", Kh.t[:, cs], Kt.t[:, cs], wc, None, ALU.mult, None, [Kt, Wt], [Kh])
                        tsc("dve", Bh.t[:, cs], Bt.t[:, cs], wc, None, ALU.mult, None, [Bt, Wt], [Bh])
                        cp("pool", wcs.t[:, c:c + 1], wc, [Wt], [wcs])
                    Vb = tb16.get()
                    cp("pool", Vb.t[:, :], vp[hp].t[:, :], [vp[hp]], [Vb])

                    def trs2(e):
                        ins = None
                        for c in range(4):
                            cs = slice(c * 128, (c + 1) * 128)
                            ins = e.transpose(trps.t[:, c * 128:(c + 1) * 128], Kh.t[:, cs], ident.t[:, :])
                            ins = e.transpose(trps.t[:, 512 + c * 128:512 + (c + 1) * 128], Bh.t[:, cs], ident.t[:, :])
                        return ins
                    S.op("pe", trs2, [Kh, Bh, ident], [trps])
                    cp("act", BKT.t[:, :], trps.t[:, :], [trps], [BKT])

                    def trs3(e, Vb=Vb):
                        ins = None
                        for c in range(4):
                            cs = slice(c * 128, (c + 1) * 128)
                            ins = e.transpose(trps.t[:, c * 128:(c + 1) * 128], Vb.t[:, cs], ident.t[:, :])
                        return ins
                    S.op("pe", trs3, [Vb, ident], [trps])
                    cp("dve", VT.t[:, :], trps.t[:, 0:512], [trps], [VT])
                    rk = tmp.get()
                    tt("pool", rk.t[:, 0:512], rp.t[:, :], kf.t[:, :], ALU.mult, [rp, kf], [rk])
                    rkb = tb16.get()
                    act(rkb.t[:, :], rk.t[:, 0:512], AF.Copy, [rk, colt], [rkb], scale=colv(O_RK, hp))
                    bsp = big.get()
                    mm(bsp.t[:, :], onesb.t[:, :], rkb.t[:, :], [onesb, rkb], [bsp])
                    tt("dve", bonb.t[:, :], bsp.t[:, :], vp[hp].t[:, :], ALU.mult, [bsp, vp[hp]], [bonb])
                    act(sgb.t[:, :], gp.t[:, :], AF.Silu, [gp], [sgb])

                    chains = [(c, hh) for c in range(4) for hh in range(2)]
                    for ci, (c, hh) in enumerate(chains):
                        cs = slice(c * 128, (c + 1) * 128)
                        R_ = slice(hh * 64, hh * 64 + 64)
                        gm, NL, Y = GmC[ci], NLC[ci], YC[ci]
                        gb = ipool.get()
                        g2 = ipool.get()

                        def fG(e, R_=R_, cs=cs, c=c, gb=gb):
                            e.matmul(gb.t[:, 0:256], Bt.t[R_, cs], AR.t[R_, c, :], start=True, stop=True)
                            return e.matmul(gb.t[:, 256:512], Kt.t[R_, cs], AR.t[R_, c, :], start=True, stop=True)
                        S.op("pe", fG, [Bt, Kt, AR], [gb])
                        mm(g2.t[:, 0:128], AR.t[R_, c, 0:128], Bt.t[R_, cs], [AR, Bt], [g2])
                        tt("dve", gm.t[:, :], gb.t[:, :], MASKB2, ALU.mult, [gb, mkb], [gm])
                        tt("dve", NL.t[:, 128:256], g2.t[:, 0:128], MASKL, ALU.mult, [g2, mkl], [NL])
                        cp("pool", NL.t[:, 0:128], gm.t[:, 0:128], [gm], [NL])
                        tt("pool", Y.t[:, :], gm.t[:, 0:128], ident.t[:, :], ALU.add, [gm, ident], [Y])
                    for lev in range(NLEV):
                        last = (lev == NLEV - 1)
                        sqs, yls = {}, {}
                        for step in range(8 + 3):
                            if step < 8:
                                NL = NLC[step]
                                sq = ipool.get()
                                sqs[step] = sq

                                def fsq(e, NL=NL, last=last, sq=sq):
                                    ins = e.matmul(sq.t[:, 128:256], NL.t[:, 0:128], NL.t[:, 128:256], start=True, stop=True)
                                    if not last:
                                        ins = e.matmul(sq.t[:, 0:128], NL.t[:, 128:256], NL.t[:, 0:128], start=True,
                                                       stop=True)
                                    return ins
                                S.op("pe", fsq, [NL], [sq])
                            ci = step - 1
                            if 0 <= ci < 8:
                                NL, sq = NLC[ci], sqs[ci]
                                if last:
                                    cp("act", NL.t[:, 128:256], sq.t[:, 128:256], [sq], [NL])
                                else:
                                    cp("act", NL.t[:, :], sq.t[:, 0:256], [sq], [NL])
                            ci = step - 2
                            if 0 <= ci < 8:
                                NL, Y = NLC[ci], YC[ci]
                                yl = ipool.get()
                                yls[ci] = yl
                                mm(yl.t[:, 0:128], NL.t[:, 128:256], Y.t[:, :], [NL, Y], [yl])
                            ci = step - 3
                            if 0 <= ci < 8:
                                Y, yl = YC[ci], yls[ci]
                                tt("dve", Y.t[:, :], yl.t[:, 0:128], Y.t[:, :], ALU.add, [yl, Y], [Y])

                def C_steps(hp):
                    st = []
                    for c in range(4):
                        cs = slice(c * 128, (c + 1) * 128)
                        info = []
                        for hh in range(2):
                            ci = c * 2 + hh
                            info.append(dict(ci=ci, hh=hh, h=2 * hp + hh, R_=slice(hh * 64, hh * 64 + 64),
                                             vs=slice(c * 128 + hh * 64, c * 128 + hh * 64 + 64), gm=GmC[ci], TT=YC[ci],
                                             p1ap=small[:, hh * 64:hh * 64 + 64],
                                             uap=small[:, 128 + hh * 64:128 + hh * 64 + 64],
                                             sap=small[hh * 64:hh * 64 + 64, 256 + hh * 64:256 + hh * 64 + 64]))

                        def s1(info=info, c=c):
                            for d in info:
                                def fP1(e, d=d):
                                    e.matmul(d["p1ap"], AR.t[d["R_"], c, 0:128], Sbf[d["R_"], hp, :], start=True, stop=False)
                                    return e.matmul(d["p1ap"], d["gm"].t[:, 256:384], VT.t[:, d["vs"]], start=False, stop=True)
                                S.op("pe", fP1, [AR, Sbdep[d["h"]], d["gm"], VT], [smalldep])
                            for d in info:
                                cp("dve", P1b[d["hh"]].t[:, :], d["p1ap"], [smalldep], [P1b[d["hh"]]])
                        st.append(s1)

                        def s2(info=info, c=c):
                            for d in info:
                                mm(d["uap"], d["TT"].t[:, :], P1b[d["hh"]].t[:, :], [d["TT"], P1b[d["hh"]]], [smalldep])
                            for d in info:
                                cp("dve", Ub[d["hh"]].t[:, :], d["uap"], [smalldep], [Ub[d["hh"]]])
                        st.append(s2)

                        def s3(info=info, c=c, cs=cs):
                            for d in info:
                                def fS(e, d=d):
                                    vs = d["vs"]
                                    e.matmul(d["sap"], BKT.t[:, 512 + vs.start:512 + vs.stop], Ub[d["hh"]].t[:, :],
                                             start=True, stop=False)
                                    return e.matmul(d["sap"], BKT.t[:, vs], VT.t[:, vs], start=False, stop=True)
                                S.op("pe", fS, [BKT, Ub[d["hh"]], VT], [smalldep])
                            for d in info:
                                def fY(e, d=d):
                                    R_ = d["R_"]
                                    e.matmul(Ytps.t[R_, cs], Sbf[R_, hp, :], AR.t[R_, c, 128:256], start=True, stop=False)
                                    e.matmul(Ytps.t[R_, cs], Ub[d["hh"]].t[:, :], d["gm"].t[:, 128:256], start=False, stop=False)
                                    return e.matmul(Ytps.t[R_, cs], VT.t[:, d["vs"]], d["gm"].t[:, 384:512], start=False,
                                                    stop=True)
                                S.op("pe", fY, [Sbdep[d["h"]], AR, Ub[d["hh"]], d["gm"], VT], [Ytps])
                            for d in info:
                                R_ = d["R_"]
                                stt(S32[R_, hp, :], S32[R_, hp, :], wcs.t[R_, c:c + 1], d["sap"], ALU.mult, ALU.add,
                                    [Sdep[d["h"]], wcs, smalldep], [Sdep[d["h"]]])
                            for d in info:
                                R_ = d["R_"]
                                cp("dve", Sbf[R_, hp, :], S32[R_, hp, :], [Sdep[d["h"]]], [Sbdep[d["h"]]])
                        st.append(s3)

                    def gn():
                        ysb = tmp.get()
                        cp("act", ysb.t[:, 0:512], Ytps.t[:, :], [Ytps], [ysb])
                        ybf = tb16.get()
                        cp("dve", ybf.t[:, :], ysb.t[:, 0:512], [ysb], [ybf])
                        mps = big.get()
                        mm(mps.t[:, :], ones64.t[:, :], ybf.t[:, :], [ones64, ybf], [mps])
                        dd = tmp.get()
                        tt("dve", dd.t[:, 0:512], ysb.t[:, 0:512], mps.t[:, :], ALU.subtract, [ysb, mps], [dd])
                        dsq = tb16.get()
                        act(dsq.t[:, :], dd.t[:, 0:512], AF.Square, [dd], [dsq])
                        vps = big.get()
                        mm(vps.t[:, :], ones64.t[:, :], dsq.t[:, :], [ones64, dsq], [vps])
                        rstd = tmp.get()
                        rsqrt(rstd.t[:, 0:512], vps.t[:, :], GN_EPS, [vps], [rstd])
                        tt("pool", dd.t[:, 0:512], dd.t[:, 0:512], rstd.t[:, 0:512], ALU.mult, [dd, rstd], [dd])
                        tsc("dve", dd.t[:, 0:512], dd.t[:, 0:512], colv(O_LG, hp), colv(O_LB, hp), ALU.mult, ALU.add,
                            [dd, colt], [dd])
                        tt("pool", dd.t[:, 0:512], dd.t[:, 0:512], bonb.t[:, :], ALU.add, [dd, bonb], [dd])
                        tt("pool", ycat.t[:, hp, :], dd.t[:, 0:512], sgb.t[:, :], ALU.mult, [dd, sgb], [ycat])
                    st.append(gn)
                    return st

                def conv_steps():
                    st = []
                    for cpi in range(4):
                        hold = {}

                        def c1(cpi=cpi, hold=hold):
                            pC = proj(21 + cpi)
                            Csb = tmp.get()
                            cp("act", Csb.t[:, 0:512], pC.t[:, :], [pC], [Csb])
                            pH = proj(25 + cpi)
                            u = tmp.get()
                            cp("pool", u.t[:, 0:2], ucarry.t[:, cpi, :], [ucarry], [u])
                            tt("dve", u.t[:, 2:514], Csb.t[:, 0:512], pH.t[:, :], ALU.mult, [Csb, pH], [u])
                            cp("pool", ucarry.t[:, cpi, :], u.t[:, 512:514], [u], [ucarry])
                            acc = tmp.get()
                            tsc("pool", acc.t[:, 0:512], u.t[:, 0:512], colv(O_CW, 0 * 4 + cpi), 0.0, ALU.mult, ALU.add,
                                [u, colt], [acc])
                            acc2 = tmp.get()
                            stt(acc2.t[:, 0:512], u.t[:, 1:513], colv(O_CW, 1 * 4 + cpi), acc.t[:, 0:512], ALU.mult, ALU.add,
                                [u, colt, acc], [acc2])
                            stt(acc.t[:, 0:512], u.t[:, 2:514], colv(O_CW, 2 * 4 + cpi), acc2.t[:, 0:512], ALU.mult, ALU.add,
                                [u, colt, acc2], [acc])
                            pB = proj(17 + cpi)
                            tt("dve", acc2.t[:, 0:512], pB.t[:, :], acc.t[:, 0:512], ALU.mult, [pB, acc], [acc2])
                            pG = proj(29 + cpi)
                            sg = tmp.get()
                            act(sg.t[:, 0:512], pG.t[:, :], AF.Silu, [pG], [sg])
                            tt("pool", ycat.t[:, 4 + cpi, :], acc2.t[:, 0:512], sg.t[:, 0:512], ALU.mult, [acc2, sg], [ycat])
                        st.append(c1)
                    return st

                def interleave(cs_, as_):
                    ia = 0
                    for k, cstep in enumerate(cs_):
                        cstep()
                        want = ((k + 1) * len(as_) + len(cs_) - 1) // len(cs_)
                        while ia < min(want, len(as_)):
                            as_[ia]()
                            ia += 1
                    while ia < len(as_):
                        as_[ia]()
                        ia += 1

                def mid():
                    for hp in range(3):
                        nxt = A_steps(hp + 1)
                        if hp == 1 and has_next:
                            nxt = nxt + prologue_steps(g + 1)
                        interleave(C_steps(hp), nxt)
                        B(hp + 1)
                        if pending_prep and i >= NT - 4:
                            pending_prep.pop(0)()

                def outproj_steps():
                    st = []
                    for j in range(4):
                        def oj(j=j):
                            for half in range(2):
                                pb = big.get()

                                def f(e, pb=pb, j=j, half=half):
                                    ins = None
                                    for kc in range(8):
                                        ins = e.matmul(pb.t[:, :], ycat.t[:, kc, j * 128:(j + 1) * 128],
                                                       Wob.t[:, kc, half * 512:(half + 1) * 512], start=(kc == 0),
                                                       stop=(kc == 7))
                                    return ins
                                S.op("pe", f, [ycat, Wob], [pb])
                                tt("dve", xt.t[:, j, half * 512:(half + 1) * 512], xt.t[:, j, half * 512:(half + 1) * 512],
                                   pb.t[:, :], ALU.add, [xjd[j], pb], [xjd[j]])
                            if last_global and final_norm:
                                junk = tmp.get()
                                act(junk.t[:, 0:512], xt.t[:, j, 0:512], AF.Square, [xjd[j]], [junk, ss], accum=ss.t[:, 0:1])
                                act(junk.t[:, 0:512], xt.t[:, j, 512:1024], AF.Square, [xjd[j]], [junk, ss],
                                    accum=ss.t[:, 1:2])
                                tt("dve", rs.t[:, 0:1], ss.t[:, 0:1], ss.t[:, 1:2], ALU.add, [ss], [rs])
                                rsqrt(rs.t[:, 0:1], rs.t[:, 0:1], 1024.0 * NORM_EPS, [rs], [rs])
                                stt(xt.t[:, j, :], xt.t[:, j, :], rs.t[:, 0:1], fgb.t[:, :], ALU.mult, ALU.mult,
                                    [xjd[j], rs, fgb], [xjd[j]])
                            r0 = t0 + j * 128
                            dma(xout_d[r0:r0 + 128, :], xt.t[:, j, :], [xjd[j]],
                                [xmid_dep[i]] if li < len(layers) - 1 else [])
                        st.append(oj)
                    return st

                return types.SimpleNamespace(g=g, has_next=has_next, head1=lambda: [s_zl, s_vall],
                                             head2=lambda: A_steps(0) + [lambda: B(0)], mid=mid,
                                             c3=lambda: C_steps(3), conv=conv_steps, outproj=outproj_steps,
                                             interleave=interleave)

            ctxs = [tile_ctx(i) for i in range(NT)]
            h1_ = ctxs[0].head1()
            h2_ = ctxs[0].head2()
            if prep_steps:
                for f_ in prep_steps[0:3]:
                    f_()
                h1_[0]()
                prep_steps[3]()
                prep_steps[4]()
                h1_[1]()
                prep_steps[5]()
                prep_steps[6]()
                ctxs[0].interleave(h2_, prep_steps[7:9] + [wout_prep])
            else:
                h1_[0]()
                h1_[1]()
                ctxs[0].interleave(h2_, [wout_prep])
            for i in range(NT):
                c_ = ctxs[i]
                n_ = ctxs[i + 1] if i + 1 < NT else None
                if c_.has_next:
                    load_x(c_.g + 1)
                c_.mid()
                c_.interleave(c_.c3(), c_.conv() + (n_.head1() if n_ else []))
                c_.interleave(c_.outproj(), n_.head2() if n_ else [])

        all_pools.extend([tmp, tb16, wstp, big, ipool])
        emit_all(True)
        emit_all(False)
        sp = S.engs["sp"]
        S.final = [(sp.dsems[k], sp.dvals[k]) for k in range(len(sp.dsems)) if sp.dvals[k] > 0]

        with nc.Block() as block:
            @block.sync
            def _(e):
                S.replay("sp", e)

            @block.tensor
            def _(e):
                S.replay("pe", e)

            @block.scalar
            def _(e):
                S.replay("act", e)

            @block.vector
            def _(e):
                S.replay("dve", e)

            @block.gpsimd
            def _(e):
                S.replay("pool", e)
    return nc


def make_consts():
    c = np.zeros((128, 1408), np.float32)
    c[:, 0:128] = np.eye(128, dtype=np.float32)
    blk = np.zeros((128, 128), np.float32)
    blk[0:64, 0:64] = 1.0
    blk[64:128, 64:128] = 1.0
    c[:, 128:256] = blk
    s = np.arange(128)[:, None]
    t = np.arange(128)[None, :]
    strict = (s < t).astype(np.float32)
    incl = (s <= t).astype(np.float32)
    c[:, 256:384] = strict
    c[:, 384:512] = incl
    c[:, 512:640] = strict
    c[:, 640:768] = incl
    c[:, 768:896] = (s > t).astype(np.float32)
    m = np.ones((128, 512), np.float32)
    m[:, 0::128] = 0.0
    c[:, 896:1408] = m
    return c


def pack_cols(l, shift_mu, decay_bias, aaa_bias, k_k, k_a, r_k, ln_gain, ln_bias, v_bias, conv_w, norm_gain):
    c = np.zeros((128, NCOLS), np.float32)
    pc = lambda v: np.ascontiguousarray(v.reshape(-1, 128).T)
    c[:, O_MU:O_MU + 17] = pc(shift_mu[l])
    c[:, O_DB:O_DB + 4] = pc(decay_bias[l])
    c[:, O_AB:O_AB + 4] = pc(aaa_bias[l])
    c[:, O_KK:O_KK + 4] = pc(k_k[l])
    c[:, O_KA:O_KA + 4] = pc(k_a[l])
    c[:, O_RK:O_RK + 4] = pc(r_k[l].reshape(-1))
    c[:, O_LG:O_LG + 4] = pc(ln_gain[l])
    c[:, O_LB:O_LB + 4] = pc(ln_bias[l])
    if l >= 1:
        c[:, O_VB:O_VB + 4] = pc(v_bias[l - 1])
    for k in range(3):
        c[:, O_CW + 4 * k:O_CW + 4 * k + 4] = pc(conv_w[l, k])
    c[:, O_NG:O_NG + 8] = pc(norm_gain[l])
    return c


_NC_CACHE = {}


def _get_nc(key, **kw):
    if key not in _NC_CACHE:
        _NC_CACHE[key] = build(**kw)
    return _NC_CACHE[key]


def host_prep(inp, layers):
    f = lambda a: np.ascontiguousarray(np.asarray(a, dtype=np.float32))
    p = {k: f(v) for k, v in inp.items() if k != "x"}
    cols = np.stack([pack_cols(l, p["shift_mu"], p["decay_bias"], p["aaa_bias"], p["k_k"], p["k_a"], p["r_k"],
                               p["ln_gain"], p["ln_bias"], p["v_bias"], p["conv_w"], p["norm_gain"]) for l in layers])
    lora = np.stack([np.concatenate([p["decay_up"][l], p["aaa_up"][l]], axis=0) for l in layers])
    return {
        "w_in": f(p["w_in"][layers]),
        "w_out": f(p["w_out"][layers]),
        "cols": f(cols),
        "lora": f(lora),
        "v_down": f(p["v_down"][0]),
        "v_up": f(p["v_up"][0]),
        "final_gain": f(p["final_gain"].reshape(1, -1)),
        "consts": make_consts(),
    }


FUSED = True


def kernel(**inputs):
    x = np.ascontiguousarray(np.asarray(inputs["x"], dtype=np.float32))
    B, T, _ = x.shape
    n_layers = np.asarray(inputs["w_in"]).shape[0]
    cores = list(range(B))
    if FUSED:
        nc = _get_nc(("fused", T), T=T, layers=list(range(n_layers)), n_layers_total=n_layers)
        shared = host_prep(inputs, list(range(n_layers)))
        in_maps = [dict(shared, x=x[b]) for b in range(B)]
        res = run_bass_kernel_spmd(nc, in_maps, core_ids=cores)
        return np.stack([np.asarray(r["out"]) for r in res.results]).astype(np.float32)
    cur = [x[b] for b in range(B)]
    vf = None
    for l in range(n_layers):
        nc = _get_nc(("layer", T, l, n_layers), T=T, layers=[l], n_layers_total=n_layers,
                     vf_in=(l > 0), vf_out=(l == 0), final_norm=(l == n_layers - 1))
        shared = host_prep(inputs, [l])
        in_maps = []
        for b in range(B):
            m = dict(shared, x=cur[b])
            if l > 0:
                m["vf"] = vf[b]
            in_maps.append(m)
        res = run_bass_kernel_spmd(nc, in_maps, core_ids=cores)
        cur = [np.asarray(r["out"]) for r in res.results]
        if l == 0:
            vf = [np.asarray(r["vf"]) for r in res.results]
    return np.stack(cur).astype(np.float32)
```
